# Optimizing a Trainium2 kernel written in Bass

```python
import jax, jax.numpy as jnp
from jax import lax
import numpy as np

D_MODEL = 2048
BATCH = 1
SEQ = 8192
DEPTH = 2
DEC_BATCH = 16
DEC_SEQ = 2048
PAST_LEN = 128

GRID_W = 64
HEAD_DIM = 128
A_HEADS = 8
A_KV_HEADS = 2
A_WINDOW = 128
A_BLOCK = 128
B_HEADS = 4
B_DK = 128
B_DV = 256
B_CHUNK = 64
B_GATE_RANK = 16
B_GATE_TAU = 16.0
C_HEADS = 8
C_WIN_R = 8
C_WIN_C = 16
D_FF = -(-8 * D_MODEL // (3 * 256)) * 256
RMS_EPS = 1e-6
NEG_INF = -1e30

A_Q = A_HEADS * HEAD_DIM
A_KV = A_KV_HEADS * HEAD_DIM
B_QK = B_HEADS * B_DK
B_V = B_HEADS * B_DV
C_W = C_HEADS * HEAD_DIM
SPLIT_SIZES = (A_Q, A_KV, A_KV, B_QK, B_QK, B_V, B_V, 2 * B_GATE_RANK, C_W, C_W, C_W, 3 * D_MODEL)
SPLIT_IDX = tuple(int(i) for i in np.cumsum(SPLIT_SIZES)[:-1])
IN_COLS = int(sum(SPLIT_SIZES))

kernel_name = 'hybrid_bidir_encoder_gqa_gla_natten'


def _rmsnorm(x, g):
    x32 = x.astype(jnp.float32)
    y = x32 * lax.rsqrt(jnp.mean(x32 * x32, axis=-1, keepdims=True) + RMS_EPS)
    return (y * g.astype(jnp.float32)).astype(x.dtype)


def _window_gqa(q, k, v, sink):
    B, S = q.shape[0], q.shape[1]
    nb = S // A_BLOCK
    G = A_HEADS // A_KV_HEADS
    qb = q.reshape(B, nb, A_BLOCK, A_KV_HEADS, G, HEAD_DIM)

    def band(t):
        tp = jnp.pad(t, ((0, 0), (A_BLOCK, A_BLOCK), (0, 0), (0, 0)))
        tp = tp.reshape(B, nb + 2, A_BLOCK, A_KV_HEADS, HEAD_DIM)
        return jnp.concatenate([tp[:, :-2], tp[:, 1:-1], tp[:, 2:]], axis=2)

    kb, vb = band(k), band(v)
    s = jnp.einsum('bnqhgd,bnkhd->bnhgqk', qb, kb, preferred_element_type=jnp.float32).astype(jnp.float32)
    s = s * (HEAD_DIM ** -0.5)
    qpos = jnp.arange(A_BLOCK)[:, None]
    kpos = jnp.arange(3 * A_BLOCK)[None, :] - A_BLOCK
    dist = jnp.abs(qpos - kpos)
    kabs = jnp.arange(nb)[:, None, None] * A_BLOCK + kpos[None]
    valid = (dist <= A_WINDOW)[None] & (kabs >= 0) & (kabs < S)
    slopes = jnp.exp2(-8.0 * jnp.arange(1, A_HEADS + 1, dtype=jnp.float32) / A_HEADS).reshape(A_KV_HEADS, G)
    s = s - slopes[:, :, None, None] * dist.astype(jnp.float32)
    s = jnp.where(valid[None, :, None, None], s, NEG_INF)
    sink_l = jnp.broadcast_to(sink.astype(jnp.float32).reshape(A_KV_HEADS, G, 1, 1), s.shape[:-1] + (1,))
    p = jax.nn.softmax(jnp.concatenate([s, sink_l], axis=-1), axis=-1)[..., :-1]
    o = jnp.einsum('bnhgqk,bnkhd->bnqhgd', p.astype(v.dtype), vb)
    return o.reshape(B, S, A_Q)


def _gla_causal(q, k, v, lg):
    B, S, H, K = q.shape
    V = v.shape[-1]
    n = S // B_CHUNK
    f32 = jnp.float32
    q, k, v, lg = [t.astype(f32).reshape(B, n, B_CHUNK, H, t.shape[-1]) for t in (q, k, v, lg)]
    b = jnp.cumsum(lg, axis=2)
    b_last = b[:, :, -1:]
    qt = q * jnp.exp(b)
    kt = k * jnp.exp(-b)
    kd = k * jnp.exp(b_last - b)
    tri = jnp.tril(jnp.ones((B_CHUNK, B_CHUNK), dtype=bool))
    a = jnp.where(tri, jnp.einsum('bnqhk,bnshk->bnhqs', qt, kt), 0.0)
    o = jnp.einsum('bnhqs,bnshv->bnqhv', a, v)
    u = jnp.einsum('bnshk,bnshv->bnhkv', kd, v)
    decay = jnp.exp(b_last[:, :, 0])

    def step(state, inp):
        d, uu = inp
        return d[..., None] * state + uu, state

    _, s_prev = lax.scan(step, jnp.zeros((B, H, K, V), f32), (jnp.moveaxis(decay, 1, 0), jnp.moveaxis(u, 1, 0)))
    s_prev = jnp.moveaxis(s_prev, 0, 1)
    o = o + jnp.einsum('bnqhk,bnhkv->bnqhv', qt, s_prev)
    return o.reshape(B, S, H, V)


def _gla_branch(q, k, v, og, gr, w2, bias, gain):
    B, S = q.shape[0], q.shape[1]
    q = q.reshape(B, S, B_HEADS, B_DK) * (B_DK ** -0.5)
    k = k.reshape(B, S, B_HEADS, B_DK)
    v = v.reshape(B, S, B_HEADS, B_DV)
    gr = gr.reshape(B, S, 2, B_GATE_RANK)
    logits = jnp.einsum('bsdr,drk->bsdk', gr, w2) + bias
    lg = (jax.nn.log_sigmoid(logits.astype(jnp.float32)) / B_GATE_TAU).reshape(B, S, 2, B_HEADS, B_DK)
    fwd = _gla_causal(q, k, v, lg[:, :, 0])
    flip = lambda t: jnp.flip(t, axis=1)
    bwd = flip(_gla_causal(flip(q), flip(k), flip(v), flip(lg[:, :, 1])))
    o = fwd + bwd
    o = o * lax.rsqrt(jnp.mean(o * o, axis=-1, keepdims=True) + RMS_EPS) * gain.astype(jnp.float32)
    return o.reshape(B, S, B_V).astype(og.dtype) * jax.nn.silu(og)


def _neighbourhood_attn(q, k, v, rpb):
    B, S = q.shape[0], q.shape[1]
    R = S // GRID_W
    kr = min(C_WIN_R, R)
    qg = q.reshape(B, R, GRID_W, C_HEADS, HEAD_DIM)
    kg = k.reshape(B, R, GRID_W, C_HEADS, HEAD_DIM)
    vg = v.reshape(B, R, GRID_W, C_HEADS, HEAD_DIM)
    rows = jnp.arange(R)
    row_idx = jnp.clip(rows - kr // 2, 0, R - kr)[:, None] + jnp.arange(kr)[None, :]
    kband = jnp.take(kg, row_idx, axis=1)
    vband = jnp.take(vg, row_idx, axis=1)
    cols = jnp.arange(GRID_W)
    col_start = jnp.clip(cols - C_WIN_C // 2, 0, GRID_W - C_WIN_C)
    in_win = (cols[None, :] >= col_start[:, None]) & (cols[None, :] < col_start[:, None] + C_WIN_C)
    dr_idx = row_idx - rows[:, None] + (C_WIN_R - 1)
    dc_idx = jnp.clip(cols[None, :] - cols[:, None], -(C_WIN_C - 1), C_WIN_C - 1) + (C_WIN_C - 1)
    bias = rpb.astype(jnp.float32)[:, dr_idx][:, :, :, dc_idx]
    bias = jnp.where(in_win[None, None, None], bias, NEG_INF).transpose(0, 1, 3, 2, 4)
    s = jnp.einsum('brqhd,brikhd->bhrqik', qg, kband, preferred_element_type=jnp.float32).astype(jnp.float32)
    s = s * (HEAD_DIM ** -0.5) + bias[None]
    p = jax.nn.softmax(s, axis=(-2, -1))
    o = jnp.einsum('bhrqik,brikhd->brqhd', p.astype(v.dtype), vband)
    return o.reshape(B, S, C_W)


def _mixer(h, w_in, sink_a, gla_w2, gla_b, gla_norm, rpb_c, w_br_a, w_br_b, w_br_c, w_out):
    B, S = h.shape[0], h.shape[1]
    proj = h @ w_in
    (a_q, a_k, a_v, b_q, b_k, b_v, b_og, b_gr, c_q, c_k, c_v, gl) = jnp.split(proj, SPLIT_IDX, axis=-1)
    o_a = _window_gqa(a_q.reshape(B, S, A_HEADS, HEAD_DIM), a_k.reshape(B, S, A_KV_HEADS, HEAD_DIM),
                      a_v.reshape(B, S, A_KV_HEADS, HEAD_DIM), sink_a)
    o_b = _gla_branch(b_q, b_k, b_v, b_og, b_gr, gla_w2, gla_b, gla_norm)
    o_c = _neighbourhood_attn(c_q.reshape(B, S, C_HEADS, HEAD_DIM), c_k.reshape(B, S, C_HEADS, HEAD_DIM),
                              c_v.reshape(B, S, C_HEADS, HEAD_DIM), rpb_c)
    g_a, g_b, g_c = jnp.split(jax.nn.sigmoid(gl), 3, axis=-1)
    merged = g_a * (o_a @ w_br_a) + g_b * (o_b @ w_br_b) + g_c * (o_c @ w_br_c)
    return merged @ w_out


def _trunk(x, norm1, w_in, sink_a, gla_w2, gla_b, gla_norm, rpb_c, w_br_a, w_br_b, w_br_c, w_out,
           norm2, w_ffn_in, w_ffn_out, norm_f):
    for l in range(DEPTH):
        h = _rmsnorm(x, norm1[l])
        x = x + _mixer(h, w_in[l], sink_a[l], gla_w2[l], gla_b[l], gla_norm[l], rpb_c[l],
                       w_br_a[l], w_br_b[l], w_br_c[l], w_out[l])
        h = _rmsnorm(x, norm2[l])
        g, u = jnp.split(h @ w_ffn_in[l], 2, axis=-1)
        x = x + (jax.nn.silu(g) * u) @ w_ffn_out[l]
    return _rmsnorm(x, norm_f)


def setup_inputs(seed: int = 0) -> dict:
    key = jax.random.key(seed)
    ks = jax.random.split(key, 20)
    f32 = jnp.float32
    n = lambda k, shape: jax.random.normal(k, shape, f32)
    return {
        'x_prompt': n(ks[0], (BATCH, SEQ, D_MODEL)),
        'x_sample': n(ks[1], (DEC_BATCH, DEC_SEQ, D_MODEL)),
        'norm1': 1.0 + 0.02 * n(ks[2], (DEPTH, D_MODEL)),
        'w_in': n(ks[3], (DEPTH, D_MODEL, IN_COLS)) * D_MODEL ** -0.5,
        'sink_a': 0.5 * n(ks[4], (DEPTH, A_HEADS)),
        'gla_w2': n(ks[5], (DEPTH, 2, B_GATE_RANK, B_QK)) * B_GATE_RANK ** -0.5,
        'gla_b': 0.1 * n(ks[6], (DEPTH, 2, B_QK)),
        'gla_norm': 1.0 + 0.02 * n(ks[7], (DEPTH, B_DV)),
        'rpb_c': 0.1 * n(ks[8], (DEPTH, C_HEADS, 2 * C_WIN_R - 1, 2 * C_WIN_C - 1)),
        'w_br_a': n(ks[9], (DEPTH, A_Q, D_MODEL)) * A_Q ** -0.5,
        'w_br_b': n(ks[10], (DEPTH, B_V, D_MODEL)) * B_V ** -0.5,
        'w_br_c': n(ks[11], (DEPTH, C_W, D_MODEL)) * C_W ** -0.5,
        'w_out': n(ks[12], (DEPTH, D_MODEL, D_MODEL)) * D_MODEL ** -0.5,
        'norm2': 1.0 + 0.02 * n(ks[13], (DEPTH, D_MODEL)),
        'w_ffn_in': n(ks[14], (DEPTH, D_MODEL, 2 * D_FF)) * D_MODEL ** -0.5,
        'w_ffn_out': n(ks[15], (DEPTH, D_FF, D_MODEL)) * D_FF ** -0.5,
        'norm_f': 1.0 + 0.02 * n(ks[16], (D_MODEL,)),
    }


def reference(x_prompt, x_sample, norm1, w_in, sink_a, gla_w2, gla_b, gla_norm, rpb_c, w_br_a, w_br_b,
              w_br_c, w_out, norm2, w_ffn_in, w_ffn_out, norm_f):
    y_prompt = _trunk(x_prompt, norm1, w_in, sink_a, gla_w2, gla_b, gla_norm, rpb_c, w_br_a, w_br_b, w_br_c,
                      w_out, norm2, w_ffn_in, w_ffn_out, norm_f)
    y_sample = _trunk(x_sample, norm1, w_in, sink_a, gla_w2, gla_b, gla_norm, rpb_c, w_br_a, w_br_b, w_br_c,
                      w_out, norm2, w_ffn_in, w_ffn_out, norm_f)
    return (y_prompt, y_sample)
```

```python
import numpy as np
from contextlib import ExitStack
import ml_dtypes

import concourse.bass as bass
import concourse.mybir as mybir
from concourse.bass_utils import run_bass_kernel_spmd

F32 = mybir.dt.float32
BF16 = mybir.dt.bfloat16
AF = mybir.ActivationFunctionType
ALU = mybir.AluOpType
AX = mybir.AxisListType

ENGS = ("pe", "act", "dve", "pool", "sp")


class Buf:
    __slots__ = ("name", "w", "r", "ap", "sem", "nd")

    def __init__(self, name, ap=None):
        self.name = name
        self.w = None
        self.r = []
        self.ap = ap
        self.sem = None
        self.nd = 0


class KB:
    def __init__(self, nc, es):
        self.nc = nc
        self.es = es
        self.prog = {e: [] for e in ENGS}
        self.cnt = {e: 0 for e in ENGS}
        self.sem = {}
        self.waited = {}
        self.nbuf = 0
        for e in ("pe", "act", "dve", "pool"):
            self._mksem("E_" + e)
        self.n_ins = 0
        self.free_sems = []
        self.dsem_cnt = {}

    def release(self, buf):
        if buf.sem is not None:
            self.free_sems.append((buf.sem, buf.nd))
            buf.sem = None

    def barrier(self):
        cur = [("E_" + e, self.cnt[e]) for e in ("pe", "act", "dve", "pool") if self.cnt[e] > 0]
        cur += [(s, 16 * n) for s, n in self.dsem_cnt.items() if n > 0]
        for eng in ENGS:
            waits = []
            for (s, v) in cur:
                if self.waited.get((eng, s), 0) < v:
                    waits.append((s, v))
                    self.waited[(eng, s)] = v
            if waits:
                self.prog[eng].append((waits, None, None, 0))

    def _mksem(self, name):
        self.sem[name] = self.es.enter_context(self.nc.semaphore(name))
        return name

    def sb(self, name, shape, dtype):
        t = self.nc.alloc_sbuf_tensor(name, list(shape), dtype)
        return Buf(name, t)

    def ps(self, name, shape, dtype=F32):
        t = self.nc.alloc_psum_tensor(name, list(shape), dtype)
        return Buf(name, t)

    def vbuf(self, name):
        return Buf(name)

    def _waits_for(self, eng, reads, writes):
        evs = []
        for b in reads:
            if b.w is not None:
                evs.append(b.w)
        for b in writes:
            if b.w is not None:
                evs.append(b.w)
            evs.extend(b.r)
        out = {}
        for (s, v) in evs:
            if self.waited.get((eng, s), 0) >= v:
                continue
            if out.get(s, 0) < v:
                out[s] = v
        for s, v in out.items():
            self.waited[(eng, s)] = v
        return list(out.items())

    def op(self, eng, fn, reads=(), writes=(), signal=True):
        waits = self._waits_for(eng, reads, writes)
        ev = None
        if signal:
            self.cnt[eng] += 1
            ev = ("E_" + eng, self.cnt[eng])
        self.prog[eng].append((waits, fn, ev[0] if ev else None, 1))
        self.n_ins += 1
        if ev is not None:
            for b in reads:
                b.r.append(ev)
            for b in writes:
                b.w = ev
                b.r = []
        return ev

    def group_begin(self, eng, reads=(), writes=()):
        waits = self._waits_for(eng, reads, writes)
        if waits:
            self.prog[eng].append((waits, None, None, 0))

    def raw(self, eng, fn):
        self.prog[eng].append(([], fn, None, 0))
        self.n_ins += 1

    def group_end(self, eng, fn, reads=(), writes=()):
        self.cnt[eng] += 1
        ev = ("E_" + eng, self.cnt[eng])
        self.prog[eng].append(([], fn, ev[0], 1))
        self.n_ins += 1
        for b in reads:
            b.r.append(ev)
        for b in writes:
            b.w = ev
            b.r = []
        return ev

    def dma(self, eng, fn, reads=(), writes=(), owner=None):
        if owner is None:
            owner = writes[0] if writes else reads[0]
        if owner.sem is None:
            if self.free_sems:
                owner.sem, owner.nd = self.free_sems.pop()
            else:
                self.nbuf += 1
                owner.sem = self._mksem("D%d" % self.nbuf)
                self.dsem_cnt[owner.sem] = 0
        waits = self._waits_for(eng, reads, writes)
        owner.nd += 1
        self.dsem_cnt[owner.sem] = owner.nd
        ev = (owner.sem, 16 * owner.nd)
        self.prog[eng].append((waits, fn, owner.sem, 16))
        self.n_ins += 1
        for b in reads:
            b.r.append(ev)
        for b in writes:
            b.w = ev
            b.r = []
        return ev

    def wait_all(self, eng, bufs):
        waits = self._waits_for(eng, (), bufs)
        if waits:
            self.prog[eng].append((waits, None, None, 0))

    def emit(self):
        nc = self.nc
        sem = self.sem
        prog = self.prog

        def run(e_obj, lst):
            for (waits, fn, incs, incv) in lst:
                for (s, v) in waits:
                    e_obj.wait_ge(sem[s], v)
                if fn is not None:
                    ins = fn(e_obj)
                    if incs is not None:
                        ins.then_inc(sem[incs], incv)

        with nc.Block() as block:
            @block.tensor
            def _(e):
                run(e, prog["pe"])

            @block.scalar
            def _(e):
                run(e, prog["act"])

            @block.vector
            def _(e):
                run(e, prog["dve"])

            @block.gpsimd
            def _(e):
                run(e, prog["pool"])

            @block.sync
            def _(e):
                run(e, prog["sp"])


def dram_ap(t, offset, dims):
    return bass.AP(t, offset, [list(d) for d in dims])


D = 2048
KC = 16
DFF = 5632
A_Q, A_KV, B_QK, B_V, C_W, GL = 1024, 256, 512, 1024, 1024, 6144
C_AQ, C_AK, C_AV, C_BQ, C_BK, C_BV, C_BOG, C_BGR, C_CQ, C_CK, C_CV, C_GL = (
    0, 1024, 1280, 1536, 2048, 2560, 3584, 4608, 4640, 5664, 6688, 7712)
IN_COLS = 13856
R_AQ, R_AK, R_BQ, R_BK, R_OG, R_GR, R_CQ, R_CK, R_GL = 0, 1024, 1280, 1792, 2304, 3328, 3360, 4384, 5408
NFM = 5408 + 6144
T_AV, T_BV, T_CV = 0, 256, 1280
NTM = 2304
EPS = 1e-6
NEG = -1e30
QS = 128 ** -0.5


class Arena:
    def __init__(self, nc, kb, nbytes):
        self.t = nc.alloc_sbuf_tensor("arena", [128, nbytes // 2], BF16)
        self.kb = kb
        self.nbytes = nbytes
        self.top = 0
        self.k = 0
        self.live = []

    def alloc(self, name, shape, dtype, parts=128):
        n = 1
        for s in shape:
            n *= s
        sz = 4 if dtype == F32 else 2
        nb = (n * sz + 31) // 32 * 32
        off = self.top
        assert off + nb <= self.nbytes, ("arena overflow", name, off, nb, self.nbytes)
        self.top = off + nb
        v = self.t[0:parts, off // 2: off // 2 + (n * sz) // 2]
        if dtype == F32:
            v = v.bitcast(F32)
        if len(shape) == 2:
            v = v.rearrange("p (a b) -> p a b", a=shape[0])
        elif len(shape) == 3:
            v = v.rearrange("p (a b c) -> p a b c", a=shape[0], b=shape[1])
        self.k += 1
        b = Buf("%s_%d" % (name, self.k), v)
        self.live.append((off, b))
        return b

    def mark(self):
        return self.top

    def reset(self, m):
        while self.live and self.live[-1][0] >= m:
            self.kb.release(self.live.pop()[1])
        self.top = m


class Ring:
    def __init__(self, items):
        self.items = items
        self.i = -1

    def next(self):
        self.i += 1
        return self.items[self.i % len(self.items)]

    def at(self, i):
        return self.items[i % len(self.items)]


class G:
    pass


def start_rows(r, R_tot, RS, typ):
    if typ == "P":
        return min(max(r - 4, 0), R_tot - 8)
    b = (r // RS) * RS
    return b + min(max(r - b - 4, 0), RS - 8)


def plan_C(T, SEG):
    R_tot, RS = T // 64, SEG // 64
    bands, keys = [], []
    for j in range(T // 128):
        r0 = 2 * j
        ss = [start_rows(r0 + rr, R_tot, RS, ty) for ty in "PS" for rr in (0, 1)]
        lo = min(ss) // 2 * 2
        hi = (max(ss) + 8 + 1) // 2 * 2
        lo = max(lo, 0)
        hi = min(hi, R_tot)
        assert lo >= r0 - 6 and hi <= r0 + 8, (j, lo, hi)
        bands.append((lo, hi))
        key = tuple(tuple(start_rows(r0 + rr, R_tot, RS, ty) - r0 for rr in (0, 1)) for ty in "PS")
        keys.append(key)
    uniq = sorted(set(keys))
    cls = [uniq.index(k) for k in keys]
    return bands, cls, uniq


def masks_C(uniq, typ):
    cols = np.arange(64)
    cstart = np.clip(cols - 8, 0, 48)
    inwin = (cols[None, :] >= cstart[:, None]) & (cols[None, :] < cstart[:, None] + 16)
    out = np.full((len(uniq), 128, 14, 64), NEG, np.float32)
    for ci, key in enumerate(uniq):
        rel = key[0 if typ == "P" else 1]
        for rr in (0, 1):
            st = rel[rr]
            for i in range(14):
                row = i - 6
                if st <= row < st + 8:
                    out[ci, rr * 64:(rr + 1) * 64, i, :] = np.where(inwin, 0.0, NEG)
    return out.reshape(len(uniq), 128, 14 * 64)


def plan_A(T, SEG):
    NT, NTS = T // 128, SEG // 128
    keys = []
    for n in range(NT):
        k = []
        for ty in "PS":
            if ty == "P":
                pv, nv = n > 0, n < NT - 1
            else:
                pv, nv = n % NTS != 0, n % NTS != NTS - 1
            k.append((pv, nv))
        keys.append(tuple(k))
    uniq = sorted(set(keys))
    return [uniq.index(k) for k in keys], uniq


def bias_A(uniq, typ):
    q = np.arange(128)[:, None]
    k = np.arange(384)[None, :] - 128
    dist = np.abs(q - k).astype(np.float32)
    out = np.zeros((len(uniq), 8, 128, 384), np.float32)
    for ci, key in enumerate(uniq):
        pv, nv = key[0 if typ == "P" else 1]
        valid = dist <= 128
        valid = valid & (pv | (k >= 0)) & (nv | (k < 128))
        for h in range(8):
            slope = 2.0 ** (-8.0 * (h + 1) / 8)
            out[ci, h] = np.where(valid, -slope * dist, NEG)
    return out.reshape(len(uniq) * 8, 128, 384)


def consts_np():
    s = np.arange(128)[:, None]
    t = np.arange(128)[None, :]
    c = np.zeros((128, 6, 128), np.float32)
    c[:, 0] = (s == t)
    c[:, 1] = np.where(s <= t, -1.0 / 16, 0.0)
    c[:, 2] = np.where(s >= t, -1.0 / 16, 0.0)
    c[:, 3] = (s <= t)
    c[:, 4] = (s >= t)
    c[:, 5] = 1.0
    return c.reshape(128, 768)


def mm_group(kb, out_ap, pairs, reads, bank):
    kb.group_begin("pe", reads=reads, writes=[bank])
    n = len(pairs)
    for i, (l, r) in enumerate(pairs):
        f = (lambda e, l=l, r=r, i=i: e.matmul(out_ap, lhsT=l, rhs=r, start=(i == 0), stop=(i == n - 1)))
        if i == n - 1:
            kb.group_end("pe", f, reads=reads, writes=[bank])
        else:
            kb.raw("pe", f)


def run_jobs(jobs, ring):
    n, nb = len(jobs), len(ring)
    for j in range(min(nb - 1, n)):
        jobs[j][0](ring[j % nb])
    for j in range(n):
        if j + nb - 1 < n:
            jobs[j + nb - 1][0](ring[(j + nb - 1) % nb])
        jobs[j][1](ring[j % nb])


def front_end(kb, g, xsrc, nvec_t, nvec_off, seg, XT):
    nc, ar = g.nc, g.ar
    SEG, NTS = g.SEG, g.SEG // 128
    m = ar.mark()
    gain = ar.alloc("gain", [D], F32)
    kb.dma("sp", lambda e: e.dma_start(out=gain.ap, in_=dram_ap(nvec_t, nvec_off, [[0, 128], [1, D]])), writes=[gain])
    xt = [ar.alloc("xt", [D], F32) for _ in range(2)]
    xn = [ar.alloc("xn", [D], BF16) for _ in range(2)]
    junk = ar.alloc("junk", [D], BF16)
    st = [ar.alloc("st", [4], F32) for _ in range(2)]
    for i in range(NTS):
        x_, n_, s_ = xt[i % 2], xn[i % 2], st[i % 2]
        r0 = seg * SEG + i * 128
        kb.dma("sp", lambda e, x_=x_, r0=r0: e.dma_start(out=x_.ap, in_=xsrc[r0:r0 + 128, :]), writes=[x_])
        kb.op("act", lambda e, x_=x_, s_=s_: e.activation(out=junk.ap, in_=x_.ap, func=AF.Square, accum_out=s_.ap[:, 0:1]),
              reads=[x_], writes=[junk, s_])
        kb.op("act", lambda e, s_=s_: e.activation(out=s_.ap[:, 1:2], in_=s_.ap[:, 0:1], func=AF.Sqrt, scale=1.0 / D, bias=g.eps.ap[:, 0:1]),
              reads=[s_, g.eps], writes=[s_])
        kb.op("dve", lambda e, s_=s_: e.reciprocal(out=s_.ap[:, 2:3], in_=s_.ap[:, 1:2]), reads=[s_], writes=[s_])
        kb.op("dve", lambda e, x_=x_, n_=n_, s_=s_: e.scalar_tensor_tensor(out=n_.ap, in0=x_.ap, scalar=s_.ap[:, 2:3], in1=gain.ap,
                                                                         op0=ALU.mult, op1=ALU.mult), reads=[x_, s_, gain], writes=[n_])
        b0, b1 = g.bank[4 + 2 * (i % 2)], g.bank[5 + 2 * (i % 2)]
        pv = g.PS[:, (4 + 2 * (i % 2)) * 512:(6 + 2 * (i % 2)) * 512].bitcast(BF16).rearrange("p (k t) -> p k t", k=KC)
        kb.group_begin("pe", reads=[n_, g.ident], writes=[b0, b1])
        for kc in range(KC):
            f = lambda e, kc=kc, n_=n_, pv=pv: e.transpose(out=pv[:, kc, :], in_=n_.ap[:, kc * 128:(kc + 1) * 128], identity=g.ident.ap)
            if kc == KC - 1:
                kb.group_end("pe", f, reads=[n_, g.ident], writes=[b0, b1])
            else:
                kb.raw("pe", f)
        eng = "act" if i % 2 == 0 else "dve"
        if eng == "act":
            kb.op("act", lambda e, pv=pv, i=i: e.copy(out=XT.ap[:, 0:KC, i * 128:(i + 1) * 128], in_=pv), writes=[XT, b0, b1])
        else:
            kb.op("dve", lambda e, pv=pv, i=i: e.tensor_copy(out=XT.ap[:, 0:KC, i * 128:(i + 1) * 128], in_=pv), writes=[XT, b0, b1])
    kb.barrier()
    ar.reset(m)


def load_xt(kb, g, XT, src_t, row0, nk, seg):
    SEG = g.SEG
    for k0 in range(0, nk, 8):
        k1 = min(nk, k0 + 8)
        src = src_t.ap()[row0 + k0 * 128: row0 + k1 * 128, seg * SEG:(seg + 1) * SEG].rearrange("(k p) t -> p k t", p=128)
        kb.dma("sp", lambda e, src=src, k0=k0, k1=k1: e.dma_start(out=XT.ap[:, k0:k1, :], in_=src), writes=[XT])


def phase_G1(kb, g, l, seg):
    nc, ar = g.nc, g.ar
    SEG, NTS, NTT = g.SEG, g.SEG // 128, g.SEG // 512
    m0 = ar.mark()
    XT = ar.alloc("XT", [KC, SEG], BF16)
    xsrc = g.x_in.ap() if l == 0 else g.xr.ap()
    front_end(kb, g, xsrc, g.norm1, l * D, seg, XT)
    slabs = [ar.alloc("ws", [KC, 512], BF16) for _ in range(3)]
    fst = Ring([ar.alloc("fst", [SEG], BF16) for _ in range(3)])
    tst = Ring([ar.alloc("tst", [4, 512], BF16) for _ in range(2)])
    banks = Ring([0, 1, 2, 3])
    evc = [0]
    tok0 = seg * SEG
    groups = [(C_AQ, A_Q, "F", R_AQ, AF.Copy, QS), (C_AK, A_KV, "F", R_AK, AF.Copy, 1.0), (C_AV, A_KV, "T", T_AV, None, 1.0),
              (C_BQ, B_QK, "F", R_BQ, AF.Copy, QS), (C_BK, B_QK, "F", R_BK, AF.Copy, 1.0), (C_BV, B_V, "T", T_BV, None, 1.0),
              (C_BOG, B_V, "F", R_OG, AF.Silu, 1.0), (C_BGR, 32, "F", R_GR, AF.Copy, 1.0),
              (C_CQ, C_W, "F", R_CQ, AF.Copy, QS), (C_CK, C_W, "F", R_CK, AF.Copy, 1.0), (C_CV, C_W, "T", T_CV, None, 1.0),
              (C_GL, GL, "F", R_GL, AF.Sigmoid, 1.0)]
    jobs = []
    for (c0, wtot, kind, dst, func, scale) in groups:
        for s0 in range(0, wtot, 512):
            w = min(512, wtot - s0)

            def load(slab, c0=c0, s0=s0, w=w):
                src = g.w_in.ap()[l, :, c0 + s0:c0 + s0 + w].rearrange("(k p) n -> p k n", p=128)
                kb.dma("pool", lambda e: e.dma_start(out=slab.ap[:, :, 0:w], in_=src), writes=[slab])

            if kind == "F":
                def comp(slab, s0=s0, w=w, dst=dst, func=func, scale=scale):
                    for c in range((w + 127) // 128):
                        cw = min(128, w - c * 128)
                        stg = fst.next()
                        for t in range(NTT):
                            b = banks.next()
                            out_ap = g.PS[0:cw, b * 512:(b + 1) * 512]
                            pairs = [(slab.ap[:, kc, c * 128:c * 128 + cw], XT.ap[:, kc, t * 512:(t + 1) * 512]) for kc in range(KC)]
                            mm_group(kb, out_ap, pairs, [slab, XT], g.bank[b])
                            kb.op("act", lambda e, out_ap=out_ap, stg=stg, t=t, cw=cw: e.activation(
                                out=stg.ap[0:cw, t * 512:(t + 1) * 512], in_=out_ap, func=func, scale=scale), writes=[g.bank[b], stg])
                        r0 = dst + s0 + c * 128
                        kb.dma("sp", lambda e, stg=stg, r0=r0, cw=cw: e.dma_start(out=g.pfm.ap()[r0:r0 + cw, tok0:tok0 + SEG], in_=stg.ap[0:cw, :]),
                               reads=[stg], writes=[g.v_pfm])
            else:
                def comp(slab, s0=s0, w=w, dst=dst):
                    for i4 in range(NTS // 4):
                        stg = tst.next()
                        for ii in range(4):
                            i = i4 * 4 + ii
                            b = banks.next()
                            out_ap = g.PS[:, b * 512:b * 512 + w]
                            pairs = [(XT.ap[:, kc, i * 128:(i + 1) * 128], slab.ap[:, kc, 0:w]) for kc in range(KC)]
                            mm_group(kb, out_ap, pairs, [slab, XT], g.bank[b])
                            evc[0] += 1
                            if evc[0] % 2:
                                kb.op("dve", lambda e, out_ap=out_ap, stg=stg, ii=ii: e.tensor_copy(out=stg.ap[:, ii, 0:w], in_=out_ap),
                                      writes=[g.bank[b], stg])
                            else:
                                kb.op("act", lambda e, out_ap=out_ap, stg=stg, ii=ii: e.copy(out=stg.ap[:, ii, 0:w], in_=out_ap),
                                      writes=[g.bank[b], stg])
                        t0 = tok0 + i4 * 512
                        dstap = g.ptm.ap()[t0:t0 + 512, dst + s0:dst + s0 + w].rearrange("(i p) c -> p i c", p=128)
                        kb.dma("sp", lambda e, stg=stg, dstap=dstap: e.dma_start(out=dstap, in_=stg.ap[:, :, 0:w]), reads=[stg], writes=[g.v_ptm])
            jobs.append((load, comp))
    run_jobs(jobs, slabs)
    kb.barrier()
    ar.reset(m0)


def phase_G2(kb, g, l, seg):
    ar = g.ar
    SEG, NTT = g.SEG, g.SEG // 512
    m0 = ar.mark()
    tok0 = seg * SEG
    OT = [ar.alloc("OT", [8, SEG], BF16) for _ in range(3)]
    for i, src in enumerate((g.oaT, g.obT, g.ocT)):
        load_xt(kb, g, OT[i], src, 0, 8, seg)
    slabs = [ar.alloc("ws", [3, 8, 512], BF16) for _ in range(2)]
    gts = Ring([ar.alloc("gts", [3, SEG], BF16) for _ in range(2)])
    mst = Ring([ar.alloc("mst", [SEG], BF16) for _ in range(2)])
    tmp = Ring([ar.alloc("tmp", [3, 512], F32) for _ in range(2)])
    bsets = Ring([(0, 1, 2), (3, 4, 5)])
    wbr = (g.w_br_a, g.w_br_b, g.w_br_c)
    jobs = []
    for js in range(4):
        def load(slab, js=js):
            for i in range(3):
                src = wbr[i].ap()[l, :, js * 512:(js + 1) * 512].rearrange("(k p) n -> p k n", p=128)
                kb.dma("pool", lambda e, src=src, i=i: e.dma_start(out=slab.ap[:, i, :, :], in_=src), writes=[slab])

        def comp(slab, js=js):
            for mch in range(4):
                fch = js * 4 + mch
                gt = gts.next()
                gsrc = dram_ap(g.pfm, (R_GL + fch * 128) * g.T + tok0, [[g.T, 128], [D * g.T, 3], [1, SEG]])
                kb.dma("sp", lambda e, gt=gt, gsrc=gsrc: e.dma_start(out=gt.ap, in_=gsrc), reads=[g.v_pfm], writes=[gt])
                stg = mst.next()
                for t in range(NTT):
                    bs = bsets.next()
                    tp = tmp.next()
                    for i in range(3):
                        out_ap = g.PS[:, bs[i] * 512:(bs[i] + 1) * 512]
                        pairs = [(slab.ap[:, i, kc, mch * 128:(mch + 1) * 128], OT[i].ap[:, kc, t * 512:(t + 1) * 512]) for kc in range(8)]
                        mm_group(kb, out_ap, pairs, [slab, OT[i]], g.bank[bs[i]])
                        kb.op("dve", lambda e, out_ap=out_ap, tp=tp, gt=gt, i=i, t=t: e.tensor_tensor(
                            out=tp.ap[:, i, :], in0=out_ap, in1=gt.ap[:, i, t * 512:(t + 1) * 512], op=ALU.mult),
                            reads=[gt], writes=[g.bank[bs[i]], tp])
                    kb.op("pool", lambda e, tp=tp: e.tensor_tensor(out=tp.ap[:, 0, :], in0=tp.ap[:, 0, :], in1=tp.ap[:, 1, :], op=ALU.add),
                          writes=[tp])
                    kb.op("pool", lambda e, tp=tp, stg=stg, t=t: e.tensor_tensor(out=stg.ap[:, t * 512:(t + 1) * 512], in0=tp.ap[:, 0, :],
                                                                                 in1=tp.ap[:, 2, :], op=ALU.add), reads=[tp], writes=[stg])
                kb.dma("sp", lambda e, stg=stg, fch=fch: e.dma_start(out=g.mT.ap()[fch * 128:(fch + 1) * 128, tok0:tok0 + SEG], in_=stg.ap),
                       reads=[stg], writes=[g.v_mT])
        jobs.append((load, comp))
    run_jobs(jobs, slabs)
    kb.barrier()
    ar.reset(m0)


def tm_update_jobs(kb, g, XT, nk, wsrc_fn, xsrc, slabs, seg):
    ar = g.ar
    SEG, NTS = g.SEG, g.SEG // 128
    tok0 = seg * SEG
    xo = Ring([ar.alloc("xo", [4, 512], F32) for _ in range(2)])
    xs = Ring([ar.alloc("xs", [4, 512], F32) for _ in range(2)])
    banks = Ring([0, 1, 2, 3])
    jobs = []
    for js in range(4):
        def load(slab, js=js):
            for k0 in range(0, nk, 8):
                k1 = min(nk, k0 + 8)
                kb.dma("pool", lambda e, k0=k0, k1=k1: e.dma_start(out=slab.ap[:, k0:k1, :], in_=wsrc_fn(js, k0, k1)), writes=[slab])

        def comp(slab, js=js):
            for i4 in range(NTS // 4):
                t0 = tok0 + i4 * 512
                xold, xnew = xo.next(), xs.next()
                sap = xsrc[t0:t0 + 512, js * 512:(js + 1) * 512].rearrange("(i p) c -> p i c", p=128)
                kb.dma("sp", lambda e, xold=xold, sap=sap: e.dma_start(out=xold.ap, in_=sap), reads=[g.v_xr], writes=[xold])
                for ii in range(4):
                    i = i4 * 4 + ii
                    b = banks.next()
                    out_ap = g.PS[:, b * 512:(b + 1) * 512]
                    pairs = [(XT.ap[:, kc, i * 128:(i + 1) * 128], slab.ap[:, kc, :]) for kc in range(nk)]
                    mm_group(kb, out_ap, pairs, [slab, XT], g.bank[b])
                    kb.op("dve", lambda e, out_ap=out_ap, xold=xold, xnew=xnew, ii=ii: e.tensor_tensor(
                        out=xnew.ap[:, ii, :], in0=out_ap, in1=xold.ap[:, ii, :], op=ALU.add), reads=[xold], writes=[g.bank[b], xnew])
                dap = g.xr.ap()[t0:t0 + 512, js * 512:(js + 1) * 512].rearrange("(i p) c -> p i c", p=128)
                kb.dma("sp", lambda e, xnew=xnew, dap=dap: e.dma_start(out=dap, in_=xnew.ap), reads=[xnew], writes=[g.v_xr])
        jobs.append((load, comp))
    run_jobs(jobs, slabs)


def phase_G3(kb, g, l, seg):
    ar = g.ar
    m0 = ar.mark()
    XT = ar.alloc("XT", [KC, g.SEG], BF16)
    load_xt(kb, g, XT, g.mT, 0, KC, seg)
    slabs = [ar.alloc("ws", [KC, 512], BF16) for _ in range(3)]
    xsrc = g.x_in.ap() if l == 0 else g.xr.ap()

    def wsrc(js, k0, k1):
        return g.w_out.ap()[l, k0 * 128:k1 * 128, js * 512:(js + 1) * 512].rearrange("(k p) n -> p k n", p=128)
    tm_update_jobs(kb, g, XT, KC, wsrc, xsrc, slabs, seg)
    kb.barrier()
    ar.reset(m0)


def phase_G4(kb, g, l, seg):
    ar = g.ar
    SEG, NTT = g.SEG, g.SEG // 512
    m0 = ar.mark()
    tok0 = seg * SEG
    XT = ar.alloc("XT", [KC, SEG], BF16)
    front_end(kb, g, g.xr.ap(), g.norm2, l * D, seg, XT)
    slabs = [ar.alloc("ws", [KC, 2, 256], BF16) for _ in range(3)]
    fst = Ring([ar.alloc("fst", [SEG], BF16) for _ in range(3)])
    tmp = Ring([ar.alloc("tmp", [512], F32) for _ in range(2)])
    bsets = Ring([(0, 1), (2, 3), (4, 5)])
    jobs = []
    for jf in range(DFF // 256):
        def load(slab, jf=jf):
            for part in range(2):
                src = g.w_ffn_in.ap()[l, :, part * DFF + jf * 256: part * DFF + (jf + 1) * 256].rearrange("(k p) n -> p k n", p=128)
                kb.dma("pool", lambda e, src=src, part=part: e.dma_start(out=slab.ap[:, :, part, :], in_=src), writes=[slab])

        def comp(slab, jf=jf):
            for c in range(2):
                stg = fst.next()
                for t in range(NTT):
                    bg, bu = bsets.next()
                    og = g.PS[:, bg * 512:(bg + 1) * 512]
                    ou = g.PS[:, bu * 512:(bu + 1) * 512]
                    mm_group(kb, og, [(slab.ap[:, kc, 0, c * 128:(c + 1) * 128], XT.ap[:, kc, t * 512:(t + 1) * 512]) for kc in range(KC)],
                             [slab, XT], g.bank[bg])
                    mm_group(kb, ou, [(slab.ap[:, kc, 1, c * 128:(c + 1) * 128], XT.ap[:, kc, t * 512:(t + 1) * 512]) for kc in range(KC)],
                             [slab, XT], g.bank[bu])
                    tp = tmp.next()
                    kb.op("act", lambda e, og=og, tp=tp: e.activation(out=tp.ap, in_=og, func=AF.Silu), writes=[g.bank[bg], tp])
                    kb.op("dve", lambda e, ou=ou, tp=tp, stg=stg, t=t: e.tensor_tensor(out=stg.ap[:, t * 512:(t + 1) * 512], in0=ou, in1=tp.ap,
                                                                                     op=ALU.mult), reads=[tp], writes=[g.bank[bu], stg])
                r0 = jf * 256 + c * 128
                kb.dma("sp", lambda e, stg=stg, r0=r0: e.dma_start(out=g.actT.ap()[r0:r0 + 128, tok0:tok0 + SEG], in_=stg.ap),
                       reads=[stg], writes=[g.v_actT])
        jobs.append((load, comp))
    run_jobs(jobs, slabs)
    kb.barrier()
    ar.reset(m0)


def phase_G5(kb, g, l, seg):
    ar = g.ar
    HK = DFF // 256
    for kh in range(2):
        m0 = ar.mark()
        XT = ar.alloc("XT", [HK, g.SEG], BF16)
        load_xt(kb, g, XT, g.actT, kh * HK * 128, HK, seg)
        slabs = [ar.alloc("ws", [HK, 512], BF16) for _ in range(2)]

        def wsrc(js, k0, k1, kh=kh):
            r0 = kh * HK * 128
            return g.w_ffn_out.ap()[l, r0 + k0 * 128:r0 + k1 * 128, js * 512:(js + 1) * 512].rearrange("(k p) n -> p k n", p=128)
        tm_update_jobs(kb, g, XT, HK, wsrc, g.xr.ap(), slabs, seg)
        kb.barrier()
        ar.reset(m0)


def phase_final(kb, g):
    ar = g.ar
    m0 = ar.mark()
    gain = ar.alloc("gain", [D], F32)
    kb.dma("sp", lambda e: e.dma_start(out=gain.ap, in_=dram_ap(g.norm_f, 0, [[0, 128], [1, D]])), writes=[gain])
    xt = [ar.alloc("xt", [D], F32) for _ in range(3)]
    yo = [ar.alloc("yo", [D], F32) for _ in range(3)]
    junk = ar.alloc("junk", [D], BF16)
    st = [ar.alloc("st", [4], F32) for _ in range(3)]
    for i in range(g.T // 128):
        x_, y_, s_ = xt[i % 3], yo[i % 3], st[i % 3]
        kb.dma("sp", lambda e, x_=x_, i=i: e.dma_start(out=x_.ap, in_=g.xr.ap()[i * 128:(i + 1) * 128, :]), reads=[g.v_xr], writes=[x_])
        kb.op("act", lambda e, x_=x_, s_=s_: e.activation(out=junk.ap, in_=x_.ap, func=AF.Square, accum_out=s_.ap[:, 0:1]),
              reads=[x_], writes=[junk, s_])
        kb.op("act", lambda e, s_=s_: e.activation(out=s_.ap[:, 1:2], in_=s_.ap[:, 0:1], func=AF.Sqrt, scale=1.0 / D, bias=g.eps.ap[:, 0:1]),
              reads=[s_, g.eps], writes=[s_])
        kb.op("dve", lambda e, s_=s_: e.reciprocal(out=s_.ap[:, 2:3], in_=s_.ap[:, 1:2]), reads=[s_], writes=[s_])
        kb.op("dve", lambda e, x_=x_, y_=y_, s_=s_: e.scalar_tensor_tensor(out=y_.ap, in0=x_.ap, scalar=s_.ap[:, 2:3], in1=gain.ap,
                                                                         op0=ALU.mult, op1=ALU.mult), reads=[x_, s_, gain], writes=[y_])
        kb.dma("sp", lambda e, y_=y_, i=i: e.dma_start(out=g.y.ap()[i * 128:(i + 1) * 128, :], in_=y_.ap), reads=[y_], writes=[g.v_y])
    kb.barrier()
    ar.reset(m0)


def run_pipeline(n_iter, make_stages, ns):
    st = {}
    for tau in range(n_iter + ns - 1):
        for k in reversed(range(ns)):
            i = tau - k
            if 0 <= i < n_iter:
                if i not in st:
                    st[i] = make_stages(i)
                st[i][k]()
                if k == ns - 1:
                    del st[i]


def attn_tail(kb, g, Sviews, nk, nkt, vbuf, vt0, small, Pf, Pn, pTp, pTpb, pT, oTp, oTpb, ost, ocol, sink_ap):
    S_ap, Sbanks = Sviews

    def s1a():
        if sink_ap is None:
            kb.op("dve", lambda e: e.tensor_reduce(out=small.ap[:, 0:1], in_=S_ap[:, 0:nk], axis=AX.X, op=ALU.max, negate=True),
                  writes=Sbanks + [small])
        else:
            kb.op("dve", lambda e: e.tensor_reduce(out=small.ap[:, 4:5], in_=S_ap[:, 0:nk], axis=AX.X, op=ALU.max), writes=Sbanks + [small])
            kb.op("dve", lambda e: e.tensor_scalar(out=small.ap[:, 0:1], in0=small.ap[:, 4:5], scalar1=sink_ap, scalar2=-1.0,
                                                   op0=ALU.max, op1=ALU.mult), reads=[g.sinkb], writes=[small])

    def s1b():
        kb.op("act", lambda e: e.activation(out=Pf.ap[:, 0:nk], in_=S_ap[:, 0:nk], func=AF.Exp, bias=small.ap[:, 0:1],
                                            accum_out=small.ap[:, 1:2]), writes=Sbanks + [small, Pf])
        if sink_ap is not None:
            kb.op("act", lambda e: e.activation(out=small.ap[:, 2:3], in_=sink_ap, func=AF.Exp, bias=small.ap[:, 0:1]),
                  reads=[g.sinkb], writes=[small])

    def s1c():
        if sink_ap is not None:
            kb.op("dve", lambda e: e.tensor_tensor(out=small.ap[:, 1:2], in0=small.ap[:, 1:2], in1=small.ap[:, 2:3], op=ALU.add),
                  writes=[small])
        kb.op("dve", lambda e: e.reciprocal(out=small.ap[:, 3:4], in_=small.ap[:, 1:2]), writes=[small])
        kb.op("pool", lambda e: e.tensor_scalar(out=Pn.ap[:, 0:nk], in0=Pf.ap[:, 0:nk], scalar1=small.ap[:, 3:4], scalar2=None, op0=ALU.mult),
              reads=[Pf, small], writes=[Pn])

    def s2a():
        kb.group_begin("pe", reads=[Pn, g.ident], writes=[pTpb])
        for k in range(nkt):
            f = lambda e, k=k: e.transpose(out=pTp[:, k, :], in_=Pn.ap[:, k * 128:(k + 1) * 128], identity=g.ident.ap)
            if k == nkt - 1:
                kb.group_end("pe", f, reads=[Pn, g.ident], writes=[pTpb])
            else:
                kb.raw("pe", f)

    def s2b():
        kb.op("act", lambda e: e.copy(out=pT.ap[:, 0:nkt, :], in_=pTp[:, 0:nkt, :]), writes=[pTpb, pT])

    def s3a():
        pairs = [(vbuf.ap[:, vt0 + k, :], pT.ap[:, k, :]) for k in range(nkt)]
        mm_group(kb, oTp, pairs, [vbuf, pT], oTpb)

    def s3b():
        kb.op("dve", lambda e: e.tensor_copy(out=ost.ap[:, ocol:ocol + 128], in_=oTp), writes=[oTpb, ost])
    return [s1a, s1b, s1c, s2a, s2b, s3a, s3b]


def mixer_A(kb, g, l):
    ar = g.ar
    T, NT = g.T, g.T // 128
    m0 = ar.mark()
    ncA = len(g.uniqA)
    biasA = ar.alloc("biasA", [ncA * 8, 384], BF16)
    kb.dma("pool", lambda e: e.dma_start(out=biasA.ap, in_=g.d_biasA.ap().rearrange("c p k -> p c k")), writes=[biasA])
    g.sinkb = ar.alloc("sink", [8], F32)
    kb.dma("sp", lambda e: e.dma_start(out=g.sinkb.ap, in_=dram_ap(g.sink_a, l * 8, [[0, 128], [1, 8]])), writes=[g.sinkb])
    kT = [ar.alloc("kT", [T], BF16) for _ in range(2)]
    vv = [ar.alloc("vv", [NT, 128], BF16) for _ in range(2)]
    qT = [ar.alloc("qT", [T], BF16) for _ in range(2)]
    Pf = [ar.alloc("Pf", [384], F32) for _ in range(2)]
    Pn = [ar.alloc("Pn", [384], BF16) for _ in range(2)]
    pT = [ar.alloc("pT", [3, 128], BF16) for _ in range(2)]
    ost = [ar.alloc("ost", [2048], BF16) for _ in range(2)]
    small = [ar.alloc("sm", [8], F32) for _ in range(4)]
    for gi in range(2):
        kb.dma("sp", lambda e, gi=gi: e.dma_start(out=kT[gi].ap, in_=g.pfm.ap()[R_AK + gi * 128:R_AK + (gi + 1) * 128, :]),
               reads=[g.v_pfm], writes=[kT[gi]])
        kb.dma("sp", lambda e, gi=gi: e.dma_start(out=vv[gi].ap, in_=g.ptm.ap()[:, T_AV + gi * 128:T_AV + (gi + 1) * 128].rearrange(
            "(n p) d -> p n d", p=128)), reads=[g.v_ptm], writes=[vv[gi]])

    def load_q(h):
        kb.dma("sp", lambda e: e.dma_start(out=qT[h % 2].ap, in_=g.pfm.ap()[R_AQ + h * 128:R_AQ + (h + 1) * 128, :]),
               reads=[g.v_pfm], writes=[qT[h % 2]])
    load_q(0)
    cnt = [0]
    for h in range(8):
        if h + 1 < 8:
            load_q(h + 1)
        gi = h // 4
        q_, k_, v_ = qT[h % 2], kT[gi], vv[gi]

        def make(n, h=h, q_=q_, k_=k_, v_=v_):
            it = cnt[0]
            cnt[0] += 1
            lo, hi = max(n - 1, 0), min(n + 1, NT - 1)
            nkt = hi - lo + 1
            nk = nkt * 128
            off = (lo - (n - 1)) * 128
            cls = g.clsA[n]
            sb = it % 3
            S_ap = g.PS[:, sb * 512:sb * 512 + 512]
            pb = 3 + it % 2
            pTp = g.PS[:, pb * 512:(pb + 1) * 512].bitcast(BF16)[:, 0:384].rearrange("p (k t) -> p k t", k=3)
            ob = 5 + it % 2
            oTp = g.PS[:, ob * 512:ob * 512 + 128]
            os_ = ost[(n // 16) % 2]

            def st0():
                kb.group_begin("pe", reads=[q_, k_, biasA, g.ident], writes=[g.bank[sb]])
                kb.raw("pe", lambda e: e.matmul(S_ap[:, 0:nk], lhsT=q_.ap[:, n * 128:(n + 1) * 128], rhs=k_.ap[:, lo * 128:(hi + 1) * 128],
                                                start=True, stop=False))
                kb.group_end("pe", lambda e: e.matmul(S_ap[:, 0:nk], lhsT=g.ident.ap, rhs=biasA.ap[:, cls * 8 + h, off:off + nk],
                                                      start=False, stop=True), reads=[q_, k_, biasA, g.ident], writes=[g.bank[sb]])
            tail = attn_tail(kb, g, (S_ap, [g.bank[sb]]), nk, nkt, v_, lo, small[it % 4], Pf[it % 2], Pn[it % 2], pTp, g.bank[pb],
                             pT[it % 2], oTp, g.bank[ob], os_, (n % 16) * 128, g.sinkb.ap[:, h:h + 1])

            def st3b():
                tail[6]()
                if n % 16 == 15 or n == NT - 1:
                    t0 = (n // 16) * 2048
                    nt = (n % 16 + 1) * 128
                    kb.dma("sp", lambda e: e.dma_start(out=g.oaT.ap()[h * 128:(h + 1) * 128, t0:t0 + nt], in_=os_.ap[:, 0:nt]),
                           reads=[os_], writes=[g.v_oaT])
            return [st0] + tail[0:6] + [st3b]
        run_pipeline(NT, make, 8)
    kb.barrier()
    ar.reset(m0)


def build_rpb_table(kb, g, l, rpbT):
    ar = g.ar
    zt = ar.alloc("zt", [960], F32)
    kb.op("pool", lambda e: e.memset(zt.ap, 0.0), writes=[zt])
    for h in range(8):
        kb.dma("sp", lambda e, h=h: e.dma_start(out=dram_ap(g.Gtab, h * 15 * 64 * 128, [[960, 128], [1, 960]]), in_=zt.ap),
               reads=[zt], writes=[g.v_G])
    for h in range(8):
        kb.dma("sp", lambda e, h=h: e.dma_start(out=dram_ap(g.Gtab, h * 15 * 64 * 128 + 48, [[64 * 128, 15], [128, 64], [1, 31]]),
                                               in_=dram_ap(g.rpb_c, (l * 8 + h) * 15 * 31, [[31, 15], [0, 64], [1, 31]])), writes=[g.v_G])
    for h in range(8):
        for rr in range(2):
            kb.dma("pool", lambda e, h=h, rr=rr: e.dma_start(
                out=rpbT.ap[rr * 64:(rr + 1) * 64, h, :].rearrange("p (i k) -> p i k", i=14),
                in_=dram_ap(g.Gtab, h * 15 * 64 * 128 + (1 - rr) * 64 * 128 + 63, [[127, 64], [64 * 128, 14], [1, 64]])),
                reads=[g.v_G], writes=[rpbT])


def mixer_C(kb, g, l):
    ar = g.ar
    T, NT = g.T, g.T // 128
    m0 = ar.mark()
    ncC = len(g.uniqC)
    rpbT = ar.alloc("rpbT", [8, 896], BF16)
    build_rpb_table(kb, g, l, rpbT)
    maskC = ar.alloc("maskC", [ncC, 896], BF16)
    kb.dma("pool", lambda e: e.dma_start(out=maskC.ap, in_=g.d_maskC.ap().rearrange("c p k -> p c k")), writes=[maskC])
    qT = [ar.alloc("qT", [T], BF16) for _ in range(2)]
    kT = [ar.alloc("kT", [T], BF16) for _ in range(2)]
    vv = [ar.alloc("vv", [NT, 128], BF16) for _ in range(2)]
    Pf = [ar.alloc("Pf", [896], F32) for _ in range(2)]
    Pn = [ar.alloc("Pn", [896], BF16) for _ in range(2)]
    pT = [ar.alloc("pT", [7, 128], BF16) for _ in range(2)]
    ost = [ar.alloc("ost", [2048], BF16) for _ in range(2)]
    small = [ar.alloc("sm", [8], F32) for _ in range(4)]

    def load_h(h):
        b = h % 2
        kb.dma("sp", lambda e: e.dma_start(out=qT[b].ap, in_=g.pfm.ap()[R_CQ + h * 128:R_CQ + (h + 1) * 128, :]), reads=[g.v_pfm], writes=[qT[b]])
        kb.dma("sp", lambda e: e.dma_start(out=kT[b].ap, in_=g.pfm.ap()[R_CK + h * 128:R_CK + (h + 1) * 128, :]), reads=[g.v_pfm], writes=[kT[b]])
        kb.dma("sp", lambda e: e.dma_start(out=vv[b].ap, in_=g.ptm.ap()[:, T_CV + h * 128:T_CV + (h + 1) * 128].rearrange(
            "(n p) d -> p n d", p=128)), reads=[g.v_ptm], writes=[vv[b]])
    load_h(0)
    cnt = [0]
    for h in range(8):
        if h + 1 < 8:
            load_h(h + 1)
        q_, k_, v_ = qT[h % 2], kT[h % 2], vv[h % 2]

        def make(j, h=h, q_=q_, k_=k_, v_=v_):
            it = cnt[0]
            cnt[0] += 1
            lo, hi = g.bandC[j]
            nk = (hi - lo) * 64
            nkt = (nk + 127) // 128
            rel = (lo - (2 * j - 6)) * 64
            cls = g.clsC[j]
            sb = 2 * (it % 2)
            S_ap = g.PS[:, sb * 512:sb * 512 + 1024]
            Sb = [g.bank[sb], g.bank[sb + 1]]
            pb = 4 + it % 2
            pTp = g.PS[:, pb * 512:(pb + 1) * 512].bitcast(BF16)[:, 0:896].rearrange("p (k t) -> p k t", k=7)
            ob = 6 + it % 2
            oTp = g.PS[:, ob * 512:ob * 512 + 128]
            os_ = ost[(j // 16) % 2]

            def st0():
                rd = [q_, k_, rpbT, maskC, g.ident]
                for ci, (c0, c1) in enumerate([(0, min(512, nk)), (512, nk)]):
                    if c1 <= c0:
                        continue
                    kb.group_begin("pe", reads=rd, writes=[Sb[ci]])
                    kb.raw("pe", lambda e, c0=c0, c1=c1: e.matmul(S_ap[:, c0:c1], lhsT=q_.ap[:, j * 128:(j + 1) * 128],
                                                                  rhs=k_.ap[:, lo * 64 + c0:lo * 64 + c1], start=True, stop=False))
                    kb.raw("pe", lambda e, c0=c0, c1=c1: e.matmul(S_ap[:, c0:c1], lhsT=g.ident.ap, rhs=rpbT.ap[:, h, rel + c0:rel + c1],
                                                                  start=False, stop=False))
                    kb.group_end("pe", lambda e, c0=c0, c1=c1: e.matmul(S_ap[:, c0:c1], lhsT=g.ident.ap, rhs=maskC.ap[:, cls, rel + c0:rel + c1],
                                                                        start=False, stop=True), reads=rd, writes=[Sb[ci]])
            tail = attn_tail(kb, g, (S_ap, Sb), nk, nkt, v_, lo // 2, small[it % 4], Pf[it % 2], Pn[it % 2], pTp, g.bank[pb],
                             pT[it % 2], oTp, g.bank[ob], os_, (j % 16) * 128, None)

            def st3b():
                tail[6]()
                if j % 16 == 15 or j == NT - 1:
                    t0 = (j // 16) * 2048
                    nt = (j % 16 + 1) * 128
                    kb.dma("sp", lambda e: e.dma_start(out=g.ocT.ap()[h * 128:(h + 1) * 128, t0:t0 + nt], in_=os_.ap[:, 0:nt]),
                           reads=[os_], writes=[g.v_ocT])
            return [st0] + tail[0:6] + [st3b]
        run_pipeline(NT, make, 8)
    kb.barrier()
    ar.reset(m0)


def mixer_B(kb, g, l):
    ar = g.ar
    T, NT, NTS = g.T, g.T // 128, g.SEG // 128
    NG = NT // 4
    m0 = ar.mark()
    gain2 = ar.alloc("gain2", [2], F32)
    kb.dma("sp", lambda e: e.dma_start(out=gain2.ap, in_=dram_ap(g.gla_norm, l * 256, [[1, 128], [128, 2]]), allow_slow_non_contiguous=True), writes=[gain2])
    Z = ar.alloc("Z", [4, 256], F32)
    Sbf = ar.alloc("Sbf", [4, 256], BF16)
    dec = ar.alloc("dec", [3, 4], F32)
    w2a = ar.alloc("w2a", [512], BF16, parts=17)
    grA = [ar.alloc("grA", [512], BF16, parts=17) for _ in range(2)]
    qB = [ar.alloc("qB", [4, 512], BF16) for _ in range(2)]
    kB = [ar.alloc("kB", [4, 512], BF16) for _ in range(2)]
    vB = [ar.alloc("vB", [4, 1024], BF16) for _ in range(2)]
    E1 = [ar.alloc("E1", [512], F32) for _ in range(2)]
    SP = [ar.alloc("SP", [512], F32) for _ in range(2)]
    EB = [ar.alloc("EB", [4, 128], F32) for _ in range(2)]
    ENB = [ar.alloc("ENB", [4, 128], F32) for _ in range(2)]
    QT = [ar.alloc("QT", [4, 128], BF16) for _ in range(2)]
    KT = [ar.alloc("KT", [4, 128], BF16) for _ in range(2)]
    KTT = [ar.alloc("KTT", [512], BF16) for _ in range(2)]
    ATM = [ar.alloc("ATM", [4, 128], BF16) for _ in range(2)]
    for b in grA:
        kb.op("pool", lambda e, b=b: e.memset(b.ap, 1.0), writes=[b])
    m1 = ar.mark()
    BL, BB, BT, BA, BO0, BO1, BU0, BU1 = range(8)
    PSL = g.PS[:, BL * 512:(BL + 1) * 512]
    PSB = g.PS[:, BB * 512:(BB + 1) * 512]
    PST = g.PS[:, BT * 512:(BT + 1) * 512].bitcast(BF16)[:, 0:512].rearrange("p (h t) -> p h t", h=4)
    PSA = g.PS[:, BA * 512:(BA + 1) * 512]
    PSO = g.PS[:, BO0 * 512:(BO1 + 1) * 512]
    PSU = g.PS[:, BU0 * 512:(BU1 + 1) * 512]
    def one_pass(d):
        ar.reset(m1)
        tri = g.cst.ap[:, 1 + d, :]
        msk = g.cst.ap[:, 3 + d, :]
        lastcol = 127 if d == 0 else 0
        if d == 0:
            OST = [ar.alloc("OST", [8, 512], F32) for _ in range(2)]
        else:
            OST = [ar.alloc("OBs", [8, 512], BF16) for _ in range(2)]
            ogB = [ar.alloc("ogB", [8, 512], BF16) for _ in range(2)]
            ofB = [ar.alloc("ofB", [8, 512], F32) for _ in range(2)]
            O32 = ar.alloc("O32", [8, 128], F32)
            SQ = ar.alloc("SQ", [8, 128], BF16)
            SD = ar.alloc("SD", [4, 128], F32)
            RS = ar.alloc("RS", [4, 128], F32)
            T1 = ar.alloc("T1", [8, 128], F32)
            RSg = ar.alloc("RSg", [2, 4, 128], F32)
        kb.dma("pool", lambda e, d=d: e.dma_start(out=w2a.ap[0:16, :], in_=g.gla_w2.ap()[l, d, :, :]), writes=[w2a])
        kb.dma("pool", lambda e, d=d: e.dma_start(out=w2a.ap[16:17, :], in_=g.gla_b.ap()[l, d:d + 1, :]), writes=[w2a])
        kb.op("dve", lambda e: e.memset(Z.ap, 0.0), writes=[Z])
        kb.op("dve", lambda e: e.memset(Sbf.ap, 0.0), writes=[Sbf])
        kb.op("dve", lambda e: e.memset(dec.ap, 1.0), writes=[dec])
        gorder = list(range(NG)) if d == 0 else list(range(NG - 1, -1, -1))

        def load_group(gi, slot):
            t0 = gi * 512
            kb.dma("sp", lambda e: e.dma_start(out=grA[slot].ap[0:16, :], in_=g.pfm.ap()[R_GR + d * 16:R_GR + d * 16 + 16, t0:t0 + 512]),
                   reads=[g.v_pfm], writes=[grA[slot]])
            kb.dma("sp", lambda e: e.dma_start(out=qB[slot].ap, in_=g.pfm.ap()[R_BQ:R_BQ + 512, t0:t0 + 512].rearrange("(h p) t -> p h t", p=128)),
                   reads=[g.v_pfm], writes=[qB[slot]])
            kb.dma("sp", lambda e: e.dma_start(out=kB[slot].ap, in_=g.pfm.ap()[R_BK:R_BK + 512, t0:t0 + 512].rearrange("(h p) t -> p h t", p=128)),
                   reads=[g.v_pfm], writes=[kB[slot]])
            kb.dma("sp", lambda e: e.dma_start(out=vB[slot].ap, in_=g.ptm.ap()[t0:t0 + 512, T_BV:T_BV + 1024].rearrange("(i p) c -> p i c", p=128)),
                   reads=[g.v_ptm], writes=[vB[slot]])
            if d == 1:
                kb.dma("sp", lambda e: e.dma_start(out=ogB[slot].ap, in_=g.pfm.ap()[R_OG:R_OG + 1024, t0:t0 + 512].rearrange(
                    "(c p) t -> p c t", p=128)), reads=[g.v_pfm], writes=[ogB[slot]])
                kb.dma("sp", lambda e: e.dma_start(out=ofB[slot].ap, in_=g.ofw.ap()[:, t0:t0 + 512].rearrange("(c p) t -> p c t", p=128)),
                       reads=[g.v_ofw], writes=[ofB[slot]])
        load_group(gorder[0], 0)
        tiles = []
        for gpos, gi in enumerate(gorder):
            for k, ii in enumerate(list(range(4)) if d == 0 else [3, 2, 1, 0]):
                tiles.append((gpos, gi, ii, k))

        def make(itx):
            gpos, gi, ii, kpos = tiles[itx]
            slot = gpos % 2
            n = gi * 4 + ii
            it = gpos * 4 + (ii if d == 0 else 3 - ii)
            w = it % 2
            par, pprev = it % 3, (it - 1) % 3
            e1, sp_, eb, enb, qt, kt, ktt, atm = E1[w], SP[w], EB[w], ENB[w], QT[w], KT[w], KTT[w], ATM[w]
            gr_, q_, k_, v_, os_ = grA[slot], qB[slot], kB[slot], vB[slot], OST[slot]
            c0 = ii * 128

            def stage0():
                if kpos == 0 and gpos + 1 < NG:
                    load_group(gorder[gpos + 1], (gpos + 1) % 2)
                mm_group(kb, PSL, [(gr_.ap[0:17, c0:c0 + 128], w2a.ap[0:17, :])], [gr_, w2a], g.bank[BL])
                kb.op("act", lambda e, e1=e1: e.activation(out=e1.ap, in_=PSL, func=AF.Exp, scale=-1.0), writes=[g.bank[BL], e1])
                kb.op("act", lambda e, e1=e1, sp_=sp_: e.activation(out=sp_.ap, in_=e1.ap, func=AF.Ln, bias=1.0), reads=[e1], writes=[sp_])
                kb.group_begin("pe", reads=[sp_, g.cst], writes=[g.bank[BB]])
                for h in range(4):
                    f = lambda e, h=h, sp_=sp_: e.matmul(PSB[:, h * 128:(h + 1) * 128], lhsT=sp_.ap[:, h * 128:(h + 1) * 128], rhs=tri,
                                                         start=True, stop=True)
                    if h == 3:
                        kb.group_end("pe", f, reads=[sp_, g.cst], writes=[g.bank[BB]])
                    else:
                        kb.raw("pe", f)
                kb.op("act", lambda e, eb=eb: e.activation(out=eb.ap.rearrange("p h t -> p (h t)"), in_=PSB, func=AF.Exp),
                      writes=[g.bank[BB], eb])
                kb.op("act", lambda e, enb=enb: e.activation(out=enb.ap.rearrange("p h t -> p (h t)"), in_=PSB, func=AF.Exp, scale=-1.0),
                      writes=[g.bank[BB], enb])
                kb.op("dve", lambda e, eb=eb, par=par: e.tensor_copy(out=dec.ap[:, par, :], in_=eb.ap[:, :, lastcol]), reads=[eb], writes=[dec])
                seg_end = (n % NTS == NTS - 1) if d == 0 else (n % NTS == 0)
                if seg_end:
                    kb.op("dve", lambda e, par=par: e.tensor_scalar(out=dec.ap[:, par, :], in0=dec.ap[:, par, :], scalar1=g.cflag.ap[:, 0:1],
                                                                    scalar2=None, op0=ALU.mult), reads=[g.cflag], writes=[dec])
                kb.op("dve", lambda e, qt=qt, q_=q_, eb=eb, c0=c0: e.tensor_tensor(out=qt.ap, in0=q_.ap[:, :, c0:c0 + 128], in1=eb.ap, op=ALU.mult),
                      reads=[q_, eb], writes=[qt])
                kb.op("pool", lambda e, kt=kt, k_=k_, enb=enb, c0=c0: e.tensor_tensor(out=kt.ap, in0=k_.ap[:, :, c0:c0 + 128], in1=enb.ap,
                                                                                      op=ALU.mult), reads=[k_, enb], writes=[kt])
                kb.group_begin("pe", reads=[kt, g.ident], writes=[g.bank[BT]])
                for h in range(4):
                    f = lambda e, h=h, kt=kt: e.transpose(out=PST[:, h, :], in_=kt.ap[:, h, :], identity=g.ident.ap)
                    if h == 3:
                        kb.group_end("pe", f, reads=[kt, g.ident], writes=[g.bank[BT]])
                    else:
                        kb.raw("pe", f)
                kb.op("act", lambda e, ktt=ktt: e.copy(out=ktt.ap.rearrange("p (h t) -> p h t", h=4), in_=PST), writes=[g.bank[BT], ktt])
                kb.group_begin("pe", reads=[kt, qt], writes=[g.bank[BA]])
                for h in range(4):
                    f = lambda e, h=h, kt=kt, qt=qt: e.matmul(PSA[:, h * 128:(h + 1) * 128], lhsT=kt.ap[:, h, :], rhs=qt.ap[:, h, :],
                                                              start=True, stop=True)
                    if h == 3:
                        kb.group_end("pe", f, reads=[kt, qt], writes=[g.bank[BA]])
                    else:
                        kb.raw("pe", f)
                for h in range(4):
                    kb.op("dve", lambda e, h=h, atm=atm: e.tensor_tensor(out=atm.ap[:, h, :], in0=PSA[:, h * 128:(h + 1) * 128], in1=msk,
                                                                         op=ALU.mult), reads=[g.cst], writes=[g.bank[BA], atm])

            def stage1():
                rdo = [v_, atm, Sbf, qt]
                kb.group_begin("pe", reads=rdo, writes=[g.bank[BO0], g.bank[BO1]])
                for hc in range(8):
                    h, c = hc // 2, hc % 2
                    oap = PSO[:, hc * 128:(hc + 1) * 128]
                    kb.raw("pe", lambda e, oap=oap, h=h, c=c, v_=v_, atm=atm, ii=ii: e.matmul(
                        oap, lhsT=v_.ap[:, ii, h * 256 + c * 128:h * 256 + (c + 1) * 128], rhs=atm.ap[:, h, :], start=True, stop=False))
                    f = lambda e, oap=oap, h=h, c=c, qt=qt: e.matmul(oap, lhsT=Sbf.ap[:, h, c * 128:(c + 1) * 128], rhs=qt.ap[:, h, :],
                                                                     start=False, stop=True)
                    if hc == 7:
                        kb.group_end("pe", f, reads=rdo, writes=[g.bank[BO0], g.bank[BO1]])
                    else:
                        kb.raw("pe", f)
                kb.group_begin("pe", reads=[ktt, v_], writes=[g.bank[BU0], g.bank[BU1]])
                for h in range(4):
                    f = lambda e, h=h, ktt=ktt, v_=v_, ii=ii: e.matmul(PSU[:, h * 256:(h + 1) * 256], lhsT=ktt.ap[:, h * 128:(h + 1) * 128],
                                                                       rhs=v_.ap[:, ii, h * 256:(h + 1) * 256], start=True, stop=True)
                    if h == 3:
                        kb.group_end("pe", f, reads=[ktt, v_], writes=[g.bank[BU0], g.bank[BU1]])
                    else:
                        kb.raw("pe", f)
                for h in range(4):
                    kb.op("dve", lambda e, h=h, pprev=pprev: e.scalar_tensor_tensor(
                        out=Z.ap[:, h, :], in0=Z.ap[:, h, :], scalar=dec.ap[:, pprev, h:h + 1], in1=PSU[:, h * 256:(h + 1) * 256],
                        op0=ALU.mult, op1=ALU.add), reads=[dec], writes=[Z, g.bank[BU0 + h // 2]])
                for h in range(4):
                    kb.op("act", lambda e, h=h, par=par: e.activation(out=Sbf.ap[:, h, :], in_=Z.ap[:, h, :], func=AF.Copy,
                                                                      scale=dec.ap[:, par, h:h + 1]), reads=[Z, dec], writes=[Sbf])
                if d == 0:
                    kb.op("dve", lambda e, os_=os_, c0=c0: e.tensor_copy(out=os_.ap[:, :, c0:c0 + 128], in_=PSO.rearrange("p (c t) -> p c t", c=8)),
                          writes=[g.bank[BO0], g.bank[BO1], os_])
                else:
                    og_, of_ = ogB[slot], ofB[slot]
                    kb.op("dve", lambda e, of_=of_, c0=c0: e.tensor_tensor(out=O32.ap, in0=PSO.rearrange("p (c t) -> p c t", c=8),
                                                                          in1=of_.ap[:, :, c0:c0 + 128], op=ALU.add),
                          reads=[of_], writes=[g.bank[BO0], g.bank[BO1], O32])
                    kb.op("act", lambda e: e.activation(out=SQ.ap, in_=O32.ap, func=AF.Square), reads=[O32], writes=[SQ])
                    kb.group_begin("pe", reads=[SQ, g.onesb], writes=[g.bank[BL]])
                    for hc in range(8):
                        h, c = hc // 2, hc % 2
                        f = lambda e, hc=hc, h=h, c=c: e.matmul(PSL[:, h * 128:(h + 1) * 128], lhsT=g.onesb.ap, rhs=SQ.ap[:, hc, :],
                                                                start=(c == 0), stop=(c == 1))
                        if hc == 7:
                            kb.group_end("pe", f, reads=[SQ, g.onesb], writes=[g.bank[BL]])
                        else:
                            kb.raw("pe", f)
                    kb.op("act", lambda e: e.activation(out=SD.ap.rearrange("p h t -> p (h t)"), in_=PSL, func=AF.Sqrt, scale=1.0 / 256, bias=EPS),
                          writes=[g.bank[BL], SD])
                    kb.op("dve", lambda e: e.reciprocal(out=RS.ap, in_=SD.ap), reads=[SD], writes=[RS])
                    O32v = O32.ap.rearrange("p (h c) t -> p h c t", c=2)
                    T1v = T1.ap.rearrange("p (h c) t -> p h c t", c=2)
                    for c in range(2):
                        kb.op("pool", lambda e, c=c: e.tensor_scalar(out=RSg.ap[:, c, :, :], in0=RS.ap, scalar1=gain2.ap[:, c:c + 1], scalar2=None,
                                                                     op0=ALU.mult), reads=[RS, gain2], writes=[RSg])
                    for c in range(2):
                        kb.op("pool", lambda e, c=c: e.tensor_tensor(out=T1v[:, :, c, :], in0=O32v[:, :, c, :], in1=RSg.ap[:, c, :, :], op=ALU.mult),
                              reads=[O32, RSg], writes=[T1])
                    kb.op("pool", lambda e, og_=og_, os_=os_, c0=c0: e.tensor_tensor(out=os_.ap[:, :, c0:c0 + 128], in0=T1.ap,
                                                                                     in1=og_.ap[:, :, c0:c0 + 128], op=ALU.mult),
                          reads=[T1, og_], writes=[os_])
                if kpos == 3:
                    t0 = gi * 512
                    if d == 0:
                        kb.dma("sp", lambda e, os_=os_, t0=t0: e.dma_start(out=g.ofw.ap()[:, t0:t0 + 512].rearrange("(c p) t -> p c t", p=128), in_=os_.ap),
                               reads=[os_], writes=[g.v_ofw])
                    else:
                        kb.dma("sp", lambda e, os_=os_, t0=t0: e.dma_start(out=g.obT.ap()[:, t0:t0 + 512].rearrange("(c p) t -> p c t", p=128), in_=os_.ap),
                               reads=[os_], writes=[g.v_obT])

            return [stage0, stage1]
        run_pipeline(len(tiles), make, 2)
        kb.barrier()
    one_pass(0)
    one_pass(1)
    ar.reset(m0)


WNAMES = ("norm1", "w_in", "sink_a", "gla_w2", "gla_b", "gla_norm", "rpb_c", "w_br_a", "w_br_b", "w_br_c", "w_out", "norm2",
          "w_ffn_in", "w_ffn_out", "norm_f")
WSHAPES = {"norm1": [2, D], "w_in": [2, D, IN_COLS], "sink_a": [2, 8], "gla_w2": [2, 2, 16, 512], "gla_b": [2, 2, 512],
           "gla_norm": [2, 256], "rpb_c": [2, 8, 15, 31], "w_br_a": [2, 1024, D], "w_br_b": [2, 1024, D], "w_br_c": [2, 1024, D],
           "w_out": [2, D, D], "norm2": [2, D], "w_ffn_in": [2, D, 2 * DFF], "w_ffn_out": [2, DFF, D], "norm_f": [D]}
ARENA_BYTES = 206 * 1024


def build(SEG, NSEG, depth=2, debug=False, stop_after=None):
    nc = bass.Bass("TRN2", target_bir_lowering=False)
    T = SEG * NSEG
    g = G()
    g.nc, g.SEG, g.NSEG, g.T = nc, SEG, NSEG, T
    g.x_in = nc.dram_tensor("x", [T, D], F32, kind="ExternalInput")
    for nm in WNAMES:
        setattr(g, nm, nc.dram_tensor(nm, WSHAPES[nm], F32, kind="ExternalInput"))
    g.bandC, g.clsC, g.uniqC = plan_C(T, SEG)
    g.clsA, g.uniqA = plan_A(T, SEG)
    g.d_cst = nc.dram_tensor("cst", [128, 768], F32, kind="ExternalInput")
    g.d_biasA = nc.dram_tensor("biasA", [len(g.uniqA) * 8, 128, 384], F32, kind="ExternalInput")
    g.d_maskC = nc.dram_tensor("maskC", [len(g.uniqC), 128, 896], F32, kind="ExternalInput")
    g.d_cflag = nc.dram_tensor("cflag", [128, 1], F32, kind="ExternalInput")
    sk = "ExternalOutput" if debug else "Internal"
    g.xr = nc.dram_tensor("xr", [T, D], F32, kind=sk)
    g.pfm = nc.dram_tensor("pfm", [NFM, T], BF16, kind=sk)
    g.ptm = nc.dram_tensor("ptm", [T, NTM], BF16, kind=sk)
    g.oaT = nc.dram_tensor("oaT", [1024, T], BF16, kind=sk)
    g.obT = nc.dram_tensor("obT", [1024, T], BF16, kind=sk)
    g.ocT = nc.dram_tensor("ocT", [1024, T], BF16, kind=sk)
    g.ofw = nc.dram_tensor("ofw", [1024, T], F32, kind=sk)
    g.mT = nc.dram_tensor("mT", [D, T], BF16, kind=sk)
    g.actT = nc.dram_tensor("actT", [DFF, T], BF16, kind=sk)
    g.Gtab = nc.dram_tensor("Gtab", [8 * 15 * 64 * 128], F32)
    g.y = nc.dram_tensor("y", [T, D], F32, kind="ExternalOutput")
    es = ExitStack()
    with es:
        kb = KB(nc, es)
        for nm in ("pfm", "ptm", "oaT", "obT", "ocT", "ofw", "mT", "actT", "xr", "y", "G"):
            setattr(g, "v_" + nm, kb.vbuf("v_" + nm))
        g.ar = Arena(nc, kb, ARENA_BYTES)
        ar = g.ar
        PSt = nc.alloc_psum_tensor("PS", [128, 4096], F32)
        g.PS = PSt[:, :]
        g.bank = [kb.vbuf("bank%d" % i) for i in range(8)]
        g.cst = ar.alloc("cst", [6, 128], F32)
        g.ident = ar.alloc("ident", [128], BF16)
        g.onesb = ar.alloc("onesb", [128], BF16)
        g.cflag = ar.alloc("cflag", [1], F32)
        g.eps = ar.alloc("eps", [1], F32)
        kb.dma("sp", lambda e: e.dma_start(out=g.cst.ap, in_=g.d_cst.ap().rearrange("p (a b) -> p a b", a=6)), writes=[g.cst])
        kb.dma("sp", lambda e: e.dma_start(out=g.cflag.ap, in_=g.d_cflag.ap()), writes=[g.cflag])
        kb.op("dve", lambda e: e.tensor_copy(out=g.ident.ap, in_=g.cst.ap[:, 0, :]), reads=[g.cst], writes=[g.ident])
        kb.op("dve", lambda e: e.tensor_copy(out=g.onesb.ap, in_=g.cst.ap[:, 5, :]), reads=[g.cst], writes=[g.onesb])
        kb.op("dve", lambda e: e.memset(g.eps.ap, EPS), writes=[g.eps])
        ar.base = ar.mark()
        kb.barrier()

        def body():
            for l in range(depth):
                for seg in range(NSEG):
                    phase_G1(kb, g, l, seg)
                if stop_after == "G1":
                    return
                mixer_A(kb, g, l)
                if stop_after == "A":
                    return
                mixer_B(kb, g, l)
                if stop_after == "B":
                    return
                mixer_C(kb, g, l)
                if stop_after == "C":
                    return
                for seg in range(NSEG):
                    phase_G2(kb, g, l, seg)
                    phase_G3(kb, g, l, seg)
                    if stop_after == "G3":
                        continue
                    phase_G4(kb, g, l, seg)
                    phase_G5(kb, g, l, seg)
                if stop_after in ("G3", "L0"):
                    return
            phase_final(kb, g)
        body()
        kb.barrier()
        kb.emit()
        g.n_ins = kb.n_ins
    return nc, g


def core_inputs(typ, SEG, NSEG):
    T = SEG * NSEG
    bands, clsC, uniqC = plan_C(T, SEG)
    clsA, uniqA = plan_A(T, SEG)
    return {"cst": consts_np(), "biasA": bias_A(uniqA, typ), "maskC": masks_C(uniqC, typ),
            "cflag": np.full((128, 1), 1.0 if typ == "P" else 0.0, np.float32)}


_CACHE = {}


def kernel(**inputs):
    SEG, NSEG, NCORE = 2048, 4, 8
    T = SEG * NSEG
    xp = np.asarray(inputs["x_prompt"], np.float32)
    xs = np.asarray(inputs["x_sample"], np.float32)
    assert xp.shape == (1, T, D) and xs.shape[1:] == (SEG, D)
    nsamp = xs.shape[0]
    slots = {0: None}
    counts = [0] * NCORE
    for s in range(nsamp):
        c = 1 + (s % (NCORE - 1))
        counts[c] += 1
    assign, s = {}, 0
    for c in range(1, NCORE):
        assign[c] = list(range(s, s + counts[c]))
        s += counts[c]
        assert counts[c] <= NSEG
    if "nc" not in _CACHE:
        _CACHE["nc"] = build(SEG, NSEG)[0]
    nc = _CACHE["nc"]
    w = {nm: np.ascontiguousarray(np.asarray(inputs[nm], np.float32)) for nm in WNAMES}
    cP, cS = core_inputs("P", SEG, NSEG), core_inputs("S", SEG, NSEG)
    in_maps = []
    for c in range(NCORE):
        if c == 0:
            x = np.ascontiguousarray(xp[0])
            m = dict(cP)
        else:
            x = np.zeros((T, D), np.float32)
            for k, si in enumerate(assign[c]):
                x[k * SEG:(k + 1) * SEG] = xs[si]
            m = dict(cS)
        m["x"] = x
        m.update(w)
        in_maps.append(m)
    res = run_bass_kernel_spmd(nc, in_maps, core_ids=list(range(NCORE)))
    outs = [np.asarray(r["y"], np.float32) for r in res.results]
    y_prompt = outs[0].reshape(1, T, D)
    y_sample = np.zeros((nsamp, SEG, D), np.float32)
    for c in range(1, NCORE):
        for k, si in enumerate(assign[c]):
            y_sample[si] = outs[c][k * SEG:(k + 1) * SEG]
    return (y_prompt, y_sample)
```

```python
import numpy as np
from contextlib import ExitStack
import ml_dtypes

import concourse.bass as bass
import concourse.mybir as mybir
from concourse.bass_utils import run_bass_kernel_spmd

F32 = mybir.dt.float32
BF16 = mybir.dt.bfloat16
AF = mybir.ActivationFunctionType
ALU = mybir.AluOpType
AX = mybir.AxisListType

ENGS = ("pe", "act", "dve", "pool", "sp")


class Buf:
    __slots__ = ("name", "w", "r", "ap", "sem", "nd")

    def __init__(self, name, ap=None):
        self.name = name
        self.w = None
        self.r = []
        self.ap = ap
        self.sem = None
        self.nd = 0


class KB:
    def __init__(self, nc, es):
        self.nc = nc
        self.es = es
        self.prog = {e: [] for e in ENGS}
        self.cnt = {e: 0 for e in ENGS}
        self.sem = {}
        self.waited = {}
        self.nbuf = 0
        for e in ("pe", "act", "dve", "pool"):
            self._mksem("E_" + e)
        self.n_ins = 0
        self.free_sems = []
        self.dsem_cnt = {}

    def release(self, buf):
        if buf.sem is not None:
            self.free_sems.append((buf.sem, buf.nd))
            buf.sem = None

    def barrier(self):
        cur = [("E_" + e, self.cnt[e]) for e in ("pe", "act", "dve", "pool") if self.cnt[e] > 0]
        cur += [(s, 16 * n) for s, n in self.dsem_cnt.items() if n > 0]
        for eng in ENGS:
            waits = []
            for (s, v) in cur:
                if self.waited.get((eng, s), 0) < v:
                    waits.append((s, v))
                    self.waited[(eng, s)] = v
            if waits:
                self.prog[eng].append((waits, None, None, 0))

    def _mksem(self, name):
        self.sem[name] = self.es.enter_context(self.nc.semaphore(name))
        return name

    def sb(self, name, shape, dtype):
        t = self.nc.alloc_sbuf_tensor(name, list(shape), dtype)
        return Buf(name, t)

    def ps(self, name, shape, dtype=F32):
        t = self.nc.alloc_psum_tensor(name, list(shape), dtype)
        return Buf(name, t)

    def vbuf(self, name):
        return Buf(name)

    def _waits_for(self, eng, reads, writes):
        evs = []
        for b in reads:
            if b.w is not None:
                evs.append(b.w)
        for b in writes:
            if b.w is not None:
                evs.append(b.w)
            evs.extend(b.r)
        out = {}
        for (s, v) in evs:
            if self.waited.get((eng, s), 0) >= v:
                continue
            if out.get(s, 0) < v:
                out[s] = v
        for s, v in out.items():
            self.waited[(eng, s)] = v
        return list(out.items())

    def op(self, eng, fn, reads=(), writes=(), signal=True):
        waits = self._waits_for(eng, reads, writes)
        ev = None
        if signal:
            self.cnt[eng] += 1
            ev = ("E_" + eng, self.cnt[eng])
        self.prog[eng].append((waits, fn, ev[0] if ev else None, 1))
        self.n_ins += 1
        if ev is not None:
            for b in reads:
                b.r.append(ev)
            for b in writes:
                b.w = ev
                b.r = []
        return ev

    def group_begin(self, eng, reads=(), writes=()):
        waits = self._waits_for(eng, reads, writes)
        if waits:
            self.prog[eng].append((waits, None, None, 0))

    def raw(self, eng, fn):
        self.prog[eng].append(([], fn, None, 0))
        self.n_ins += 1

    def group_end(self, eng, fn, reads=(), writes=()):
        self.cnt[eng] += 1
        ev = ("E_" + eng, self.cnt[eng])
        self.prog[eng].append(([], fn, ev[0], 1))
        self.n_ins += 1
        for b in reads:
            b.r.append(ev)
        for b in writes:
            b.w = ev
            b.r = []
        return ev

    def dma(self, eng, fn, reads=(), writes=(), owner=None):
        if owner is None:
            owner = writes[0] if writes else reads[0]
        if owner.sem is None:
            if self.free_sems:
                owner.sem, owner.nd = self.free_sems.pop()
            else:
                self.nbuf += 1
                owner.sem = self._mksem("D%d" % self.nbuf)
                self.dsem_cnt[owner.sem] = 0
        waits = self._waits_for(eng, reads, writes)
        owner.nd += 1
        self.dsem_cnt[owner.sem] = owner.nd
        ev = (owner.sem, 16 * owner.nd)
        self.prog[eng].append((waits, fn, owner.sem, 16))
        self.n_ins += 1
        for b in reads:
            b.r.append(ev)
        for b in writes:
            b.w = ev
            b.r = []
        return ev

    def wait_all(self, eng, bufs):
        waits = self._waits_for(eng, (), bufs)
        if waits:
            self.prog[eng].append((waits, None, None, 0))

    def emit(self):
        nc = self.nc
        sem = self.sem
        prog = self.prog

        def run(e_obj, lst):
            for (waits, fn, incs, incv) in lst:
                for (s, v) in waits:
                    e_obj.wait_ge(sem[s], v)
                if fn is not None:
                    ins = fn(e_obj)
                    if incs is not None:
                        ins.then_inc(sem[incs], incv)

        with nc.Block() as block:
            @block.tensor
            def _(e):
                run(e, prog["pe"])

            @block.scalar
            def _(e):
                run(e, prog["act"])

            @block.vector
            def _(e):
                run(e, prog["dve"])

            @block.gpsimd
            def _(e):
                run(e, prog["pool"])

            @block.sync
            def _(e):
                run(e, prog["sp"])


def dram_ap(t, offset, dims):
    return bass.AP(t, offset, [list(d) for d in dims])


D = 2048
KC = 16
DFF = 5632
A_Q, A_KV, B_QK, B_V, C_W, GL = 1024, 256, 512, 1024, 1024, 6144
C_AQ, C_AK, C_AV, C_BQ, C_BK, C_BV, C_BOG, C_BGR, C_CQ, C_CK, C_CV, C_GL = (
    0, 1024, 1280, 1536, 2048, 2560, 3584, 4608, 4640, 5664, 6688, 7712)
IN_COLS = 13856
R_AQ, R_AK, R_BQ, R_BK, R_OG, R_GR, R_CQ, R_CK, R_GL = 0, 1024, 1280, 1792, 2304, 3328, 3360, 4384, 5408
NFM = 5408 + 6144
T_AV, T_BV, T_CV = 0, 256, 1280
NTM = 2304
EPS = 1e-6
NEG = -1e30
QS = 128 ** -0.5


class Arena:
    def __init__(self, nc, kb, nbytes):
        self.t = nc.alloc_sbuf_tensor("arena", [128, nbytes // 2], BF16)
        self.kb = kb
        self.nbytes = nbytes
        self.top = 0
        self.k = 0
        self.live = []

    def alloc(self, name, shape, dtype, parts=128):
        n = 1
        for s in shape:
            n *= s
        sz = 4 if dtype == F32 else 2
        nb = (n * sz + 31) // 32 * 32
        off = self.top
        assert off + nb <= self.nbytes, ("arena overflow", name, off, nb, self.nbytes)
        self.top = off + nb
        v = self.t[0:parts, off // 2: off // 2 + (n * sz) // 2]
        if dtype == F32:
            v = v.bitcast(F32)
        if len(shape) == 2:
            v = v.rearrange("p (a b) -> p a b", a=shape[0])
        elif len(shape) == 3:
            v = v.rearrange("p (a b c) -> p a b c", a=shape[0], b=shape[1])
        self.k += 1
        b = Buf("%s_%d" % (name, self.k), v)
        self.live.append((off, b))
        return b

    def mark(self):
        return self.top

    def reset(self, m):
        while self.live and self.live[-1][0] >= m:
            self.kb.release(self.live.pop()[1])
        self.top = m


class Ring:
    def __init__(self, items):
        self.items = items
        self.i = -1

    def next(self):
        self.i += 1
        return self.items[self.i % len(self.items)]

    def at(self, i):
        return self.items[i % len(self.items)]


class G:
    pass


def start_rows(r, R_tot, RS, typ):
    if typ == "P":
        return min(max(r - 4, 0), R_tot - 8)
    b = (r // RS) * RS
    return b + min(max(r - b - 4, 0), RS - 8)


def plan_C(T, SEG):
    R_tot, RS = T // 64, SEG // 64
    bands, keys = [], []
    for j in range(T // 128):
        r0 = 2 * j
        ss = [start_rows(r0 + rr, R_tot, RS, ty) for ty in "PS" for rr in (0, 1)]
        lo = min(ss) // 2 * 2
        hi = (max(ss) + 8 + 1) // 2 * 2
        lo = max(lo, 0)
        hi = min(hi, R_tot)
        assert lo >= r0 - 6 and hi <= r0 + 8, (j, lo, hi)
        bands.append((lo, hi))
        key = tuple(tuple(start_rows(r0 + rr, R_tot, RS, ty) - r0 for rr in (0, 1)) for ty in "PS")
        keys.append(key)
    uniq = sorted(set(keys))
    cls = [uniq.index(k) for k in keys]
    return bands, cls, uniq


def masks_C(uniq, typ):
    cols = np.arange(64)
    cstart = np.clip(cols - 8, 0, 48)
    inwin = (cols[None, :] >= cstart[:, None]) & (cols[None, :] < cstart[:, None] + 16)
    out = np.full((len(uniq), 128, 14, 64), NEG, np.float32)
    for ci, key in enumerate(uniq):
        rel = key[0 if typ == "P" else 1]
        for rr in (0, 1):
            st = rel[rr]
            for i in range(14):
                row = i - 6
                if st <= row < st + 8:
                    out[ci, rr * 64:(rr + 1) * 64, i, :] = np.where(inwin, 0.0, NEG)
    return out.reshape(len(uniq), 128, 14 * 64)


def plan_A(T, SEG):
    NT, NTS = T // 128, SEG // 128
    keys = []
    for n in range(NT):
        k = []
        for ty in "PS":
            if ty == "P":
                pv, nv = n > 0, n < NT - 1
            else:
                pv, nv = n % NTS != 0, n % NTS != NTS - 1
            k.append((pv, nv))
        keys.append(tuple(k))
    uniq = sorted(set(keys))
    return [uniq.index(k) for k in keys], uniq


def bias_A(uniq, typ):
    q = np.arange(128)[:, None]
    k = np.arange(384)[None, :] - 128
    dist = np.abs(q - k).astype(np.float32)
    out = np.zeros((len(uniq), 8, 128, 384), np.float32)
    for ci, key in enumerate(uniq):
        pv, nv = key[0 if typ == "P" else 1]
        valid = dist <= 128
        valid = valid & (pv | (k >= 0)) & (nv | (k < 128))
        for h in range(8):
            slope = 2.0 ** (-8.0 * (h + 1) / 8)
            out[ci, h] = np.where(valid, -slope * dist, NEG)
    return out.reshape(len(uniq) * 8, 128, 384)


def consts_np():
    s = np.arange(128)[:, None]
    t = np.arange(128)[None, :]
    c = np.zeros((128, 6, 128), np.float32)
    c[:, 0] = (s == t)
    c[:, 1] = np.where(s <= t, -1.0 / 16, 0.0)
    c[:, 2] = np.where(s >= t, -1.0 / 16, 0.0)
    c[:, 3] = (s <= t)
    c[:, 4] = (s >= t)
    c[:, 5] = 1.0
    return c.reshape(128, 768)


def mm_group(kb, out_ap, pairs, reads, bank):
    kb.group_begin("pe", reads=reads, writes=[bank])
    n = len(pairs)
    for i, (l, r) in enumerate(pairs):
        f = (lambda e, l=l, r=r, i=i: e.matmul(out_ap, lhsT=l, rhs=r, start=(i == 0), stop=(i == n - 1)))
        if i == n - 1:
            kb.group_end("pe", f, reads=reads, writes=[bank])
        else:
            kb.raw("pe", f)


def run_jobs(jobs, ring):
    n, nb = len(jobs), len(ring)
    for j in range(min(nb - 1, n)):
        jobs[j][0](ring[j % nb])
    for j in range(n):
        if j + nb - 1 < n:
            jobs[j + nb - 1][0](ring[(j + nb - 1) % nb])
        jobs[j][1](ring[j % nb])


def front_end(kb, g, xsrc, nvec_t, nvec_off, seg, XT):
    nc, ar = g.nc, g.ar
    SEG, NTS = g.SEG, g.SEG // 128
    m = ar.mark()
    gain = ar.alloc("gain", [D], F32)
    kb.dma("sp", lambda e: e.dma_start(out=gain.ap, in_=dram_ap(nvec_t, nvec_off, [[0, 128], [1, D]])), writes=[gain])
    xt = [ar.alloc("xt", [D], F32) for _ in range(2)]
    xn = [ar.alloc("xn", [D], BF16) for _ in range(2)]
    junk = ar.alloc("junk", [D], BF16)
    st = [ar.alloc("st", [4], F32) for _ in range(2)]
    for i in range(NTS):
        x_, n_, s_ = xt[i % 2], xn[i % 2], st[i % 2]
        r0 = seg * SEG + i * 128
        kb.dma("sp", lambda e, x_=x_, r0=r0: e.dma_start(out=x_.ap, in_=xsrc[r0:r0 + 128, :]), writes=[x_])
        kb.op("act", lambda e, x_=x_, s_=s_: e.activation(out=junk.ap, in_=x_.ap, func=AF.Square, accum_out=s_.ap[:, 0:1]),
              reads=[x_], writes=[junk, s_])
        kb.op("act", lambda e, s_=s_: e.activation(out=s_.ap[:, 1:2], in_=s_.ap[:, 0:1], func=AF.Sqrt, scale=1.0 / D, bias=g.eps.ap[:, 0:1]),
              reads=[s_, g.eps], writes=[s_])
        kb.op("dve", lambda e, s_=s_: e.reciprocal(out=s_.ap[:, 2:3], in_=s_.ap[:, 1:2]), reads=[s_], writes=[s_])
        kb.op("dve", lambda e, x_=x_, n_=n_, s_=s_: e.scalar_tensor_tensor(out=n_.ap, in0=x_.ap, scalar=s_.ap[:, 2:3], in1=gain.ap,
                                                                         op0=ALU.mult, op1=ALU.mult), reads=[x_, s_, gain], writes=[n_])
        b0, b1 = g.bank[4 + 2 * (i % 2)], g.bank[5 + 2 * (i % 2)]
        pv = g.PS[:, (4 + 2 * (i % 2)) * 512:(6 + 2 * (i % 2)) * 512].bitcast(BF16).rearrange("p (k t) -> p k t", k=KC)
        kb.group_begin("pe", reads=[n_, g.ident], writes=[b0, b1])
        for kc in range(KC):
            f = lambda e, kc=kc, n_=n_, pv=pv: e.transpose(out=pv[:, kc, :], in_=n_.ap[:, kc * 128:(kc + 1) * 128], identity=g.ident.ap)
            if kc == KC - 1:
                kb.group_end("pe", f, reads=[n_, g.ident], writes=[b0, b1])
            else:
                kb.raw("pe", f)
        eng = "act" if i % 2 == 0 else "dve"
        if eng == "act":
            kb.op("act", lambda e, pv=pv, i=i: e.copy(out=XT.ap[:, 0:KC, i * 128:(i + 1) * 128], in_=pv), writes=[XT, b0, b1])
        else:
            kb.op("dve", lambda e, pv=pv, i=i: e.tensor_copy(out=XT.ap[:, 0:KC, i * 128:(i + 1) * 128], in_=pv), writes=[XT, b0, b1])
    kb.barrier()
    ar.reset(m)


def load_xt(kb, g, XT, src_t, row0, nk, seg):
    SEG = g.SEG
    for k0 in range(0, nk, 8):
        k1 = min(nk, k0 + 8)
        src = src_t.ap()[row0 + k0 * 128: row0 + k1 * 128, seg * SEG:(seg + 1) * SEG].rearrange("(k p) t -> p k t", p=128)
        kb.dma("sp", lambda e, src=src, k0=k0, k1=k1: e.dma_start(out=XT.ap[:, k0:k1, :], in_=src), writes=[XT])


def phase_G1(kb, g, l, seg):
    nc, ar = g.nc, g.ar
    SEG, NTS, NTT = g.SEG, g.SEG // 128, g.SEG // 512
    m0 = ar.mark()
    XT = ar.alloc("XT", [KC, SEG], BF16)
    xsrc = g.x_in.ap() if l == 0 else g.xr.ap()
    front_end(kb, g, xsrc, g.norm1, l * D, seg, XT)
    slabs = [ar.alloc("ws", [KC, 512], BF16) for _ in range(3)]
    fst = Ring([ar.alloc("fst", [SEG], BF16) for _ in range(3)])
    tst = Ring([ar.alloc("tst", [4, 512], BF16) for _ in range(2)])
    banks = Ring([0, 1, 2, 3])
    evc = [0]
    tok0 = seg * SEG
    groups = [(C_AQ, A_Q, "F", R_AQ, AF.Copy, QS), (C_AK, A_KV, "F", R_AK, AF.Copy, 1.0), (C_AV, A_KV, "T", T_AV, None, 1.0),
              (C_BQ, B_QK, "F", R_BQ, AF.Copy, QS), (C_BK, B_QK, "F", R_BK, AF.Copy, 1.0), (C_BV, B_V, "T", T_BV, None, 1.0),
              (C_BOG, B_V, "F", R_OG, AF.Silu, 1.0), (C_BGR, 32, "F", R_GR, AF.Copy, 1.0),
              (C_CQ, C_W, "F", R_CQ, AF.Copy, QS), (C_CK, C_W, "F", R_CK, AF.Copy, 1.0), (C_CV, C_W, "T", T_CV, None, 1.0),
              (C_GL, GL, "F", R_GL, AF.Sigmoid, 1.0)]
    jobs = []
    for (c0, wtot, kind, dst, func, scale) in groups:
        for s0 in range(0, wtot, 512):
            w = min(512, wtot - s0)

            def load(slab, c0=c0, s0=s0, w=w):
                src = g.w_in.ap()[l, :, c0 + s0:c0 + s0 + w].rearrange("(k p) n -> p k n", p=128)
                kb.dma("pool", lambda e: e.dma_start(out=slab.ap[:, :, 0:w], in_=src), writes=[slab])

            if kind == "F":
                def comp(slab, s0=s0, w=w, dst=dst, func=func, scale=scale):
                    for c in range((w + 127) // 128):
                        cw = min(128, w - c * 128)
                        stg = fst.next()
                        for t in range(NTT):
                            b = banks.next()
                            out_ap = g.PS[0:cw, b * 512:(b + 1) * 512]
                            pairs = [(slab.ap[:, kc, c * 128:c * 128 + cw], XT.ap[:, kc, t * 512:(t + 1) * 512]) for kc in range(KC)]
                            mm_group(kb, out_ap, pairs, [slab, XT], g.bank[b])
                            kb.op("act", lambda e, out_ap=out_ap, stg=stg, t=t, cw=cw: e.activation(
                                out=stg.ap[0:cw, t * 512:(t + 1) * 512], in_=out_ap, func=func, scale=scale), writes=[g.bank[b], stg])
                        r0 = dst + s0 + c * 128
                        kb.dma("sp", lambda e, stg=stg, r0=r0, cw=cw: e.dma_start(out=g.pfm.ap()[r0:r0 + cw, tok0:tok0 + SEG], in_=stg.ap[0:cw, :]),
                               reads=[stg], writes=[g.v_pfm])
            else:
                def comp(slab, s0=s0, w=w, dst=dst):
                    for i4 in range(NTS // 4):
                        stg = tst.next()
                        for ii in range(4):
                            i = i4 * 4 + ii
                            b = banks.next()
                            out_ap = g.PS[:, b * 512:b * 512 + w]
                            pairs = [(XT.ap[:, kc, i * 128:(i + 1) * 128], slab.ap[:, kc, 0:w]) for kc in range(KC)]
                            mm_group(kb, out_ap, pairs, [slab, XT], g.bank[b])
                            evc[0] += 1
                            if evc[0] % 2:
                                kb.op("dve", lambda e, out_ap=out_ap, stg=stg, ii=ii: e.tensor_copy(out=stg.ap[:, ii, 0:w], in_=out_ap),
                                      writes=[g.bank[b], stg])
                            else:
                                kb.op("act", lambda e, out_ap=out_ap, stg=stg, ii=ii: e.copy(out=stg.ap[:, ii, 0:w], in_=out_ap),
                                      writes=[g.bank[b], stg])
                        t0 = tok0 + i4 * 512
                        dstap = g.ptm.ap()[t0:t0 + 512, dst + s0:dst + s0 + w].rearrange("(i p) c -> p i c", p=128)
                        kb.dma("sp", lambda e, stg=stg, dstap=dstap: e.dma_start(out=dstap, in_=stg.ap[:, :, 0:w]), reads=[stg], writes=[g.v_ptm])
            jobs.append((load, comp))
    run_jobs(jobs, slabs)
    kb.barrier()
    ar.reset(m0)


def phase_G2(kb, g, l, seg):
    ar = g.ar
    SEG, NTT = g.SEG, g.SEG // 512
    m0 = ar.mark()
    tok0 = seg * SEG
    OT = [ar.alloc("OT", [8, SEG], BF16) for _ in range(3)]
    for i, src in enumerate((g.oaT, g.obT, g.ocT)):
        load_xt(kb, g, OT[i], src, 0, 8, seg)
    slabs = [ar.alloc("ws", [3, 8, 512], BF16) for _ in range(2)]
    gts = Ring([ar.alloc("gts", [3, SEG], BF16) for _ in range(2)])
    mst = Ring([ar.alloc("mst", [SEG], BF16) for _ in range(2)])
    tmp = Ring([ar.alloc("tmp", [3, 512], F32) for _ in range(2)])
    bsets = Ring([(0, 1, 2), (3, 4, 5)])
    wbr = (g.w_br_a, g.w_br_b, g.w_br_c)
    jobs = []
    for js in range(4):
        def load(slab, js=js):
            for i in range(3):
                src = wbr[i].ap()[l, :, js * 512:(js + 1) * 512].rearrange("(k p) n -> p k n", p=128)
                kb.dma("pool", lambda e, src=src, i=i: e.dma_start(out=slab.ap[:, i, :, :], in_=src), writes=[slab])

        def comp(slab, js=js):
            for mch in range(4):
                fch = js * 4 + mch
                gt = gts.next()
                gsrc = dram_ap(g.pfm, (R_GL + fch * 128) * g.T + tok0, [[g.T, 128], [D * g.T, 3], [1, SEG]])
                kb.dma("sp", lambda e, gt=gt, gsrc=gsrc: e.dma_start(out=gt.ap, in_=gsrc), reads=[g.v_pfm], writes=[gt])
                stg = mst.next()
                for t in range(NTT):
                    bs = bsets.next()
                    tp = tmp.next()
                    for i in range(3):
                        out_ap = g.PS[:, bs[i] * 512:(bs[i] + 1) * 512]
                        pairs = [(slab.ap[:, i, kc, mch * 128:(mch + 1) * 128], OT[i].ap[:, kc, t * 512:(t + 1) * 512]) for kc in range(8)]
                        mm_group(kb, out_ap, pairs, [slab, OT[i]], g.bank[bs[i]])
                        kb.op("dve", lambda e, out_ap=out_ap, tp=tp, gt=gt, i=i, t=t: e.tensor_tensor(
                            out=tp.ap[:, i, :], in0=out_ap, in1=gt.ap[:, i, t * 512:(t + 1) * 512], op=ALU.mult),
                            reads=[gt], writes=[g.bank[bs[i]], tp])
                    kb.op("pool", lambda e, tp=tp: e.tensor_tensor(out=tp.ap[:, 0, :], in0=tp.ap[:, 0, :], in1=tp.ap[:, 1, :], op=ALU.add),
                          writes=[tp])
                    kb.op("pool", lambda e, tp=tp, stg=stg, t=t: e.tensor_tensor(out=stg.ap[:, t * 512:(t + 1) * 512], in0=tp.ap[:, 0, :],
                                                                                 in1=tp.ap[:, 2, :], op=ALU.add), reads=[tp], writes=[stg])
                kb.dma("sp", lambda e, stg=stg, fch=fch: e.dma_start(out=g.mT.ap()[fch * 128:(fch + 1) * 128, tok0:tok0 + SEG], in_=stg.ap),
                       reads=[stg], writes=[g.v_mT])
        jobs.append((load, comp))
    run_jobs(jobs, slabs)
    kb.barrier()
    ar.reset(m0)


def tm_update_jobs(kb, g, XT, nk, wsrc_fn, xsrc, slabs, seg):
    ar = g.ar
    SEG, NTS = g.SEG, g.SEG // 128
    tok0 = seg * SEG
    xo = Ring([ar.alloc("xo", [4, 512], F32) for _ in range(2)])
    xs = Ring([ar.alloc("xs", [4, 512], F32) for _ in range(2)])
    banks = Ring([0, 1, 2, 3])
    jobs = []
    for js in range(4):
        def load(slab, js=js):
            for k0 in range(0, nk, 8):
                k1 = min(nk, k0 + 8)
                kb.dma("pool", lambda e, k0=k0, k1=k1: e.dma_start(out=slab.ap[:, k0:k1, :], in_=wsrc_fn(js, k0, k1)), writes=[slab])

        def comp(slab, js=js):
            for i4 in range(NTS // 4):
                t0 = tok0 + i4 * 512
                xold, xnew = xo.next(), xs.next()
                sap = xsrc[t0:t0 + 512, js * 512:(js + 1) * 512].rearrange("(i p) c -> p i c", p=128)
                kb.dma("sp", lambda e, xold=xold, sap=sap: e.dma_start(out=xold.ap, in_=sap), reads=[g.v_xr], writes=[xold])
                for ii in range(4):
                    i = i4 * 4 + ii
                    b = banks.next()
                    out_ap = g.PS[:, b * 512:(b + 1) * 512]
                    pairs = [(XT.ap[:, kc, i * 128:(i + 1) * 128], slab.ap[:, kc, :]) for kc in range(nk)]
                    mm_group(kb, out_ap, pairs, [slab, XT], g.bank[b])
                    kb.op("dve", lambda e, out_ap=out_ap, xold=xold, xnew=xnew, ii=ii: e.tensor_tensor(
                        out=xnew.ap[:, ii, :], in0=out_ap, in1=xold.ap[:, ii, :], op=ALU.add), reads=[xold], writes=[g.bank[b], xnew])
                dap = g.xr.ap()[t0:t0 + 512, js * 512:(js + 1) * 512].rearrange("(i p) c -> p i c", p=128)
                kb.dma("sp", lambda e, xnew=xnew, dap=dap: e.dma_start(out=dap, in_=xnew.ap), reads=[xnew], writes=[g.v_xr])
        jobs.append((load, comp))
    run_jobs(jobs, slabs)


def phase_G3(kb, g, l, seg):
    ar = g.ar
    m0 = ar.mark()
    XT = ar.alloc("XT", [KC, g.SEG], BF16)
    load_xt(kb, g, XT, g.mT, 0, KC, seg)
    slabs = [ar.alloc("ws", [KC, 512], BF16) for _ in range(3)]
    xsrc = g.x_in.ap() if l == 0 else g.xr.ap()

    def wsrc(js, k0, k1):
        return g.w_out.ap()[l, k0 * 128:k1 * 128, js * 512:(js + 1) * 512].rearrange("(k p) n -> p k n", p=128)
    tm_update_jobs(kb, g, XT, KC, wsrc, xsrc, slabs, seg)
    kb.barrier()
    ar.reset(m0)


def phase_G4(kb, g, l, seg):
    ar = g.ar
    SEG, NTT = g.SEG, g.SEG // 512
    m0 = ar.mark()
    tok0 = seg * SEG
    XT = ar.alloc("XT", [KC, SEG], BF16)
    front_end(kb, g, g.xr.ap(), g.norm2, l * D, seg, XT)
    slabs = [ar.alloc("ws", [KC, 2, 256], BF16) for _ in range(3)]
    fst = Ring([ar.alloc("fst", [SEG], BF16) for _ in range(3)])
    tmp = Ring([ar.alloc("tmp", [512], F32) for _ in range(2)])
    bsets = Ring([(0, 1), (2, 3), (4, 5)])
    jobs = []
    for jf in range(DFF // 256):
        def load(slab, jf=jf):
            for part in range(2):
                src = g.w_ffn_in.ap()[l, :, part * DFF + jf * 256: part * DFF + (jf + 1) * 256].rearrange("(k p) n -> p k n", p=128)
                kb.dma("pool", lambda e, src=src, part=part: e.dma_start(out=slab.ap[:, :, part, :], in_=src), writes=[slab])

        def comp(slab, jf=jf):
            for c in range(2):
                stg = fst.next()
                for t in range(NTT):
                    bg, bu = bsets.next()
                    og = g.PS[:, bg * 512:(bg + 1) * 512]
                    ou = g.PS[:, bu * 512:(bu + 1) * 512]
                    mm_group(kb, og, [(slab.ap[:, kc, 0, c * 128:(c + 1) * 128], XT.ap[:, kc, t * 512:(t + 1) * 512]) for kc in range(KC)],
                             [slab, XT], g.bank[bg])
                    mm_group(kb, ou, [(slab.ap[:, kc, 1, c * 128:(c + 1) * 128], XT.ap[:, kc, t * 512:(t + 1) * 512]) for kc in range(KC)],
                             [slab, XT], g.bank[bu])
                    tp = tmp.next()
                    kb.op("act", lambda e, og=og, tp=tp: e.activation(out=tp.ap, in_=og, func=AF.Silu), writes=[g.bank[bg], tp])
                    kb.op("dve", lambda e, ou=ou, tp=tp, stg=stg, t=t: e.tensor_tensor(out=stg.ap[:, t * 512:(t + 1) * 512], in0=ou, in1=tp.ap,
                                                                                     op=ALU.mult), reads=[tp], writes=[g.bank[bu], stg])
                r0 = jf * 256 + c * 128
                kb.dma("sp", lambda e, stg=stg, r0=r0: e.dma_start(out=g.actT.ap()[r0:r0 + 128, tok0:tok0 + SEG], in_=stg.ap),
                       reads=[stg], writes=[g.v_actT])
        jobs.append((load, comp))
    run_jobs(jobs, slabs)
    kb.barrier()
    ar.reset(m0)


def phase_G5(kb, g, l, seg):
    ar = g.ar
    HK = DFF // 256
    for kh in range(2):
        m0 = ar.mark()
        XT = ar.alloc("XT", [HK, g.SEG], BF16)
        load_xt(kb, g, XT, g.actT, kh * HK * 128, HK, seg)
        slabs = [ar.alloc("ws", [HK, 512], BF16) for _ in range(2)]

        def wsrc(js, k0, k1, kh=kh):
            r0 = kh * HK * 128
            return g.w_ffn_out.ap()[l, r0 + k0 * 128:r0 + k1 * 128, js * 512:(js + 1) * 512].rearrange("(k p) n -> p k n", p=128)
        tm_update_jobs(kb, g, XT, HK, wsrc, g.xr.ap(), slabs, seg)
        kb.barrier()
        ar.reset(m0)


def phase_final(kb, g):
    ar = g.ar
    m0 = ar.mark()
    gain = ar.alloc("gain", [D], F32)
    kb.dma("sp", lambda e: e.dma_start(out=gain.ap, in_=dram_ap(g.norm_f, 0, [[0, 128], [1, D]])), writes=[gain])
    xt = [ar.alloc("xt", [D], F32) for _ in range(3)]
    yo = [ar.alloc("yo", [D], F32) for _ in range(3)]
    junk = ar.alloc("junk", [D], BF16)
    st = [ar.alloc("st", [4], F32) for _ in range(3)]
    for i in range(g.T // 128):
        x_, y_, s_ = xt[i % 3], yo[i % 3], st[i % 3]
        kb.dma("sp", lambda e, x_=x_, i=i: e.dma_start(out=x_.ap, in_=g.xr.ap()[i * 128:(i + 1) * 128, :]), reads=[g.v_xr], writes=[x_])
        kb.op("act", lambda e, x_=x_, s_=s_: e.activation(out=junk.ap, in_=x_.ap, func=AF.Square, accum_out=s_.ap[:, 0:1]),
              reads=[x_], writes=[junk, s_])
        kb.op("act", lambda e, s_=s_: e.activation(out=s_.ap[:, 1:2], in_=s_.ap[:, 0:1], func=AF.Sqrt, scale=1.0 / D, bias=g.eps.ap[:, 0:1]),
              reads=[s_, g.eps], writes=[s_])
        kb.op("dve", lambda e, s_=s_: e.reciprocal(out=s_.ap[:, 2:3], in_=s_.ap[:, 1:2]), reads=[s_], writes=[s_])
        kb.op("dve", lambda e, x_=x_, y_=y_, s_=s_: e.scalar_tensor_tensor(out=y_.ap, in0=x_.ap, scalar=s_.ap[:, 2:3], in1=gain.ap,
                                                                         op0=ALU.mult, op1=ALU.mult), reads=[x_, s_, gain], writes=[y_])
        kb.dma("sp", lambda e, y_=y_, i=i: e.dma_start(out=g.y.ap()[i * 128:(i + 1) * 128, :], in_=y_.ap), reads=[y_], writes=[g.v_y])
    kb.barrier()
    ar.reset(m0)


def run_pipeline(n_iter, make_stages, ns):
    st = {}
    for tau in range(n_iter + ns - 1):
        for k in reversed(range(ns)):
            i = tau - k
            if 0 <= i < n_iter:
                if i not in st:
                    st[i] = make_stages(i)
                st[i][k]()
                if k == ns - 1:
                    del st[i]


def attn_tail(kb, g, Sviews, nk, nkt, vbuf, vt0, small, Pf, Pn, pTp, pTpb, pT, oTp, oTpb, ost, ocol, sink_ap):
    S_ap, Sbanks = Sviews

    def s1a():
        if sink_ap is None:
            kb.op("dve", lambda e: e.tensor_reduce(out=small.ap[:, 0:1], in_=S_ap[:, 0:nk], axis=AX.X, op=ALU.max, negate=True),
                  writes=Sbanks + [small])
        else:
            kb.op("dve", lambda e: e.tensor_reduce(out=small.ap[:, 4:5], in_=S_ap[:, 0:nk], axis=AX.X, op=ALU.max), writes=Sbanks + [small])
            kb.op("dve", lambda e: e.tensor_scalar(out=small.ap[:, 0:1], in0=small.ap[:, 4:5], scalar1=sink_ap, scalar2=-1.0,
                                                   op0=ALU.max, op1=ALU.mult), reads=[g.sinkb], writes=[small])

    def s1b():
        kb.op("act", lambda e: e.activation(out=Pf.ap[:, 0:nk], in_=S_ap[:, 0:nk], func=AF.Exp, bias=small.ap[:, 0:1],
                                            accum_out=small.ap[:, 1:2]), writes=Sbanks + [small, Pf])
        if sink_ap is not None:
            kb.op("act", lambda e: e.activation(out=small.ap[:, 2:3], in_=sink_ap, func=AF.Exp, bias=small.ap[:, 0:1]),
                  reads=[g.sinkb], writes=[small])

    def s1c():
        if sink_ap is not None:
            kb.op("dve", lambda e: e.tensor_tensor(out=small.ap[:, 1:2], in0=small.ap[:, 1:2], in1=small.ap[:, 2:3], op=ALU.add),
                  writes=[small])
        kb.op("dve", lambda e: e.reciprocal(out=small.ap[:, 3:4], in_=small.ap[:, 1:2]), writes=[small])
        kb.op("act", lambda e: e.activation(out=Pn.ap[:, 0:nk], in_=Pf.ap[:, 0:nk], func=AF.Copy, scale=small.ap[:, 3:4]),
              reads=[Pf, small], writes=[Pn])

    def s2a():
        kb.group_begin("pe", reads=[Pn, g.ident], writes=[pTpb])
        for k in range(nkt):
            f = lambda e, k=k: e.transpose(out=pTp[:, k, :], in_=Pn.ap[:, k * 128:(k + 1) * 128], identity=g.ident.ap)
            if k == nkt - 1:
                kb.group_end("pe", f, reads=[Pn, g.ident], writes=[pTpb])
            else:
                kb.raw("pe", f)

    def s2b():
        kb.op("dve", lambda e: e.tensor_copy(out=pT.ap[:, 0:nkt, :], in_=pTp[:, 0:nkt, :]), writes=[pTpb, pT])

    def s3a():
        pairs = [(vbuf.ap[:, vt0 + k, :], pT.ap[:, k, :]) for k in range(nkt)]
        mm_group(kb, oTp, pairs, [vbuf, pT], oTpb)

    def s3b():
        kb.op("dve", lambda e: e.tensor_copy(out=ost.ap[:, ocol:ocol + 128], in_=oTp), writes=[oTpb, ost])
    return [s1a, s1b, s1c, s2a, s2b, s3a, s3b]


def mixer_A(kb, g, l):
    ar = g.ar
    T, NT = g.T, g.T // 128
    m0 = ar.mark()
    ncA = len(g.uniqA)
    biasA = ar.alloc("biasA", [ncA * 8, 384], BF16)
    kb.dma("pool", lambda e: e.dma_start(out=biasA.ap, in_=g.d_biasA.ap().rearrange("c p k -> p c k")), writes=[biasA])
    g.sinkb = ar.alloc("sink", [8], F32)
    kb.dma("sp", lambda e: e.dma_start(out=g.sinkb.ap, in_=dram_ap(g.sink_a, l * 8, [[0, 128], [1, 8]])), writes=[g.sinkb])
    kT = [ar.alloc("kT", [T], BF16) for _ in range(2)]
    vv = [ar.alloc("vv", [NT, 128], BF16) for _ in range(2)]
    qT = [ar.alloc("qT", [T], BF16) for _ in range(2)]
    Pf = [ar.alloc("Pf", [384], F32) for _ in range(2)]
    Pn = [ar.alloc("Pn", [384], BF16) for _ in range(2)]
    pT = [ar.alloc("pT", [3, 128], BF16) for _ in range(2)]
    ost = [ar.alloc("ost", [2048], BF16) for _ in range(2)]
    small = [ar.alloc("sm", [8], F32) for _ in range(4)]
    for gi in range(2):
        kb.dma("sp", lambda e, gi=gi: e.dma_start(out=kT[gi].ap, in_=g.pfm.ap()[R_AK + gi * 128:R_AK + (gi + 1) * 128, :]),
               reads=[g.v_pfm], writes=[kT[gi]])
        kb.dma("sp", lambda e, gi=gi: e.dma_start(out=vv[gi].ap, in_=g.ptm.ap()[:, T_AV + gi * 128:T_AV + (gi + 1) * 128].rearrange(
            "(n p) d -> p n d", p=128)), reads=[g.v_ptm], writes=[vv[gi]])

    def load_q(h):
        kb.dma("sp", lambda e: e.dma_start(out=qT[h % 2].ap, in_=g.pfm.ap()[R_AQ + h * 128:R_AQ + (h + 1) * 128, :]),
               reads=[g.v_pfm], writes=[qT[h % 2]])
    load_q(0)
    cnt = [0]
    for h in range(8):
        if h + 1 < 8:
            load_q(h + 1)
        gi = h // 4
        q_, k_, v_ = qT[h % 2], kT[gi], vv[gi]

        def make(n, h=h, q_=q_, k_=k_, v_=v_):
            it = cnt[0]
            cnt[0] += 1
            lo, hi = max(n - 1, 0), min(n + 1, NT - 1)
            nkt = hi - lo + 1
            nk = nkt * 128
            off = (lo - (n - 1)) * 128
            cls = g.clsA[n]
            sb = it % 3
            S_ap = g.PS[:, sb * 512:sb * 512 + 512]
            pb = 3 + it % 2
            pTp = g.PS[:, pb * 512:(pb + 1) * 512].bitcast(BF16)[:, 0:384].rearrange("p (k t) -> p k t", k=3)
            ob = 5 + it % 2
            oTp = g.PS[:, ob * 512:ob * 512 + 128]
            os_ = ost[(n // 16) % 2]

            def st0():
                kb.group_begin("pe", reads=[q_, k_, biasA, g.ident], writes=[g.bank[sb]])
                kb.raw("pe", lambda e: e.matmul(S_ap[:, 0:nk], lhsT=q_.ap[:, n * 128:(n + 1) * 128], rhs=k_.ap[:, lo * 128:(hi + 1) * 128],
                                                start=True, stop=False))
                kb.group_end("pe", lambda e: e.matmul(S_ap[:, 0:nk], lhsT=g.ident.ap, rhs=biasA.ap[:, cls * 8 + h, off:off + nk],
                                                      start=False, stop=True), reads=[q_, k_, biasA, g.ident], writes=[g.bank[sb]])
            tail = attn_tail(kb, g, (S_ap, [g.bank[sb]]), nk, nkt, v_, lo, small[it % 4], Pf[it % 2], Pn[it % 2], pTp, g.bank[pb],
                             pT[it % 2], oTp, g.bank[ob], os_, (n % 16) * 128, g.sinkb.ap[:, h:h + 1])

            def st3b():
                tail[6]()
                if n % 16 == 15 or n == NT - 1:
                    t0 = (n // 16) * 2048
                    nt = (n % 16 + 1) * 128
                    kb.dma("sp", lambda e: e.dma_start(out=g.oaT.ap()[h * 128:(h + 1) * 128, t0:t0 + nt], in_=os_.ap[:, 0:nt]),
                           reads=[os_], writes=[g.v_oaT])
            return [st0] + tail[0:6] + [st3b]
        run_pipeline(NT, make, 8)
    kb.barrier()
    ar.reset(m0)


def build_rpb_table(kb, g, l, rpbT):
    ar = g.ar
    zt = ar.alloc("zt", [960], F32)
    kb.op("pool", lambda e: e.memset(zt.ap, 0.0), writes=[zt])
    for h in range(8):
        kb.dma("sp", lambda e, h=h: e.dma_start(out=dram_ap(g.Gtab, h * 15 * 64 * 128, [[960, 128], [1, 960]]), in_=zt.ap),
               reads=[zt], writes=[g.v_G])
    for h in range(8):
        kb.dma("sp", lambda e, h=h: e.dma_start(out=dram_ap(g.Gtab, h * 15 * 64 * 128 + 48, [[64 * 128, 15], [128, 64], [1, 31]]),
                                               in_=dram_ap(g.rpb_c, (l * 8 + h) * 15 * 31, [[31, 15], [0, 64], [1, 31]])), writes=[g.v_G])
    for h in range(8):
        for rr in range(2):
            kb.dma("pool", lambda e, h=h, rr=rr: e.dma_start(
                out=rpbT.ap[rr * 64:(rr + 1) * 64, h, :].rearrange("p (i k) -> p i k", i=14),
                in_=dram_ap(g.Gtab, h * 15 * 64 * 128 + (1 - rr) * 64 * 128 + 63, [[127, 64], [64 * 128, 14], [1, 64]])),
                reads=[g.v_G], writes=[rpbT])


def mixer_C(kb, g, l):
    ar = g.ar
    T, NT = g.T, g.T // 128
    m0 = ar.mark()
    ncC = len(g.uniqC)
    rpbT = ar.alloc("rpbT", [8, 896], BF16)
    build_rpb_table(kb, g, l, rpbT)
    maskC = ar.alloc("maskC", [ncC, 896], BF16)
    kb.dma("pool", lambda e: e.dma_start(out=maskC.ap, in_=g.d_maskC.ap().rearrange("c p k -> p c k")), writes=[maskC])
    qT = [ar.alloc("qT", [T], BF16) for _ in range(2)]
    kT = [ar.alloc("kT", [T], BF16) for _ in range(2)]
    vv = [ar.alloc("vv", [NT, 128], BF16) for _ in range(2)]
    Pf = [ar.alloc("Pf", [896], F32) for _ in range(2)]
    Pn = [ar.alloc("Pn", [896], BF16) for _ in range(2)]
    pT = [ar.alloc("pT", [7, 128], BF16) for _ in range(2)]
    ost = [ar.alloc("ost", [2048], BF16) for _ in range(2)]
    small = [ar.alloc("sm", [8], F32) for _ in range(4)]

    def load_h(h):
        b = h % 2
        kb.dma("sp", lambda e: e.dma_start(out=qT[b].ap, in_=g.pfm.ap()[R_CQ + h * 128:R_CQ + (h + 1) * 128, :]), reads=[g.v_pfm], writes=[qT[b]])
        kb.dma("sp", lambda e: e.dma_start(out=kT[b].ap, in_=g.pfm.ap()[R_CK + h * 128:R_CK + (h + 1) * 128, :]), reads=[g.v_pfm], writes=[kT[b]])
        kb.dma("sp", lambda e: e.dma_start(out=vv[b].ap, in_=g.ptm.ap()[:, T_CV + h * 128:T_CV + (h + 1) * 128].rearrange(
            "(n p) d -> p n d", p=128)), reads=[g.v_ptm], writes=[vv[b]])
    load_h(0)
    cnt = [0]
    for h in range(8):
        if h + 1 < 8:
            load_h(h + 1)
        q_, k_, v_ = qT[h % 2], kT[h % 2], vv[h % 2]

        def make(j, h=h, q_=q_, k_=k_, v_=v_):
            it = cnt[0]
            cnt[0] += 1
            lo, hi = g.bandC[j]
            nk = (hi - lo) * 64
            nkt = (nk + 127) // 128
            rel = (lo - (2 * j - 6)) * 64
            cls = g.clsC[j]
            sb = 2 * (it % 2)
            S_ap = g.PS[:, sb * 512:sb * 512 + 1024]
            Sb = [g.bank[sb], g.bank[sb + 1]]
            pb = 4 + it % 2
            pTp = g.PS[:, pb * 512:(pb + 1) * 512].bitcast(BF16)[:, 0:896].rearrange("p (k t) -> p k t", k=7)
            ob = 6 + it % 2
            oTp = g.PS[:, ob * 512:ob * 512 + 128]
            os_ = ost[(j // 16) % 2]

            def st0():
                rd = [q_, k_, rpbT, maskC, g.ident]
                for ci, (c0, c1) in enumerate([(0, min(512, nk)), (512, nk)]):
                    if c1 <= c0:
                        continue
                    kb.group_begin("pe", reads=rd, writes=[Sb[ci]])
                    kb.raw("pe", lambda e, c0=c0, c1=c1: e.matmul(S_ap[:, c0:c1], lhsT=q_.ap[:, j * 128:(j + 1) * 128],
                                                                  rhs=k_.ap[:, lo * 64 + c0:lo * 64 + c1], start=True, stop=False))
                    kb.raw("pe", lambda e, c0=c0, c1=c1: e.matmul(S_ap[:, c0:c1], lhsT=g.ident.ap, rhs=rpbT.ap[:, h, rel + c0:rel + c1],
                                                                  start=False, stop=False))
                    kb.group_end("pe", lambda e, c0=c0, c1=c1: e.matmul(S_ap[:, c0:c1], lhsT=g.ident.ap, rhs=maskC.ap[:, cls, rel + c0:rel + c1],
                                                                        start=False, stop=True), reads=rd, writes=[Sb[ci]])
            tail = attn_tail(kb, g, (S_ap, Sb), nk, nkt, v_, lo // 2, small[it % 4], Pf[it % 2], Pn[it % 2], pTp, g.bank[pb],
                             pT[it % 2], oTp, g.bank[ob], os_, (j % 16) * 128, None)

            def st3b():
                tail[6]()
                if j % 16 == 15 or j == NT - 1:
                    t0 = (j // 16) * 2048
                    nt = (j % 16 + 1) * 128
                    kb.dma("sp", lambda e: e.dma_start(out=g.ocT.ap()[h * 128:(h + 1) * 128, t0:t0 + nt], in_=os_.ap[:, 0:nt]),
                           reads=[os_], writes=[g.v_ocT])
            return [st0] + tail[0:6] + [st3b]
        run_pipeline(NT, make, 8)
    kb.barrier()
    ar.reset(m0)


def mixer_B(kb, g, l):
    ar = g.ar
    T, NT, NTS = g.T, g.T // 128, g.SEG // 128
    NG = NT // 4
    m0 = ar.mark()
    gain2 = ar.alloc("gain2", [2], F32)
    kb.dma("sp", lambda e: e.dma_start(out=gain2.ap, in_=dram_ap(g.gla_norm, l * 256, [[1, 128], [128, 2]]), allow_slow_non_contiguous=True), writes=[gain2])
    Z = ar.alloc("Z", [4, 256], F32)
    Sbf = ar.alloc("Sbf", [4, 256], BF16)
    dec = ar.alloc("dec", [3, 4], F32)
    w2a = ar.alloc("w2a", [512], BF16, parts=17)
    grA = [ar.alloc("grA", [512], BF16, parts=17) for _ in range(2)]
    qB = [ar.alloc("qB", [4, 512], BF16) for _ in range(2)]
    kB = [ar.alloc("kB", [4, 512], BF16) for _ in range(2)]
    vB = [ar.alloc("vB", [4, 1024], BF16) for _ in range(2)]
    E1 = [ar.alloc("E1", [512], F32) for _ in range(2)]
    SP = [ar.alloc("SP", [512], F32) for _ in range(2)]
    EB = [ar.alloc("EB", [4, 128], F32) for _ in range(2)]
    ENB = [ar.alloc("ENB", [4, 128], F32) for _ in range(2)]
    QT = [ar.alloc("QT", [4, 128], BF16) for _ in range(2)]
    KT = [ar.alloc("KT", [4, 128], BF16) for _ in range(2)]
    KTT = [ar.alloc("KTT", [512], BF16) for _ in range(2)]
    ATM = [ar.alloc("ATM", [4, 128], BF16) for _ in range(2)]
    for b in grA:
        kb.op("pool", lambda e, b=b: e.memset(b.ap, 1.0), writes=[b])
    m1 = ar.mark()
    BL, BB, BT, BA, BO0, BO1, BU0, BU1 = range(8)
    PSL = g.PS[:, BL * 512:(BL + 1) * 512]
    PSB = g.PS[:, BB * 512:(BB + 1) * 512]
    PST = g.PS[:, BT * 512:(BT + 1) * 512].bitcast(BF16)[:, 0:512].rearrange("p (h t) -> p h t", h=4)
    PSA = g.PS[:, BA * 512:(BA + 1) * 512]
    PSO = g.PS[:, BO0 * 512:(BO1 + 1) * 512]
    PSU = g.PS[:, BU0 * 512:(BU1 + 1) * 512]
    def one_pass(d):
        ar.reset(m1)
        tri = g.cst.ap[:, 1 + d, :]
        msk = g.cst.ap[:, 3 + d, :]
        lastcol = 127 if d == 0 else 0
        if d == 0:
            OST = [ar.alloc("OST", [8, 512], F32) for _ in range(2)]
        else:
            OST = [ar.alloc("OBs", [8, 512], BF16) for _ in range(2)]
            ogB = [ar.alloc("ogB", [8, 512], BF16) for _ in range(2)]
            ofB = [ar.alloc("ofB", [8, 512], F32) for _ in range(2)]
            O32 = ar.alloc("O32", [8, 128], F32)
            SQ = ar.alloc("SQ", [8, 128], BF16)
            SD = ar.alloc("SD", [4, 128], F32)
            RS = ar.alloc("RS", [4, 128], F32)
            T1 = ar.alloc("T1", [8, 128], F32)
            RSg = ar.alloc("RSg", [2, 4, 128], F32)
        kb.dma("pool", lambda e, d=d: e.dma_start(out=w2a.ap[0:16, :], in_=g.gla_w2.ap()[l, d, :, :]), writes=[w2a])
        kb.dma("pool", lambda e, d=d: e.dma_start(out=w2a.ap[16:17, :], in_=g.gla_b.ap()[l, d:d + 1, :]), writes=[w2a])
        kb.op("dve", lambda e: e.memset(Z.ap, 0.0), writes=[Z])
        kb.op("dve", lambda e: e.memset(Sbf.ap, 0.0), writes=[Sbf])
        kb.op("dve", lambda e: e.memset(dec.ap, 1.0), writes=[dec])
        gorder = list(range(NG)) if d == 0 else list(range(NG - 1, -1, -1))

        def load_group(gi, slot):
            t0 = gi * 512
            kb.dma("sp", lambda e: e.dma_start(out=grA[slot].ap[0:16, :], in_=g.pfm.ap()[R_GR + d * 16:R_GR + d * 16 + 16, t0:t0 + 512]),
                   reads=[g.v_pfm], writes=[grA[slot]])
            kb.dma("sp", lambda e: e.dma_start(out=qB[slot].ap, in_=g.pfm.ap()[R_BQ:R_BQ + 512, t0:t0 + 512].rearrange("(h p) t -> p h t", p=128)),
                   reads=[g.v_pfm], writes=[qB[slot]])
            kb.dma("sp", lambda e: e.dma_start(out=kB[slot].ap, in_=g.pfm.ap()[R_BK:R_BK + 512, t0:t0 + 512].rearrange("(h p) t -> p h t", p=128)),
                   reads=[g.v_pfm], writes=[kB[slot]])
            kb.dma("sp", lambda e: e.dma_start(out=vB[slot].ap, in_=g.ptm.ap()[t0:t0 + 512, T_BV:T_BV + 1024].rearrange("(i p) c -> p i c", p=128)),
                   reads=[g.v_ptm], writes=[vB[slot]])
            if d == 1:
                kb.dma("sp", lambda e: e.dma_start(out=ogB[slot].ap, in_=g.pfm.ap()[R_OG:R_OG + 1024, t0:t0 + 512].rearrange(
                    "(c p) t -> p c t", p=128)), reads=[g.v_pfm], writes=[ogB[slot]])
                kb.dma("sp", lambda e: e.dma_start(out=ofB[slot].ap, in_=g.ofw.ap()[:, t0:t0 + 512].rearrange("(c p) t -> p c t", p=128)),
                       reads=[g.v_ofw], writes=[ofB[slot]])
        load_group(gorder[0], 0)
        tiles = []
        for gpos, gi in enumerate(gorder):
            for k, ii in enumerate(list(range(4)) if d == 0 else [3, 2, 1, 0]):
                tiles.append((gpos, gi, ii, k))

        def make(itx):
            gpos, gi, ii, kpos = tiles[itx]
            slot = gpos % 2
            n = gi * 4 + ii
            it = gpos * 4 + (ii if d == 0 else 3 - ii)
            w = it % 2
            par, pprev = it % 3, (it - 1) % 3
            e1, sp_, eb, enb, qt, kt, ktt, atm = E1[w], SP[w], EB[w], ENB[w], QT[w], KT[w], KTT[w], ATM[w]
            gr_, q_, k_, v_, os_ = grA[slot], qB[slot], kB[slot], vB[slot], OST[slot]
            c0 = ii * 128

            def stage0():
                if kpos == 0 and gpos + 1 < NG:
                    load_group(gorder[gpos + 1], (gpos + 1) % 2)
                mm_group(kb, PSL, [(gr_.ap[0:17, c0:c0 + 128], w2a.ap[0:17, :])], [gr_, w2a], g.bank[BL])
                kb.op("act", lambda e, e1=e1: e.activation(out=e1.ap, in_=PSL, func=AF.Exp, scale=-1.0), writes=[g.bank[BL], e1])
                kb.op("act", lambda e, e1=e1, sp_=sp_: e.activation(out=sp_.ap, in_=e1.ap, func=AF.Ln, bias=1.0), reads=[e1], writes=[sp_])
                kb.group_begin("pe", reads=[sp_, g.cst], writes=[g.bank[BB]])
                for h in range(4):
                    f = lambda e, h=h, sp_=sp_: e.matmul(PSB[:, h * 128:(h + 1) * 128], lhsT=sp_.ap[:, h * 128:(h + 1) * 128], rhs=tri,
                                                         start=True, stop=True)
                    if h == 3:
                        kb.group_end("pe", f, reads=[sp_, g.cst], writes=[g.bank[BB]])
                    else:
                        kb.raw("pe", f)
                kb.op("act", lambda e, eb=eb: e.activation(out=eb.ap.rearrange("p h t -> p (h t)"), in_=PSB, func=AF.Exp),
                      writes=[g.bank[BB], eb])
                kb.op("act", lambda e, enb=enb: e.activation(out=enb.ap.rearrange("p h t -> p (h t)"), in_=PSB, func=AF.Exp, scale=-1.0),
                      writes=[g.bank[BB], enb])
                kb.op("dve", lambda e, eb=eb, par=par: e.tensor_copy(out=dec.ap[:, par, :], in_=eb.ap[:, :, lastcol]), reads=[eb], writes=[dec])
                seg_end = (n % NTS == NTS - 1) if d == 0 else (n % NTS == 0)
                if seg_end:
                    kb.op("dve", lambda e, par=par: e.tensor_scalar(out=dec.ap[:, par, :], in0=dec.ap[:, par, :], scalar1=g.cflag.ap[:, 0:1],
                                                                    scalar2=None, op0=ALU.mult), reads=[g.cflag], writes=[dec])
                kb.op("dve", lambda e, qt=qt, q_=q_, eb=eb, c0=c0: e.tensor_tensor(out=qt.ap, in0=q_.ap[:, :, c0:c0 + 128], in1=eb.ap, op=ALU.mult),
                      reads=[q_, eb], writes=[qt])
                kb.op("pool", lambda e, kt=kt, k_=k_, enb=enb, c0=c0: e.tensor_tensor(out=kt.ap, in0=k_.ap[:, :, c0:c0 + 128], in1=enb.ap,
                                                                                      op=ALU.mult), reads=[k_, enb], writes=[kt])
                kb.group_begin("pe", reads=[kt, g.ident], writes=[g.bank[BT]])
                for h in range(4):
                    f = lambda e, h=h, kt=kt: e.transpose(out=PST[:, h, :], in_=kt.ap[:, h, :], identity=g.ident.ap)
                    if h == 3:
                        kb.group_end("pe", f, reads=[kt, g.ident], writes=[g.bank[BT]])
                    else:
                        kb.raw("pe", f)
                kb.op("act", lambda e, ktt=ktt: e.copy(out=ktt.ap.rearrange("p (h t) -> p h t", h=4), in_=PST), writes=[g.bank[BT], ktt])
                kb.group_begin("pe", reads=[kt, qt], writes=[g.bank[BA]])
                for h in range(4):
                    f = lambda e, h=h, kt=kt, qt=qt: e.matmul(PSA[:, h * 128:(h + 1) * 128], lhsT=kt.ap[:, h, :], rhs=qt.ap[:, h, :],
                                                              start=True, stop=True)
                    if h == 3:
                        kb.group_end("pe", f, reads=[kt, qt], writes=[g.bank[BA]])
                    else:
                        kb.raw("pe", f)
                for h in range(4):
                    kb.op("dve", lambda e, h=h, atm=atm: e.tensor_tensor(out=atm.ap[:, h, :], in0=PSA[:, h * 128:(h + 1) * 128], in1=msk,
                                                                         op=ALU.mult), reads=[g.cst], writes=[g.bank[BA], atm])

            def stage1():
                rdo = [v_, atm, Sbf, qt]
                kb.group_begin("pe", reads=rdo, writes=[g.bank[BO0], g.bank[BO1]])
                for hc in range(8):
                    h, c = hc // 2, hc % 2
                    oap = PSO[:, hc * 128:(hc + 1) * 128]
                    kb.raw("pe", lambda e, oap=oap, h=h, c=c, v_=v_, atm=atm, ii=ii: e.matmul(
                        oap, lhsT=v_.ap[:, ii, h * 256 + c * 128:h * 256 + (c + 1) * 128], rhs=atm.ap[:, h, :], start=True, stop=False))
                    f = lambda e, oap=oap, h=h, c=c, qt=qt: e.matmul(oap, lhsT=Sbf.ap[:, h, c * 128:(c + 1) * 128], rhs=qt.ap[:, h, :],
                                                                     start=False, stop=True)
                    if hc == 7:
                        kb.group_end("pe", f, reads=rdo, writes=[g.bank[BO0], g.bank[BO1]])
                    else:
                        kb.raw("pe", f)
                kb.group_begin("pe", reads=[ktt, v_], writes=[g.bank[BU0], g.bank[BU1]])
                for h in range(4):
                    f = lambda e, h=h, ktt=ktt, v_=v_, ii=ii: e.matmul(PSU[:, h * 256:(h + 1) * 256], lhsT=ktt.ap[:, h * 128:(h + 1) * 128],
                                                                       rhs=v_.ap[:, ii, h * 256:(h + 1) * 256], start=True, stop=True)
                    if h == 3:
                        kb.group_end("pe", f, reads=[ktt, v_], writes=[g.bank[BU0], g.bank[BU1]])
                    else:
                        kb.raw("pe", f)
                for h in range(4):
                    kb.op("dve", lambda e, h=h, pprev=pprev: e.scalar_tensor_tensor(
                        out=Z.ap[:, h, :], in0=Z.ap[:, h, :], scalar=dec.ap[:, pprev, h:h + 1], in1=PSU[:, h * 256:(h + 1) * 256],
                        op0=ALU.mult, op1=ALU.add), reads=[dec], writes=[Z, g.bank[BU0 + h // 2]])
                for h in range(4):
                    kb.op("act", lambda e, h=h, par=par: e.activation(out=Sbf.ap[:, h, :], in_=Z.ap[:, h, :], func=AF.Copy,
                                                                      scale=dec.ap[:, par, h:h + 1]), reads=[Z, dec], writes=[Sbf])
                if d == 0:
                    kb.op("dve", lambda e, os_=os_, c0=c0: e.tensor_copy(out=os_.ap[:, :, c0:c0 + 128], in_=PSO.rearrange("p (c t) -> p c t", c=8)),
                          writes=[g.bank[BO0], g.bank[BO1], os_])
                else:
                    og_, of_ = ogB[slot], ofB[slot]
                    kb.op("dve", lambda e, of_=of_, c0=c0: e.tensor_tensor(out=O32.ap, in0=PSO.rearrange("p (c t) -> p c t", c=8),
                                                                          in1=of_.ap[:, :, c0:c0 + 128], op=ALU.add),
                          reads=[of_], writes=[g.bank[BO0], g.bank[BO1], O32])
                    kb.op("act", lambda e: e.activation(out=SQ.ap, in_=O32.ap, func=AF.Square), reads=[O32], writes=[SQ])
                    kb.group_begin("pe", reads=[SQ, g.onesb], writes=[g.bank[BL]])
                    for hc in range(8):
                        h, c = hc // 2, hc % 2
                        f = lambda e, hc=hc, h=h, c=c: e.matmul(PSL[:, h * 128:(h + 1) * 128], lhsT=g.onesb.ap, rhs=SQ.ap[:, hc, :],
                                                                start=(c == 0), stop=(c == 1))
                        if hc == 7:
                            kb.group_end("pe", f, reads=[SQ, g.onesb], writes=[g.bank[BL]])
                        else:
                            kb.raw("pe", f)
                    kb.op("act", lambda e: e.activation(out=SD.ap.rearrange("p h t -> p (h t)"), in_=PSL, func=AF.Sqrt, scale=1.0 / 256, bias=EPS),
                          writes=[g.bank[BL], SD])
                    kb.op("dve", lambda e: e.reciprocal(out=RS.ap, in_=SD.ap), reads=[SD], writes=[RS])
                    O32v = O32.ap.rearrange("p (h c) t -> p h c t", c=2)
                    T1v = T1.ap.rearrange("p (h c) t -> p h c t", c=2)
                    for c in range(2):
                        kb.op("dve", lambda e, c=c: e.tensor_scalar(out=RSg.ap[:, c, :, :], in0=RS.ap, scalar1=gain2.ap[:, c:c + 1], scalar2=None,
                                                                     op0=ALU.mult), reads=[RS, gain2], writes=[RSg])
                    for c in range(2):
                        kb.op("dve", lambda e, c=c: e.tensor_tensor(out=T1v[:, :, c, :], in0=O32v[:, :, c, :], in1=RSg.ap[:, c, :, :], op=ALU.mult),
                              reads=[O32, RSg], writes=[T1])
                    kb.op("dve", lambda e, og_=og_, os_=os_, c0=c0: e.tensor_tensor(out=os_.ap[:, :, c0:c0 + 128], in0=T1.ap,
                                                                                     in1=og_.ap[:, :, c0:c0 + 128], op=ALU.mult),
                          reads=[T1, og_], writes=[os_])
                if kpos == 3:
                    t0 = gi * 512
                    if d == 0:
                        kb.dma("sp", lambda e, os_=os_, t0=t0: e.dma_start(out=g.ofw.ap()[:, t0:t0 + 512].rearrange("(c p) t -> p c t", p=128), in_=os_.ap),
                               reads=[os_], writes=[g.v_ofw])
                    else:
                        kb.dma("sp", lambda e, os_=os_, t0=t0: e.dma_start(out=g.obT.ap()[:, t0:t0 + 512].rearrange("(c p) t -> p c t", p=128), in_=os_.ap),
                               reads=[os_], writes=[g.v_obT])

            return [stage0, stage1]
        run_pipeline(len(tiles), make, 2)
        kb.barrier()
    one_pass(0)
    one_pass(1)
    ar.reset(m0)


WNAMES = ("norm1", "w_in", "sink_a", "gla_w2", "gla_b", "gla_norm", "rpb_c", "w_br_a", "w_br_b", "w_br_c", "w_out", "norm2",
          "w_ffn_in", "w_ffn_out", "norm_f")
WSHAPES = {"norm1": [2, D], "w_in": [2, D, IN_COLS], "sink_a": [2, 8], "gla_w2": [2, 2, 16, 512], "gla_b": [2, 2, 512],
           "gla_norm": [2, 256], "rpb_c": [2, 8, 15, 31], "w_br_a": [2, 1024, D], "w_br_b": [2, 1024, D], "w_br_c": [2, 1024, D],
           "w_out": [2, D, D], "norm2": [2, D], "w_ffn_in": [2, D, 2 * DFF], "w_ffn_out": [2, DFF, D], "norm_f": [D]}
ARENA_BYTES = 206 * 1024


def build(SEG, NSEG, depth=2, debug=False, stop_after=None):
    nc = bass.Bass("TRN2", target_bir_lowering=False)
    T = SEG * NSEG
    g = G()
    g.nc, g.SEG, g.NSEG, g.T = nc, SEG, NSEG, T
    g.x_in = nc.dram_tensor("x", [T, D], F32, kind="ExternalInput")
    for nm in WNAMES:
        setattr(g, nm, nc.dram_tensor(nm, WSHAPES[nm], F32, kind="ExternalInput"))
    g.bandC, g.clsC, g.uniqC = plan_C(T, SEG)
    g.clsA, g.uniqA = plan_A(T, SEG)
    g.d_cst = nc.dram_tensor("cst", [128, 768], F32, kind="ExternalInput")
    g.d_biasA = nc.dram_tensor("biasA", [len(g.uniqA) * 8, 128, 384], F32, kind="ExternalInput")
    g.d_maskC = nc.dram_tensor("maskC", [len(g.uniqC), 128, 896], F32, kind="ExternalInput")
    g.d_cflag = nc.dram_tensor("cflag", [128, 1], F32, kind="ExternalInput")
    sk = "ExternalOutput" if debug else "Internal"
    g.xr = nc.dram_tensor("xr", [T, D], F32, kind=sk)
    g.pfm = nc.dram_tensor("pfm", [NFM, T], BF16, kind=sk)
    g.ptm = nc.dram_tensor("ptm", [T, NTM], BF16, kind=sk)
    g.oaT = nc.dram_tensor("oaT", [1024, T], BF16, kind=sk)
    g.obT = nc.dram_tensor("obT", [1024, T], BF16, kind=sk)
    g.ocT = nc.dram_tensor("ocT", [1024, T], BF16, kind=sk)
    g.ofw = nc.dram_tensor("ofw", [1024, T], F32, kind=sk)
    g.mT = nc.dram_tensor("mT", [D, T], BF16, kind=sk)
    g.actT = nc.dram_tensor("actT", [DFF, T], BF16, kind=sk)
    g.Gtab = nc.dram_tensor("Gtab", [8 * 15 * 64 * 128], F32)
    g.y = nc.dram_tensor("y", [T, D], F32, kind="ExternalOutput")
    es = ExitStack()
    with es:
        kb = KB(nc, es)
        for nm in ("pfm", "ptm", "oaT", "obT", "ocT", "ofw", "mT", "actT", "xr", "y", "G"):
            setattr(g, "v_" + nm, kb.vbuf("v_" + nm))
        g.ar = Arena(nc, kb, ARENA_BYTES)
        ar = g.ar
        PSt = nc.alloc_psum_tensor("PS", [128, 4096], F32)
        g.PS = PSt[:, :]
        g.bank = [kb.vbuf("bank%d" % i) for i in range(8)]
        g.cst = ar.alloc("cst", [6, 128], F32)
        g.ident = ar.alloc("ident", [128], BF16)
        g.onesb = ar.alloc("onesb", [128], BF16)
        g.cflag = ar.alloc("cflag", [1], F32)
        g.eps = ar.alloc("eps", [1], F32)
        kb.dma("sp", lambda e: e.dma_start(out=g.cst.ap, in_=g.d_cst.ap().rearrange("p (a b) -> p a b", a=6)), writes=[g.cst])
        kb.dma("sp", lambda e: e.dma_start(out=g.cflag.ap, in_=g.d_cflag.ap()), writes=[g.cflag])
        kb.op("dve", lambda e: e.tensor_copy(out=g.ident.ap, in_=g.cst.ap[:, 0, :]), reads=[g.cst], writes=[g.ident])
        kb.op("dve", lambda e: e.tensor_copy(out=g.onesb.ap, in_=g.cst.ap[:, 5, :]), reads=[g.cst], writes=[g.onesb])
        kb.op("dve", lambda e: e.memset(g.eps.ap, EPS), writes=[g.eps])
        ar.base = ar.mark()
        kb.barrier()

        def body():
            for l in range(depth):
                for seg in range(NSEG):
                    phase_G1(kb, g, l, seg)
                if stop_after == "G1":
                    return
                mixer_A(kb, g, l)
                if stop_after == "A":
                    return
                mixer_B(kb, g, l)
                if stop_after == "B":
                    return
                mixer_C(kb, g, l)
                if stop_after == "C":
                    return
                for seg in range(NSEG):
                    phase_G2(kb, g, l, seg)
                    phase_G3(kb, g, l, seg)
                    if stop_after == "G3":
                        continue
                    phase_G4(kb, g, l, seg)
                    phase_G5(kb, g, l, seg)
                if stop_after in ("G3", "L0"):
                    return
            phase_final(kb, g)
        body()
        kb.barrier()
        kb.emit()
        g.n_ins = kb.n_ins
    return nc, g


def core_inputs(typ, SEG, NSEG):
    T = SEG * NSEG
    bands, clsC, uniqC = plan_C(T, SEG)
    clsA, uniqA = plan_A(T, SEG)
    return {"cst": consts_np(), "biasA": bias_A(uniqA, typ), "maskC": masks_C(uniqC, typ),
            "cflag": np.full((128, 1), 1.0 if typ == "P" else 0.0, np.float32)}


_CACHE = {}


def kernel(**inputs):
    SEG, NSEG, NCORE = 2048, 4, 8
    T = SEG * NSEG
    xp = np.asarray(inputs["x_prompt"], np.float32)
    xs = np.asarray(inputs["x_sample"], np.float32)
    assert xp.shape == (1, T, D) and xs.shape[1:] == (SEG, D)
    nsamp = xs.shape[0]
    slots = {0: None}
    counts = [0] * NCORE
    for s in range(nsamp):
        c = 1 + (s % (NCORE - 1))
        counts[c] += 1
    assign, s = {}, 0
    for c in range(1, NCORE):
        assign[c] = list(range(s, s + counts[c]))
        s += counts[c]
        assert counts[c] <= NSEG
    if "nc" not in _CACHE:
        _CACHE["nc"] = build(SEG, NSEG)[0]
    nc = _CACHE["nc"]
    w = {nm: np.ascontiguousarray(np.asarray(inputs[nm], np.float32)) for nm in WNAMES}
    cP, cS = core_inputs("P", SEG, NSEG), core_inputs("S", SEG, NSEG)
    in_maps = []
    for c in range(NCORE):
        if c == 0:
            x = np.ascontiguousarray(xp[0])
            m = dict(cP)
        else:
            x = np.zeros((T, D), np.float32)
            for k, si in enumerate(assign[c]):
                x[k * SEG:(k + 1) * SEG] = xs[si]
            m = dict(cS)
        m["x"] = x
        m.update(w)
        in_maps.append(m)
    res = run_bass_kernel_spmd(nc, in_maps, core_ids=list(range(NCORE)))
    outs = [np.asarray(r["y"], np.float32) for r in res.results]
    y_prompt = outs[0].reshape(1, T, D)
    y_sample = np.zeros((nsamp, SEG, D), np.float32)
    for c in range(1, NCORE):
        for k, si in enumerate(assign[c]):
            y_sample[si] = outs[c][k * SEG:(k + 1) * SEG]
    return (y_prompt, y_sample)
```

```python
import numpy as np
from contextlib import ExitStack
import ml_dtypes

import concourse.bass as bass
import concourse.mybir as mybir
from concourse.bass_utils import run_bass_kernel_spmd

F32 = mybir.dt.float32
BF16 = mybir.dt.bfloat16
AF = mybir.ActivationFunctionType
ALU = mybir.AluOpType
AX = mybir.AxisListType

ENGS = ("pe", "act", "dve", "pool", "sp")


class Buf:
    __slots__ = ("name", "w", "r", "ap", "sem", "nd", "q")

    def __init__(self, name, ap=None):
        self.name = name
        self.w = None
        self.r = []
        self.ap = ap
        self.sem = None
        self.nd = 0


class KB:
    def __init__(self, nc, es):
        self.nc = nc
        self.es = es
        self.prog = {e: [] for e in ENGS}
        self.cnt = {e: 0 for e in ENGS}
        self.sem = {}
        self.waited = {}
        self.nbuf = 0
        for e in ("pe", "act", "dve", "pool"):
            self._mksem("E_" + e)
        self.n_ins = 0
        self.free_sems = []
        self.dsem_cnt = {}

    def release(self, buf):
        if buf.sem is not None:
            self.free_sems.append((buf.sem, buf.nd))
            buf.sem = None

    def barrier(self):
        cur = [("E_" + e, self.cnt[e]) for e in ("pe", "act", "dve", "pool") if self.cnt[e] > 0]
        cur += [(s, 16 * n) for s, n in self.dsem_cnt.items() if n > 0]
        for eng in ENGS:
            waits = []
            for (s, v) in cur:
                if self.waited.get((eng, s), 0) < v:
                    waits.append((s, v))
                    self.waited[(eng, s)] = v
            if waits:
                self.prog[eng].append((waits, None, None, 0))

    def _mksem(self, name):
        self.sem[name] = self.es.enter_context(self.nc.semaphore(name))
        return name

    def sb(self, name, shape, dtype):
        t = self.nc.alloc_sbuf_tensor(name, list(shape), dtype)
        return Buf(name, t)

    def ps(self, name, shape, dtype=F32):
        t = self.nc.alloc_psum_tensor(name, list(shape), dtype)
        return Buf(name, t)

    def vbuf(self, name):
        return Buf(name)

    def _waits_for(self, eng, reads, writes):
        evs = []
        for b in reads:
            if b.w is not None:
                evs.append(b.w)
        for b in writes:
            if b.w is not None:
                evs.append(b.w)
            evs.extend(b.r)
        out = {}
        for (s, v) in evs:
            if self.waited.get((eng, s), 0) >= v:
                continue
            if out.get(s, 0) < v:
                out[s] = v
        for s, v in out.items():
            self.waited[(eng, s)] = v
        return list(out.items())

    def op(self, eng, fn, reads=(), writes=(), signal=True):
        waits = self._waits_for(eng, reads, writes)
        ev = None
        if signal:
            self.cnt[eng] += 1
            ev = ("E_" + eng, self.cnt[eng])
        self.prog[eng].append((waits, fn, ev[0] if ev else None, 1))
        self.n_ins += 1
        if ev is not None:
            for b in reads:
                b.r.append(ev)
            for b in writes:
                b.w = ev
                b.r = []
        return ev

    def group_begin(self, eng, reads=(), writes=()):
        waits = self._waits_for(eng, reads, writes)
        if waits:
            self.prog[eng].append((waits, None, None, 0))

    def raw(self, eng, fn):
        self.prog[eng].append(([], fn, None, 0))
        self.n_ins += 1

    def group_end(self, eng, fn, reads=(), writes=()):
        self.cnt[eng] += 1
        ev = ("E_" + eng, self.cnt[eng])
        self.prog[eng].append(([], fn, ev[0], 1))
        self.n_ins += 1
        for b in reads:
            b.r.append(ev)
        for b in writes:
            b.w = ev
            b.r = []
        return ev

    def dma(self, eng, fn, reads=(), writes=(), owner=None):
        if owner is None:
            owner = writes[0] if writes else reads[0]
        if owner.sem is None:
            if self.free_sems:
                owner.sem, owner.nd = self.free_sems.pop()
            else:
                self.nbuf += 1
                owner.sem = self._mksem("D%d" % self.nbuf)
                self.dsem_cnt[owner.sem] = 0
        waits = self._waits_for(eng, reads, writes)
        owner.nd += 1
        self.dsem_cnt[owner.sem] = owner.nd
        ev = (owner.sem, 16 * owner.nd)
        self.prog[eng].append((waits, fn, owner.sem, 16))
        self.n_ins += 1
        for b in reads:
            b.r.append(ev)
        for b in writes:
            b.w = ev
            b.r = []
        return ev

    def wait_all(self, eng, bufs):
        waits = self._waits_for(eng, (), bufs)
        if waits:
            self.prog[eng].append((waits, None, None, 0))

    def emit(self):
        nc = self.nc
        sem = self.sem
        prog = self.prog

        def run(e_obj, lst):
            for (waits, fn, incs, incv) in lst:
                for (s, v) in waits:
                    e_obj.wait_ge(sem[s], v)
                if fn is not None:
                    ins = fn(e_obj)
                    if incs is not None:
                        ins.then_inc(sem[incs], incv)

        with nc.Block() as block:
            @block.tensor
            def _(e):
                run(e, prog["pe"])

            @block.scalar
            def _(e):
                run(e, prog["act"])

            @block.vector
            def _(e):
                run(e, prog["dve"])

            @block.gpsimd
            def _(e):
                run(e, prog["pool"])

            @block.sync
            def _(e):
                run(e, prog["sp"])


def dram_ap(t, offset, dims):
    return bass.AP(t, offset, [list(d) for d in dims])


D = 2048
KC = 16
DFF = 5632
A_Q, A_KV, B_QK, B_V, C_W, GL = 1024, 256, 512, 1024, 1024, 6144
C_AQ, C_AK, C_AV, C_BQ, C_BK, C_BV, C_BOG, C_BGR, C_CQ, C_CK, C_CV, C_GL = (
    0, 1024, 1280, 1536, 2048, 2560, 3584, 4608, 4640, 5664, 6688, 7712)
IN_COLS = 13856
R_AQ, R_AK, R_BQ, R_BK, R_OG, R_GR, R_CQ, R_CK, R_GL = 0, 1024, 1280, 1792, 2304, 3328, 3360, 4384, 5408
NFM = 5408 + 6144
T_AV, T_BV, T_CV = 0, 256, 1280
NTM = 2304
EPS = 1e-6
NEG = -1e30
QS = 128 ** -0.5


class Arena:
    def __init__(self, nc, kb, nbytes):
        self.t = nc.alloc_sbuf_tensor("arena", [128, nbytes // 2], BF16)
        self.kb = kb
        self.nbytes = nbytes
        self.top = 0
        self.k = 0
        self.live = []

    def alloc(self, name, shape, dtype, parts=128):
        n = 1
        for s in shape:
            n *= s
        sz = 4 if dtype == F32 else 2
        nb = (n * sz + 31) // 32 * 32
        off = self.top
        assert off + nb <= self.nbytes, ("arena overflow", name, off, nb, self.nbytes)
        self.top = off + nb
        v = self.t[0:parts, off // 2: off // 2 + (n * sz) // 2]
        if dtype == F32:
            v = v.bitcast(F32)
        if len(shape) == 2:
            v = v.rearrange("p (a b) -> p a b", a=shape[0])
        elif len(shape) == 3:
            v = v.rearrange("p (a b c) -> p a b c", a=shape[0], b=shape[1])
        self.k += 1
        b = Buf("%s_%d" % (name, self.k), v)
        self.live.append((off, b))
        return b

    def track(self, name):
        self.k += 1
        b = Buf("%s_%d" % (name, self.k))
        self.live.append((self.top, b))
        return b

    def alloc_xt(self, name, nk, ntok):
        b = self.alloc(name, [nk, ntok], BF16)
        b.q = [self.track(name + "q") for _ in range(ntok // 512)]
        return b

    def mark(self):
        return self.top

    def reset(self, m):
        while self.live and self.live[-1][0] >= m:
            self.kb.release(self.live.pop()[1])
        self.top = m


class Ring:
    def __init__(self, items):
        self.items = items
        self.i = -1

    def next(self):
        self.i += 1
        return self.items[self.i % len(self.items)]

    def at(self, i):
        return self.items[i % len(self.items)]


class G:
    pass


def start_rows(r, R_tot, RS, typ):
    if typ == "P":
        return min(max(r - 4, 0), R_tot - 8)
    b = (r // RS) * RS
    return b + min(max(r - b - 4, 0), RS - 8)


def plan_C(T, SEG):
    R_tot, RS = T // 64, SEG // 64
    bands, keys = [], []
    for j in range(T // 128):
        r0 = 2 * j
        ss = [start_rows(r0 + rr, R_tot, RS, ty) for ty in "PS" for rr in (0, 1)]
        lo = min(ss) // 2 * 2
        hi = (max(ss) + 8 + 1) // 2 * 2
        lo = max(lo, 0)
        hi = min(hi, R_tot)
        assert lo >= r0 - 6 and hi <= r0 + 8, (j, lo, hi)
        bands.append((lo, hi))
        key = tuple(tuple(start_rows(r0 + rr, R_tot, RS, ty) - r0 for rr in (0, 1)) for ty in "PS")
        keys.append(key)
    uniq = sorted(set(keys))
    cls = [uniq.index(k) for k in keys]
    return bands, cls, uniq


def masks_C(uniq, typ):
    cols = np.arange(64)
    cstart = np.clip(cols - 8, 0, 48)
    inwin = (cols[None, :] >= cstart[:, None]) & (cols[None, :] < cstart[:, None] + 16)
    out = np.full((len(uniq), 128, 14, 64), NEG, np.float32)
    for ci, key in enumerate(uniq):
        rel = key[0 if typ == "P" else 1]
        for rr in (0, 1):
            st = rel[rr]
            for i in range(14):
                row = i - 6
                if st <= row < st + 8:
                    out[ci, rr * 64:(rr + 1) * 64, i, :] = np.where(inwin, 0.0, NEG)
    return out.reshape(len(uniq), 128, 14 * 64)


def plan_A(T, SEG):
    NT, NTS = T // 128, SEG // 128
    keys = []
    for n in range(NT):
        k = []
        for ty in "PS":
            if ty == "P":
                pv, nv = n > 0, n < NT - 1
            else:
                pv, nv = n % NTS != 0, n % NTS != NTS - 1
            k.append((pv, nv))
        keys.append(tuple(k))
    uniq = sorted(set(keys))
    return [uniq.index(k) for k in keys], uniq


def bias_A(uniq, typ):
    q = np.arange(128)[:, None]
    k = np.arange(384)[None, :] - 128
    dist = np.abs(q - k).astype(np.float32)
    out = np.zeros((len(uniq), 8, 128, 384), np.float32)
    for ci, key in enumerate(uniq):
        pv, nv = key[0 if typ == "P" else 1]
        valid = dist <= 128
        valid = valid & (pv | (k >= 0)) & (nv | (k < 128))
        for h in range(8):
            slope = 2.0 ** (-8.0 * (h + 1) / 8)
            out[ci, h] = np.where(valid, -slope * dist, NEG)
    return out.reshape(len(uniq) * 8, 128, 384)


def consts_np():
    s = np.arange(128)[:, None]
    t = np.arange(128)[None, :]
    c = np.zeros((128, 6, 128), np.float32)
    c[:, 0] = (s == t)
    c[:, 1] = np.where(s <= t, -1.0 / 16, 0.0)
    c[:, 2] = np.where(s >= t, -1.0 / 16, 0.0)
    c[:, 3] = (s <= t)
    c[:, 4] = (s >= t)
    c[:, 5] = 1.0
    return c.reshape(128, 768)


def mm_group(kb, out_ap, pairs, reads, bank):
    kb.group_begin("pe", reads=reads, writes=[bank])
    n = len(pairs)
    for i, (l, r) in enumerate(pairs):
        f = (lambda e, l=l, r=r, i=i: e.matmul(out_ap, lhsT=l, rhs=r, start=(i == 0), stop=(i == n - 1)))
        if i == n - 1:
            kb.group_end("pe", f, reads=reads, writes=[bank])
        else:
            kb.raw("pe", f)


def run_jobs(jobs, ring):
    n, nb = len(jobs), len(ring)
    for j in range(min(nb - 1, n)):
        jobs[j][0](ring[j % nb])
    for j in range(n):
        if j + nb - 1 < n:
            jobs[j + nb - 1][0](ring[(j + nb - 1) % nb])
        jobs[j][1](ring[j % nb])


def front_end(kb, g, xsrc, nvec_t, nvec_off, seg, XT):
    nc, ar = g.nc, g.ar
    SEG, NTS = g.SEG, g.SEG // 128
    m = ar.mark()
    gain = ar.alloc("gain", [D], F32)
    kb.dma("sp", lambda e: e.dma_start(out=gain.ap, in_=dram_ap(nvec_t, nvec_off, [[0, 128], [1, D]])), writes=[gain])
    xt = [ar.alloc("xt", [D], F32) for _ in range(2)]
    xn = [ar.alloc("xn", [D], BF16) for _ in range(2)]
    junk = ar.alloc("junk", [D], BF16)
    st = [ar.alloc("st", [4], F32) for _ in range(2)]
    for i in range(NTS):
        x_, n_, s_ = xt[i % 2], xn[i % 2], st[i % 2]
        r0 = seg * SEG + i * 128
        kb.dma("sp", lambda e, x_=x_, r0=r0: e.dma_start(out=x_.ap, in_=xsrc[r0:r0 + 128, :]), writes=[x_])
        kb.op("act", lambda e, x_=x_, s_=s_: e.activation(out=junk.ap, in_=x_.ap, func=AF.Square, accum_out=s_.ap[:, 0:1]),
              reads=[x_], writes=[junk, s_])
        kb.op("act", lambda e, s_=s_: e.activation(out=s_.ap[:, 1:2], in_=s_.ap[:, 0:1], func=AF.Sqrt, scale=1.0 / D, bias=g.eps.ap[:, 0:1]),
              reads=[s_, g.eps], writes=[s_])
        kb.op("dve", lambda e, s_=s_: e.reciprocal(out=s_.ap[:, 2:3], in_=s_.ap[:, 1:2]), reads=[s_], writes=[s_])
        kb.op("dve", lambda e, x_=x_, n_=n_, s_=s_: e.scalar_tensor_tensor(out=n_.ap, in0=x_.ap, scalar=s_.ap[:, 2:3], in1=gain.ap,
                                                                         op0=ALU.mult, op1=ALU.mult), reads=[x_, s_, gain], writes=[n_])
        b0, b1 = g.bank[4 + 2 * (i % 2)], g.bank[5 + 2 * (i % 2)]
        pv = g.PS[:, (4 + 2 * (i % 2)) * 512:(6 + 2 * (i % 2)) * 512].bitcast(BF16).rearrange("p (k t) -> p k t", k=KC)
        kb.group_begin("pe", reads=[n_, g.ident], writes=[b0, b1])
        for kc in range(KC):
            f = lambda e, kc=kc, n_=n_, pv=pv: e.transpose(out=pv[:, kc, :], in_=n_.ap[:, kc * 128:(kc + 1) * 128], identity=g.ident.ap)
            if kc == KC - 1:
                kb.group_end("pe", f, reads=[n_, g.ident], writes=[b0, b1])
            else:
                kb.raw("pe", f)
        eng = "act" if i % 2 == 0 else "dve"
        if eng == "act":
            kb.op("act", lambda e, pv=pv, i=i: e.copy(out=XT.ap[:, 0:KC, i * 128:(i + 1) * 128], in_=pv), writes=[XT.q[i // 4], b0, b1])
        else:
            kb.op("dve", lambda e, pv=pv, i=i: e.tensor_copy(out=XT.ap[:, 0:KC, i * 128:(i + 1) * 128], in_=pv), writes=[XT.q[i // 4], b0, b1])
    kb.barrier()
    ar.reset(m)


def load_xt(kb, g, XT, src_t, row0, nk, seg):
    SEG = g.SEG
    for t in range(SEG // 512):
        c0 = seg * SEG + t * 512
        src = src_t.ap()[row0: row0 + nk * 128, c0:c0 + 512].rearrange("(k p) t -> p k t", p=128)
        kb.dma("sp", lambda e, src=src, t=t: e.dma_start(out=XT.ap[:, 0:nk, t * 512:(t + 1) * 512], in_=src), writes=[XT.q[t]])


def phase_G1(kb, g, l, seg):
    nc, ar = g.nc, g.ar
    SEG, NTS, NTT = g.SEG, g.SEG // 128, g.SEG // 512
    m0 = ar.mark()
    XT = ar.alloc_xt("XT", KC, SEG)
    xsrc = g.x_in.ap() if l == 0 else g.xr.ap()
    front_end(kb, g, xsrc, g.norm1, l * D, seg, XT)
    slabs = [ar.alloc("ws", [KC, 512], BF16) for _ in range(3)]
    fst = Ring([ar.alloc("fst", [SEG], BF16) for _ in range(3)])
    tst = Ring([ar.alloc("tst", [4, 512], BF16) for _ in range(2)])
    banks = Ring([0, 1, 2, 3])
    evc = [0]
    tok0 = seg * SEG
    groups = [(C_AQ, A_Q, "F", R_AQ, AF.Copy, QS), (C_AK, A_KV, "F", R_AK, AF.Copy, 1.0), (C_AV, A_KV, "T", T_AV, None, 1.0),
              (C_BQ, B_QK, "F", R_BQ, AF.Copy, QS), (C_BK, B_QK, "F", R_BK, AF.Copy, 1.0), (C_BV, B_V, "T", T_BV, None, 1.0),
              (C_BOG, B_V, "F", R_OG, AF.Silu, 1.0), (C_BGR, 32, "F", R_GR, AF.Copy, 1.0),
              (C_CQ, C_W, "F", R_CQ, AF.Copy, QS), (C_CK, C_W, "F", R_CK, AF.Copy, 1.0), (C_CV, C_W, "T", T_CV, None, 1.0),
              (C_GL, GL, "F", R_GL, AF.Sigmoid, 1.0)]
    jobs = []
    for (c0, wtot, kind, dst, func, scale) in groups:
        for s0 in range(0, wtot, 512):
            w = min(512, wtot - s0)

            def load(slab, c0=c0, s0=s0, w=w):
                src = g.w_in.ap()[l, :, c0 + s0:c0 + s0 + w].rearrange("(k p) n -> p k n", p=128)
                kb.dma("pool", lambda e: e.dma_start(out=slab.ap[:, :, 0:w], in_=src), writes=[slab])

            if kind == "F":
                def comp(slab, s0=s0, w=w, dst=dst, func=func, scale=scale):
                    for c in range((w + 127) // 128):
                        cw = min(128, w - c * 128)
                        stg = fst.next()
                        for t in range(NTT):
                            b = banks.next()
                            out_ap = g.PS[0:cw, b * 512:(b + 1) * 512]
                            pairs = [(slab.ap[:, kc, c * 128:c * 128 + cw], XT.ap[:, kc, t * 512:(t + 1) * 512]) for kc in range(KC)]
                            mm_group(kb, out_ap, pairs, [slab, XT.q[t]], g.bank[b])
                            kb.op("act", lambda e, out_ap=out_ap, stg=stg, t=t, cw=cw: e.activation(
                                out=stg.ap[0:cw, t * 512:(t + 1) * 512], in_=out_ap, func=func, scale=scale), writes=[g.bank[b], stg])
                        r0 = dst + s0 + c * 128
                        kb.dma("sp", lambda e, stg=stg, r0=r0, cw=cw: e.dma_start(out=g.pfm.ap()[r0:r0 + cw, tok0:tok0 + SEG], in_=stg.ap[0:cw, :]),
                               reads=[stg], writes=[g.v_pfm])
            else:
                def comp(slab, s0=s0, w=w, dst=dst):
                    for i4 in range(NTS // 4):
                        stg = tst.next()
                        for ii in range(4):
                            i = i4 * 4 + ii
                            b = banks.next()
                            out_ap = g.PS[:, b * 512:b * 512 + w]
                            pairs = [(XT.ap[:, kc, i * 128:(i + 1) * 128], slab.ap[:, kc, 0:w]) for kc in range(KC)]
                            mm_group(kb, out_ap, pairs, [slab, XT.q[i4]], g.bank[b])
                            evc[0] += 1
                            if evc[0] % 2:
                                kb.op("dve", lambda e, out_ap=out_ap, stg=stg, ii=ii: e.tensor_copy(out=stg.ap[:, ii, 0:w], in_=out_ap),
                                      writes=[g.bank[b], stg])
                            else:
                                kb.op("act", lambda e, out_ap=out_ap, stg=stg, ii=ii: e.copy(out=stg.ap[:, ii, 0:w], in_=out_ap),
                                      writes=[g.bank[b], stg])
                        t0 = tok0 + i4 * 512
                        dstap = g.ptm.ap()[t0:t0 + 512, dst + s0:dst + s0 + w].rearrange("(i p) c -> p i c", p=128)
                        kb.dma("sp", lambda e, stg=stg, dstap=dstap: e.dma_start(out=dstap, in_=stg.ap[:, :, 0:w]), reads=[stg], writes=[g.v_ptm])
            jobs.append((load, comp))
    run_jobs(jobs, slabs)
    kb.barrier()
    ar.reset(m0)


def phase_G2(kb, g, l, seg):
    ar = g.ar
    SEG, NTT = g.SEG, g.SEG // 512
    m0 = ar.mark()
    tok0 = seg * SEG
    OT = [ar.alloc_xt("OT", 8, SEG) for _ in range(3)]
    for i, src in enumerate((g.oaT, g.obT, g.ocT)):
        load_xt(kb, g, OT[i], src, 0, 8, seg)
    slabs = [ar.alloc("ws", [3, 8, 512], BF16) for _ in range(2)]
    gts = Ring([ar.alloc("gts", [3, SEG], BF16) for _ in range(2)])
    mst = Ring([ar.alloc("mst", [SEG], BF16) for _ in range(2)])
    tmp = Ring([ar.alloc("tmp", [3, 512], F32) for _ in range(2)])
    bsets = Ring([(0, 1, 2), (3, 4, 5)])
    wbr = (g.w_br_a, g.w_br_b, g.w_br_c)
    jobs = []
    for js in range(4):
        def load(slab, js=js):
            for i in range(3):
                src = wbr[i].ap()[l, :, js * 512:(js + 1) * 512].rearrange("(k p) n -> p k n", p=128)
                kb.dma("pool", lambda e, src=src, i=i: e.dma_start(out=slab.ap[:, i, :, :], in_=src), writes=[slab])

        def comp(slab, js=js):
            for mch in range(4):
                fch = js * 4 + mch
                gt = gts.next()
                gsrc = dram_ap(g.pfm, (R_GL + fch * 128) * g.T + tok0, [[g.T, 128], [D * g.T, 3], [1, SEG]])
                kb.dma("sp", lambda e, gt=gt, gsrc=gsrc: e.dma_start(out=gt.ap, in_=gsrc), reads=[g.v_pfm], writes=[gt])
                stg = mst.next()
                for t in range(NTT):
                    bs = bsets.next()
                    tp = tmp.next()
                    for i in range(3):
                        out_ap = g.PS[:, bs[i] * 512:(bs[i] + 1) * 512]
                        pairs = [(slab.ap[:, i, kc, mch * 128:(mch + 1) * 128], OT[i].ap[:, kc, t * 512:(t + 1) * 512]) for kc in range(8)]
                        mm_group(kb, out_ap, pairs, [slab, OT[i].q[t]], g.bank[bs[i]])
                        kb.op("dve", lambda e, out_ap=out_ap, tp=tp, gt=gt, i=i, t=t: e.tensor_tensor(
                            out=tp.ap[:, i, :], in0=out_ap, in1=gt.ap[:, i, t * 512:(t + 1) * 512], op=ALU.mult),
                            reads=[gt], writes=[g.bank[bs[i]], tp])
                    kb.op("pool", lambda e, tp=tp: e.tensor_tensor(out=tp.ap[:, 0, :], in0=tp.ap[:, 0, :], in1=tp.ap[:, 1, :], op=ALU.add),
                          writes=[tp])
                    kb.op("pool", lambda e, tp=tp, stg=stg, t=t: e.tensor_tensor(out=stg.ap[:, t * 512:(t + 1) * 512], in0=tp.ap[:, 0, :],
                                                                                 in1=tp.ap[:, 2, :], op=ALU.add), reads=[tp], writes=[stg])
                kb.dma("sp", lambda e, stg=stg, fch=fch: e.dma_start(out=g.mT.ap()[fch * 128:(fch + 1) * 128, tok0:tok0 + SEG], in_=stg.ap),
                       reads=[stg], writes=[g.v_mT])
        jobs.append((load, comp))
    run_jobs(jobs, slabs)
    kb.barrier()
    ar.reset(m0)


def tm_update_jobs(kb, g, XT, nk, wsrc_fn, xsrc, slabs, seg):
    ar = g.ar
    SEG, NTS = g.SEG, g.SEG // 128
    tok0 = seg * SEG
    xo = Ring([ar.alloc("xo", [4, 512], F32) for _ in range(2)])
    xs = Ring([ar.alloc("xs", [4, 512], F32) for _ in range(2)])
    banks = Ring([0, 1, 2, 3])
    jobs = []
    for js in range(4):
        def load(slab, js=js):
            for k0 in range(0, nk, 8):
                k1 = min(nk, k0 + 8)
                kb.dma("pool", lambda e, k0=k0, k1=k1: e.dma_start(out=slab.ap[:, k0:k1, :], in_=wsrc_fn(js, k0, k1)), writes=[slab])

        def comp(slab, js=js):
            for i4 in range(NTS // 4):
                t0 = tok0 + i4 * 512
                xold, xnew = xo.next(), xs.next()
                sap = xsrc[t0:t0 + 512, js * 512:(js + 1) * 512].rearrange("(i p) c -> p i c", p=128)
                kb.dma("sp", lambda e, xold=xold, sap=sap: e.dma_start(out=xold.ap, in_=sap), reads=[g.v_xr], writes=[xold])
                for ii in range(4):
                    i = i4 * 4 + ii
                    b = banks.next()
                    out_ap = g.PS[:, b * 512:(b + 1) * 512]
                    pairs = [(XT.ap[:, kc, i * 128:(i + 1) * 128], slab.ap[:, kc, :]) for kc in range(nk)]
                    mm_group(kb, out_ap, pairs, [slab, XT.q[i4]], g.bank[b])
                    kb.op("dve", lambda e, out_ap=out_ap, xold=xold, xnew=xnew, ii=ii: e.tensor_tensor(
                        out=xnew.ap[:, ii, :], in0=out_ap, in1=xold.ap[:, ii, :], op=ALU.add), reads=[xold], writes=[g.bank[b], xnew])
                dap = g.xr.ap()[t0:t0 + 512, js * 512:(js + 1) * 512].rearrange("(i p) c -> p i c", p=128)
                kb.dma("sp", lambda e, xnew=xnew, dap=dap: e.dma_start(out=dap, in_=xnew.ap), reads=[xnew], writes=[g.v_xr])
        jobs.append((load, comp))
    run_jobs(jobs, slabs)


def phase_G3(kb, g, l, seg):
    ar = g.ar
    m0 = ar.mark()
    XT = ar.alloc_xt("XT", KC, g.SEG)
    load_xt(kb, g, XT, g.mT, 0, KC, seg)
    slabs = [ar.alloc("ws", [KC, 512], BF16) for _ in range(3)]
    xsrc = g.x_in.ap() if l == 0 else g.xr.ap()

    def wsrc(js, k0, k1):
        return g.w_out.ap()[l, k0 * 128:k1 * 128, js * 512:(js + 1) * 512].rearrange("(k p) n -> p k n", p=128)
    tm_update_jobs(kb, g, XT, KC, wsrc, xsrc, slabs, seg)
    kb.barrier()
    ar.reset(m0)


def phase_G4(kb, g, l, seg):
    ar = g.ar
    SEG, NTT = g.SEG, g.SEG // 512
    m0 = ar.mark()
    tok0 = seg * SEG
    XT = ar.alloc_xt("XT", KC, SEG)
    front_end(kb, g, g.xr.ap(), g.norm2, l * D, seg, XT)
    slabs = [ar.alloc("ws", [KC, 2, 256], BF16) for _ in range(3)]
    fst = Ring([ar.alloc("fst", [SEG], BF16) for _ in range(3)])
    tmp = Ring([ar.alloc("tmp", [512], F32) for _ in range(2)])
    bsets = Ring([(0, 1), (2, 3), (4, 5)])
    jobs = []
    for jf in range(DFF // 256):
        def load(slab, jf=jf):
            for part in range(2):
                src = g.w_ffn_in.ap()[l, :, part * DFF + jf * 256: part * DFF + (jf + 1) * 256].rearrange("(k p) n -> p k n", p=128)
                kb.dma("pool", lambda e, src=src, part=part: e.dma_start(out=slab.ap[:, :, part, :], in_=src), writes=[slab])

        def comp(slab, jf=jf):
            for c in range(2):
                stg = fst.next()
                for t in range(NTT):
                    bg, bu = bsets.next()
                    og = g.PS[:, bg * 512:(bg + 1) * 512]
                    ou = g.PS[:, bu * 512:(bu + 1) * 512]
                    mm_group(kb, og, [(slab.ap[:, kc, 0, c * 128:(c + 1) * 128], XT.ap[:, kc, t * 512:(t + 1) * 512]) for kc in range(KC)],
                             [slab, XT.q[t]], g.bank[bg])
                    mm_group(kb, ou, [(slab.ap[:, kc, 1, c * 128:(c + 1) * 128], XT.ap[:, kc, t * 512:(t + 1) * 512]) for kc in range(KC)],
                             [slab, XT.q[t]], g.bank[bu])
                    tp = tmp.next()
                    kb.op("act", lambda e, og=og, tp=tp: e.activation(out=tp.ap, in_=og, func=AF.Silu), writes=[g.bank[bg], tp])
                    kb.op("dve", lambda e, ou=ou, tp=tp, stg=stg, t=t: e.tensor_tensor(out=stg.ap[:, t * 512:(t + 1) * 512], in0=ou, in1=tp.ap,
                                                                                     op=ALU.mult), reads=[tp], writes=[g.bank[bu], stg])
                r0 = jf * 256 + c * 128
                kb.dma("sp", lambda e, stg=stg, r0=r0: e.dma_start(out=g.actT.ap()[r0:r0 + 128, tok0:tok0 + SEG], in_=stg.ap),
                       reads=[stg], writes=[g.v_actT])
        jobs.append((load, comp))
    run_jobs(jobs, slabs)
    kb.barrier()
    ar.reset(m0)


def phase_G5(kb, g, l, seg):
    ar = g.ar
    HK = DFF // 256
    for kh in range(2):
        m0 = ar.mark()
        XT = ar.alloc_xt("XT", HK, g.SEG)
        load_xt(kb, g, XT, g.actT, kh * HK * 128, HK, seg)
        slabs = [ar.alloc("ws", [HK, 512], BF16) for _ in range(2)]

        def wsrc(js, k0, k1, kh=kh):
            r0 = kh * HK * 128
            return g.w_ffn_out.ap()[l, r0 + k0 * 128:r0 + k1 * 128, js * 512:(js + 1) * 512].rearrange("(k p) n -> p k n", p=128)
        tm_update_jobs(kb, g, XT, HK, wsrc, g.xr.ap(), slabs, seg)
        kb.barrier()
        ar.reset(m0)


def phase_final(kb, g):
    ar = g.ar
    m0 = ar.mark()
    gain = ar.alloc("gain", [D], F32)
    kb.dma("sp", lambda e: e.dma_start(out=gain.ap, in_=dram_ap(g.norm_f, 0, [[0, 128], [1, D]])), writes=[gain])
    xt = [ar.alloc("xt", [D], F32) for _ in range(3)]
    yo = [ar.alloc("yo", [D], F32) for _ in range(3)]
    junk = ar.alloc("junk", [D], BF16)
    st = [ar.alloc("st", [4], F32) for _ in range(3)]
    for i in range(g.T // 128):
        x_, y_, s_ = xt[i % 3], yo[i % 3], st[i % 3]
        kb.dma("sp", lambda e, x_=x_, i=i: e.dma_start(out=x_.ap, in_=g.xr.ap()[i * 128:(i + 1) * 128, :]), reads=[g.v_xr], writes=[x_])
        kb.op("act", lambda e, x_=x_, s_=s_: e.activation(out=junk.ap, in_=x_.ap, func=AF.Square, accum_out=s_.ap[:, 0:1]),
              reads=[x_], writes=[junk, s_])
        kb.op("act", lambda e, s_=s_: e.activation(out=s_.ap[:, 1:2], in_=s_.ap[:, 0:1], func=AF.Sqrt, scale=1.0 / D, bias=g.eps.ap[:, 0:1]),
              reads=[s_, g.eps], writes=[s_])
        kb.op("dve", lambda e, s_=s_: e.reciprocal(out=s_.ap[:, 2:3], in_=s_.ap[:, 1:2]), reads=[s_], writes=[s_])
        kb.op("dve", lambda e, x_=x_, y_=y_, s_=s_: e.scalar_tensor_tensor(out=y_.ap, in0=x_.ap, scalar=s_.ap[:, 2:3], in1=gain.ap,
                                                                         op0=ALU.mult, op1=ALU.mult), reads=[x_, s_, gain], writes=[y_])
        kb.dma("sp", lambda e, y_=y_, i=i: e.dma_start(out=g.y.ap()[i * 128:(i + 1) * 128, :], in_=y_.ap), reads=[y_], writes=[g.v_y])
    kb.barrier()
    ar.reset(m0)


def run_pipeline(n_iter, make_stages, ns):
    st = {}
    for tau in range(n_iter + ns - 1):
        for k in reversed(range(ns)):
            i = tau - k
            if 0 <= i < n_iter:
                if i not in st:
                    st[i] = make_stages(i)
                st[i][k]()
                if k == ns - 1:
                    del st[i]


def attn_tail(kb, g, Sviews, nk, nkt, vbuf, vt0, small, Pf, Pn, pTp, pTpb, pT, oTp, oTpb, ost, ocol, sink_ap):
    S_ap, Sbanks = Sviews

    def s1a():
        if sink_ap is None:
            kb.op("dve", lambda e: e.tensor_reduce(out=small.ap[:, 0:1], in_=S_ap[:, 0:nk], axis=AX.X, op=ALU.max, negate=True),
                  writes=Sbanks + [small])
        else:
            kb.op("dve", lambda e: e.tensor_reduce(out=small.ap[:, 4:5], in_=S_ap[:, 0:nk], axis=AX.X, op=ALU.max), writes=Sbanks + [small])
            kb.op("dve", lambda e: e.tensor_scalar(out=small.ap[:, 0:1], in0=small.ap[:, 4:5], scalar1=sink_ap, scalar2=-1.0,
                                                   op0=ALU.max, op1=ALU.mult), reads=[g.sinkb], writes=[small])

    def s1b():
        kb.op("act", lambda e: e.activation(out=Pf.ap[:, 0:nk], in_=S_ap[:, 0:nk], func=AF.Exp, bias=small.ap[:, 0:1],
                                            accum_out=small.ap[:, 1:2]), writes=Sbanks + [small, Pf])
        if sink_ap is not None:
            kb.op("act", lambda e: e.activation(out=small.ap[:, 2:3], in_=sink_ap, func=AF.Exp, bias=small.ap[:, 0:1]),
                  reads=[g.sinkb], writes=[small])

    def s1c():
        if sink_ap is not None:
            kb.op("dve", lambda e: e.tensor_tensor(out=small.ap[:, 1:2], in0=small.ap[:, 1:2], in1=small.ap[:, 2:3], op=ALU.add),
                  writes=[small])
        kb.op("dve", lambda e: e.reciprocal(out=small.ap[:, 3:4], in_=small.ap[:, 1:2]), writes=[small])
        kb.op("act", lambda e: e.activation(out=Pn.ap[:, 0:nk], in_=Pf.ap[:, 0:nk], func=AF.Copy, scale=small.ap[:, 3:4]),
              reads=[Pf, small], writes=[Pn])

    def s2a():
        kb.group_begin("pe", reads=[Pn, g.ident], writes=[pTpb])
        for k in range(nkt):
            f = lambda e, k=k: e.transpose(out=pTp[:, k, :], in_=Pn.ap[:, k * 128:(k + 1) * 128], identity=g.ident.ap)
            if k == nkt - 1:
                kb.group_end("pe", f, reads=[Pn, g.ident], writes=[pTpb])
            else:
                kb.raw("pe", f)

    def s2b():
        kb.op("dve", lambda e: e.tensor_copy(out=pT.ap[:, 0:nkt, :], in_=pTp[:, 0:nkt, :]), writes=[pTpb, pT])

    def s3a():
        pairs = [(vbuf.ap[:, vt0 + k, :], pT.ap[:, k, :]) for k in range(nkt)]
        mm_group(kb, oTp, pairs, [vbuf, pT], oTpb)

    def s3b():
        kb.op("dve", lambda e: e.tensor_copy(out=ost.ap[:, ocol:ocol + 128], in_=oTp), writes=[oTpb, ost])
    return [s1a, s1b, s1c, s2a, s2b, s3a, s3b]


def mixer_A(kb, g, l):
    ar = g.ar
    T, NT = g.T, g.T // 128
    m0 = ar.mark()
    ncA = len(g.uniqA)
    biasA = ar.alloc("biasA", [ncA * 8, 384], BF16)
    kb.dma("pool", lambda e: e.dma_start(out=biasA.ap, in_=g.d_biasA.ap().rearrange("c p k -> p c k")), writes=[biasA])
    g.sinkb = ar.alloc("sink", [8], F32)
    kb.dma("sp", lambda e: e.dma_start(out=g.sinkb.ap, in_=dram_ap(g.sink_a, l * 8, [[0, 128], [1, 8]])), writes=[g.sinkb])
    kT = [ar.alloc("kT", [T], BF16) for _ in range(2)]
    vv = [ar.alloc("vv", [NT, 128], BF16) for _ in range(2)]
    qT = [ar.alloc("qT", [T], BF16) for _ in range(2)]
    Pf = [ar.alloc("Pf", [384], F32) for _ in range(2)]
    Pn = [ar.alloc("Pn", [384], BF16) for _ in range(2)]
    pT = [ar.alloc("pT", [3, 128], BF16) for _ in range(2)]
    ost = [ar.alloc("ost", [2048], BF16) for _ in range(2)]
    small = [ar.alloc("sm", [8], F32) for _ in range(4)]
    for gi in range(2):
        kb.dma("sp", lambda e, gi=gi: e.dma_start(out=kT[gi].ap, in_=g.pfm.ap()[R_AK + gi * 128:R_AK + (gi + 1) * 128, :]),
               reads=[g.v_pfm], writes=[kT[gi]])
        kb.dma("sp", lambda e, gi=gi: e.dma_start(out=vv[gi].ap, in_=g.ptm.ap()[:, T_AV + gi * 128:T_AV + (gi + 1) * 128].rearrange(
            "(n p) d -> p n d", p=128)), reads=[g.v_ptm], writes=[vv[gi]])

    def load_q(h):
        kb.dma("sp", lambda e: e.dma_start(out=qT[h % 2].ap, in_=g.pfm.ap()[R_AQ + h * 128:R_AQ + (h + 1) * 128, :]),
               reads=[g.v_pfm], writes=[qT[h % 2]])
    load_q(0)
    cnt = [0]
    for h in range(8):
        if h + 1 < 8:
            load_q(h + 1)
        gi = h // 4
        q_, k_, v_ = qT[h % 2], kT[gi], vv[gi]

        def make(n, h=h, q_=q_, k_=k_, v_=v_):
            it = cnt[0]
            cnt[0] += 1
            lo, hi = max(n - 1, 0), min(n + 1, NT - 1)
            nkt = hi - lo + 1
            nk = nkt * 128
            off = (lo - (n - 1)) * 128
            cls = g.clsA[n]
            sb = it % 3
            S_ap = g.PS[:, sb * 512:sb * 512 + 512]
            pb = 3 + it % 2
            pTp = g.PS[:, pb * 512:(pb + 1) * 512].bitcast(BF16)[:, 0:384].rearrange("p (k t) -> p k t", k=3)
            ob = 5 + it % 2
            oTp = g.PS[:, ob * 512:ob * 512 + 128]
            os_ = ost[(n // 16) % 2]

            def st0():
                kb.group_begin("pe", reads=[q_, k_, biasA, g.ident], writes=[g.bank[sb]])
                kb.raw("pe", lambda e: e.matmul(S_ap[:, 0:nk], lhsT=q_.ap[:, n * 128:(n + 1) * 128], rhs=k_.ap[:, lo * 128:(hi + 1) * 128],
                                                start=True, stop=False))
                kb.group_end("pe", lambda e: e.matmul(S_ap[:, 0:nk], lhsT=g.ident.ap, rhs=biasA.ap[:, cls * 8 + h, off:off + nk],
                                                      start=False, stop=True), reads=[q_, k_, biasA, g.ident], writes=[g.bank[sb]])
            tail = attn_tail(kb, g, (S_ap, [g.bank[sb]]), nk, nkt, v_, lo, small[it % 4], Pf[it % 2], Pn[it % 2], pTp, g.bank[pb],
                             pT[it % 2], oTp, g.bank[ob], os_, (n % 16) * 128, g.sinkb.ap[:, h:h + 1])

            def st3b():
                tail[6]()
                if n % 16 == 15 or n == NT - 1:
                    t0 = (n // 16) * 2048
                    nt = (n % 16 + 1) * 128
                    kb.dma("sp", lambda e: e.dma_start(out=g.oaT.ap()[h * 128:(h + 1) * 128, t0:t0 + nt], in_=os_.ap[:, 0:nt]),
                           reads=[os_], writes=[g.v_oaT])
            return [st0] + tail[0:6] + [st3b]
        run_pipeline(NT, make, 8)
    kb.barrier()
    ar.reset(m0)


def build_rpb_table(kb, g, l, rpbT):
    ar = g.ar
    zt = ar.alloc("zt", [960], F32)
    kb.op("pool", lambda e: e.memset(zt.ap, 0.0), writes=[zt])
    for h in range(8):
        kb.dma("sp", lambda e, h=h: e.dma_start(out=dram_ap(g.Gtab, h * 15 * 64 * 128, [[960, 128], [1, 960]]), in_=zt.ap),
               reads=[zt], writes=[g.v_G])
    for h in range(8):
        kb.dma("sp", lambda e, h=h: e.dma_start(out=dram_ap(g.Gtab, h * 15 * 64 * 128 + 48, [[64 * 128, 15], [128, 64], [1, 31]]),
                                               in_=dram_ap(g.rpb_c, (l * 8 + h) * 15 * 31, [[31, 15], [0, 64], [1, 31]])), writes=[g.v_G])
    for h in range(8):
        for rr in range(2):
            kb.dma("pool", lambda e, h=h, rr=rr: e.dma_start(
                out=rpbT.ap[rr * 64:(rr + 1) * 64, h, :].rearrange("p (i k) -> p i k", i=14),
                in_=dram_ap(g.Gtab, h * 15 * 64 * 128 + (1 - rr) * 64 * 128 + 63, [[127, 64], [64 * 128, 14], [1, 64]])),
                reads=[g.v_G], writes=[rpbT])


def mixer_C(kb, g, l):
    ar = g.ar
    T, NT = g.T, g.T // 128
    m0 = ar.mark()
    ncC = len(g.uniqC)
    rpbT = ar.alloc("rpbT", [8, 896], BF16)
    build_rpb_table(kb, g, l, rpbT)
    maskC = ar.alloc("maskC", [ncC, 896], BF16)
    kb.dma("pool", lambda e: e.dma_start(out=maskC.ap, in_=g.d_maskC.ap().rearrange("c p k -> p c k")), writes=[maskC])
    qT = [ar.alloc("qT", [T], BF16) for _ in range(2)]
    kT = [ar.alloc("kT", [T], BF16) for _ in range(2)]
    vv = [ar.alloc("vv", [NT, 128], BF16) for _ in range(2)]
    Pf = [ar.alloc("Pf", [896], F32) for _ in range(2)]
    Pn = [ar.alloc("Pn", [896], BF16) for _ in range(2)]
    pT = [ar.alloc("pT", [7, 128], BF16) for _ in range(2)]
    ost = [ar.alloc("ost", [2048], BF16) for _ in range(2)]
    small = [ar.alloc("sm", [8], F32) for _ in range(4)]

    def load_h(h):
        b = h % 2
        kb.dma("sp", lambda e: e.dma_start(out=qT[b].ap, in_=g.pfm.ap()[R_CQ + h * 128:R_CQ + (h + 1) * 128, :]), reads=[g.v_pfm], writes=[qT[b]])
        kb.dma("sp", lambda e: e.dma_start(out=kT[b].ap, in_=g.pfm.ap()[R_CK + h * 128:R_CK + (h + 1) * 128, :]), reads=[g.v_pfm], writes=[kT[b]])
        kb.dma("sp", lambda e: e.dma_start(out=vv[b].ap, in_=g.ptm.ap()[:, T_CV + h * 128:T_CV + (h + 1) * 128].rearrange(
            "(n p) d -> p n d", p=128)), reads=[g.v_ptm], writes=[vv[b]])
    load_h(0)
    cnt = [0]
    for h in range(8):
        if h + 1 < 8:
            load_h(h + 1)
        q_, k_, v_ = qT[h % 2], kT[h % 2], vv[h % 2]

        def make(j, h=h, q_=q_, k_=k_, v_=v_):
            it = cnt[0]
            cnt[0] += 1
            lo, hi = g.bandC[j]
            nk = (hi - lo) * 64
            nkt = (nk + 127) // 128
            rel = (lo - (2 * j - 6)) * 64
            cls = g.clsC[j]
            sb = 2 * (it % 2)
            S_ap = g.PS[:, sb * 512:sb * 512 + 1024]
            Sb = [g.bank[sb], g.bank[sb + 1]]
            pb = 4 + it % 2
            pTp = g.PS[:, pb * 512:(pb + 1) * 512].bitcast(BF16)[:, 0:896].rearrange("p (k t) -> p k t", k=7)
            ob = 6 + it % 2
            oTp = g.PS[:, ob * 512:ob * 512 + 128]
            os_ = ost[(j // 16) % 2]

            def st0():
                rd = [q_, k_, rpbT, maskC, g.ident]
                for ci, (c0, c1) in enumerate([(0, min(512, nk)), (512, nk)]):
                    if c1 <= c0:
                        continue
                    kb.group_begin("pe", reads=rd, writes=[Sb[ci]])
                    kb.raw("pe", lambda e, c0=c0, c1=c1: e.matmul(S_ap[:, c0:c1], lhsT=q_.ap[:, j * 128:(j + 1) * 128],
                                                                  rhs=k_.ap[:, lo * 64 + c0:lo * 64 + c1], start=True, stop=False))
                    kb.raw("pe", lambda e, c0=c0, c1=c1: e.matmul(S_ap[:, c0:c1], lhsT=g.ident.ap, rhs=rpbT.ap[:, h, rel + c0:rel + c1],
                                                                  start=False, stop=False))
                    kb.group_end("pe", lambda e, c0=c0, c1=c1: e.matmul(S_ap[:, c0:c1], lhsT=g.ident.ap, rhs=maskC.ap[:, cls, rel + c0:rel + c1],
                                                                        start=False, stop=True), reads=rd, writes=[Sb[ci]])
            tail = attn_tail(kb, g, (S_ap, Sb), nk, nkt, v_, lo // 2, small[it % 4], Pf[it % 2], Pn[it % 2], pTp, g.bank[pb],
                             pT[it % 2], oTp, g.bank[ob], os_, (j % 16) * 128, None)

            def st3b():
                tail[6]()
                if j % 16 == 15 or j == NT - 1:
                    t0 = (j // 16) * 2048
                    nt = (j % 16 + 1) * 128
                    kb.dma("sp", lambda e: e.dma_start(out=g.ocT.ap()[h * 128:(h + 1) * 128, t0:t0 + nt], in_=os_.ap[:, 0:nt]),
                           reads=[os_], writes=[g.v_ocT])
            return [st0] + tail[0:6] + [st3b]
        run_pipeline(NT, make, 8)
    kb.barrier()
    ar.reset(m0)


def mixer_B(kb, g, l):
    ar = g.ar
    T, NT, NTS = g.T, g.T // 128, g.SEG // 128
    NG = NT // 4
    m0 = ar.mark()
    gain2 = ar.alloc("gain2", [2], F32)
    kb.dma("sp", lambda e: e.dma_start(out=gain2.ap, in_=dram_ap(g.gla_norm, l * 256, [[1, 128], [128, 2]]), allow_slow_non_contiguous=True), writes=[gain2])
    Z = ar.alloc("Z", [4, 256], F32)
    Sbf = ar.alloc("Sbf", [4, 256], BF16)
    dec = ar.alloc("dec", [3, 4], F32)
    w2a = ar.alloc("w2a", [512], BF16, parts=17)
    grA = [ar.alloc("grA", [512], BF16, parts=17) for _ in range(2)]
    qB = [ar.alloc("qB", [4, 512], BF16) for _ in range(2)]
    kB = [ar.alloc("kB", [4, 512], BF16) for _ in range(2)]
    vB = [ar.alloc("vB", [4, 1024], BF16) for _ in range(2)]
    E1 = [ar.alloc("E1", [512], F32) for _ in range(2)]
    SP = [ar.alloc("SP", [512], F32) for _ in range(2)]
    EB = [ar.alloc("EB", [4, 128], F32) for _ in range(2)]
    ENB = [ar.alloc("ENB", [4, 128], F32) for _ in range(2)]
    QT = [ar.alloc("QT", [4, 128], BF16) for _ in range(2)]
    KT = [ar.alloc("KT", [4, 128], BF16) for _ in range(2)]
    KTT = [ar.alloc("KTT", [512], BF16) for _ in range(2)]
    ATM = [ar.alloc("ATM", [4, 128], BF16) for _ in range(2)]
    for b in grA:
        kb.op("pool", lambda e, b=b: e.memset(b.ap, 1.0), writes=[b])
    m1 = ar.mark()
    BL, BB, BT, BA, BO0, BO1, BU0, BU1 = range(8)
    PSL = g.PS[:, BL * 512:(BL + 1) * 512]
    PSB = g.PS[:, BB * 512:(BB + 1) * 512]
    PST = g.PS[:, BT * 512:(BT + 1) * 512].bitcast(BF16)[:, 0:512].rearrange("p (h t) -> p h t", h=4)
    PSA = g.PS[:, BA * 512:(BA + 1) * 512]
    PSO = g.PS[:, BO0 * 512:(BO1 + 1) * 512]
    PSU = g.PS[:, BU0 * 512:(BU1 + 1) * 512]
    def one_pass(d):
        ar.reset(m1)
        tri = g.cst.ap[:, 1 + d, :]
        msk = g.cst.ap[:, 3 + d, :]
        lastcol = 127 if d == 0 else 0
        if d == 0:
            OST = [ar.alloc("OST", [8, 512], F32) for _ in range(2)]
        else:
            OST = [ar.alloc("OBs", [8, 512], BF16) for _ in range(2)]
            ogB = [ar.alloc("ogB", [8, 512], BF16) for _ in range(2)]
            ofB = [ar.alloc("ofB", [8, 512], F32) for _ in range(2)]
            O32 = ar.alloc("O32", [8, 128], F32)
            SQ = ar.alloc("SQ", [8, 128], BF16)
            SD = ar.alloc("SD", [4, 128], F32)
            RS = ar.alloc("RS", [4, 128], F32)
            T1 = ar.alloc("T1", [8, 128], F32)
            RSg = ar.alloc("RSg", [2, 4, 128], F32)
        kb.dma("pool", lambda e, d=d: e.dma_start(out=w2a.ap[0:16, :], in_=g.gla_w2.ap()[l, d, :, :]), writes=[w2a])
        kb.dma("pool", lambda e, d=d: e.dma_start(out=w2a.ap[16:17, :], in_=g.gla_b.ap()[l, d:d + 1, :]), writes=[w2a])
        kb.op("dve", lambda e: e.memset(Z.ap, 0.0), writes=[Z])
        kb.op("dve", lambda e: e.memset(Sbf.ap, 0.0), writes=[Sbf])
        kb.op("dve", lambda e: e.memset(dec.ap, 1.0), writes=[dec])
        gorder = list(range(NG)) if d == 0 else list(range(NG - 1, -1, -1))

        def load_group(gi, slot):
            t0 = gi * 512
            kb.dma("sp", lambda e: e.dma_start(out=grA[slot].ap[0:16, :], in_=g.pfm.ap()[R_GR + d * 16:R_GR + d * 16 + 16, t0:t0 + 512]),
                   reads=[g.v_pfm], writes=[grA[slot]])
            kb.dma("sp", lambda e: e.dma_start(out=qB[slot].ap, in_=g.pfm.ap()[R_BQ:R_BQ + 512, t0:t0 + 512].rearrange("(h p) t -> p h t", p=128)),
                   reads=[g.v_pfm], writes=[qB[slot]])
            kb.dma("sp", lambda e: e.dma_start(out=kB[slot].ap, in_=g.pfm.ap()[R_BK:R_BK + 512, t0:t0 + 512].rearrange("(h p) t -> p h t", p=128)),
                   reads=[g.v_pfm], writes=[kB[slot]])
            kb.dma("sp", lambda e: e.dma_start(out=vB[slot].ap, in_=g.ptm.ap()[t0:t0 + 512, T_BV:T_BV + 1024].rearrange("(i p) c -> p i c", p=128)),
                   reads=[g.v_ptm], writes=[vB[slot]])
            if d == 1:
                kb.dma("sp", lambda e: e.dma_start(out=ogB[slot].ap, in_=g.pfm.ap()[R_OG:R_OG + 1024, t0:t0 + 512].rearrange(
                    "(c p) t -> p c t", p=128)), reads=[g.v_pfm], writes=[ogB[slot]])
                kb.dma("sp", lambda e: e.dma_start(out=ofB[slot].ap, in_=g.ofw.ap()[:, t0:t0 + 512].rearrange("(c p) t -> p c t", p=128)),
                       reads=[g.v_ofw], writes=[ofB[slot]])
        load_group(gorder[0], 0)
        tiles = []
        for gpos, gi in enumerate(gorder):
            for k, ii in enumerate(list(range(4)) if d == 0 else [3, 2, 1, 0]):
                tiles.append((gpos, gi, ii, k))

        def make(itx):
            gpos, gi, ii, kpos = tiles[itx]
            slot = gpos % 2
            n = gi * 4 + ii
            it = gpos * 4 + (ii if d == 0 else 3 - ii)
            w = it % 2
            par, pprev = it % 3, (it - 1) % 3
            e1, sp_, eb, enb, qt, kt, ktt, atm = E1[w], SP[w], EB[w], ENB[w], QT[w], KT[w], KTT[w], ATM[w]
            gr_, q_, k_, v_, os_ = grA[slot], qB[slot], kB[slot], vB[slot], OST[slot]
            c0 = ii * 128

            def stage0():
                if kpos == 0 and gpos + 1 < NG:
                    load_group(gorder[gpos + 1], (gpos + 1) % 2)
                mm_group(kb, PSL, [(gr_.ap[0:17, c0:c0 + 128], w2a.ap[0:17, :])], [gr_, w2a], g.bank[BL])
                kb.op("act", lambda e, e1=e1: e.activation(out=e1.ap, in_=PSL, func=AF.Exp, scale=-1.0), writes=[g.bank[BL], e1])
                kb.op("act", lambda e, e1=e1, sp_=sp_: e.activation(out=sp_.ap, in_=e1.ap, func=AF.Ln, bias=1.0), reads=[e1], writes=[sp_])
                kb.group_begin("pe", reads=[sp_, g.cst], writes=[g.bank[BB]])
                for h in range(4):
                    f = lambda e, h=h, sp_=sp_: e.matmul(PSB[:, h * 128:(h + 1) * 128], lhsT=sp_.ap[:, h * 128:(h + 1) * 128], rhs=tri,
                                                         start=True, stop=True)
                    if h == 3:
                        kb.group_end("pe", f, reads=[sp_, g.cst], writes=[g.bank[BB]])
                    else:
                        kb.raw("pe", f)
                kb.op("act", lambda e, eb=eb: e.activation(out=eb.ap.rearrange("p h t -> p (h t)"), in_=PSB, func=AF.Exp),
                      writes=[g.bank[BB], eb])
                kb.op("act", lambda e, enb=enb: e.activation(out=enb.ap.rearrange("p h t -> p (h t)"), in_=PSB, func=AF.Exp, scale=-1.0),
                      writes=[g.bank[BB], enb])
                kb.op("dve", lambda e, eb=eb, par=par: e.tensor_copy(out=dec.ap[:, par, :], in_=eb.ap[:, :, lastcol]), reads=[eb], writes=[dec])
                seg_end = (n % NTS == NTS - 1) if d == 0 else (n % NTS == 0)
                if seg_end:
                    kb.op("dve", lambda e, par=par: e.tensor_scalar(out=dec.ap[:, par, :], in0=dec.ap[:, par, :], scalar1=g.cflag.ap[:, 0:1],
                                                                    scalar2=None, op0=ALU.mult), reads=[g.cflag], writes=[dec])
                kb.op("dve", lambda e, qt=qt, q_=q_, eb=eb, c0=c0: e.tensor_tensor(out=qt.ap, in0=q_.ap[:, :, c0:c0 + 128], in1=eb.ap, op=ALU.mult),
                      reads=[q_, eb], writes=[qt])
                kb.op("pool", lambda e, kt=kt, k_=k_, enb=enb, c0=c0: e.tensor_tensor(out=kt.ap, in0=k_.ap[:, :, c0:c0 + 128], in1=enb.ap,
                                                                                      op=ALU.mult), reads=[k_, enb], writes=[kt])
                kb.group_begin("pe", reads=[kt, g.ident], writes=[g.bank[BT]])
                for h in range(4):
                    f = lambda e, h=h, kt=kt: e.transpose(out=PST[:, h, :], in_=kt.ap[:, h, :], identity=g.ident.ap)
                    if h == 3:
                        kb.group_end("pe", f, reads=[kt, g.ident], writes=[g.bank[BT]])
                    else:
                        kb.raw("pe", f)
                kb.op("act", lambda e, ktt=ktt: e.copy(out=ktt.ap.rearrange("p (h t) -> p h t", h=4), in_=PST), writes=[g.bank[BT], ktt])
                kb.group_begin("pe", reads=[kt, qt], writes=[g.bank[BA]])
                for h in range(4):
                    f = lambda e, h=h, kt=kt, qt=qt: e.matmul(PSA[:, h * 128:(h + 1) * 128], lhsT=kt.ap[:, h, :], rhs=qt.ap[:, h, :],
                                                              start=True, stop=True)
                    if h == 3:
                        kb.group_end("pe", f, reads=[kt, qt], writes=[g.bank[BA]])
                    else:
                        kb.raw("pe", f)
                for h in range(4):
                    kb.op("dve", lambda e, h=h, atm=atm: e.tensor_tensor(out=atm.ap[:, h, :], in0=PSA[:, h * 128:(h + 1) * 128], in1=msk,
                                                                         op=ALU.mult), reads=[g.cst], writes=[g.bank[BA], atm])

            def stage1():
                rdo = [v_, atm, Sbf, qt]
                kb.group_begin("pe", reads=rdo, writes=[g.bank[BO0], g.bank[BO1]])
                for hc in range(8):
                    h, c = hc // 2, hc % 2
                    oap = PSO[:, hc * 128:(hc + 1) * 128]
                    kb.raw("pe", lambda e, oap=oap, h=h, c=c, v_=v_, atm=atm, ii=ii: e.matmul(
                        oap, lhsT=v_.ap[:, ii, h * 256 + c * 128:h * 256 + (c + 1) * 128], rhs=atm.ap[:, h, :], start=True, stop=False))
                    f = lambda e, oap=oap, h=h, c=c, qt=qt: e.matmul(oap, lhsT=Sbf.ap[:, h, c * 128:(c + 1) * 128], rhs=qt.ap[:, h, :],
                                                                     start=False, stop=True)
                    if hc == 7:
                        kb.group_end("pe", f, reads=rdo, writes=[g.bank[BO0], g.bank[BO1]])
                    else:
                        kb.raw("pe", f)
                kb.group_begin("pe", reads=[ktt, v_], writes=[g.bank[BU0], g.bank[BU1]])
                for h in range(4):
                    f = lambda e, h=h, ktt=ktt, v_=v_, ii=ii: e.matmul(PSU[:, h * 256:(h + 1) * 256], lhsT=ktt.ap[:, h * 128:(h + 1) * 128],
                                                                       rhs=v_.ap[:, ii, h * 256:(h + 1) * 256], start=True, stop=True)
                    if h == 3:
                        kb.group_end("pe", f, reads=[ktt, v_], writes=[g.bank[BU0], g.bank[BU1]])
                    else:
                        kb.raw("pe", f)
                for h in range(4):
                    kb.op("dve", lambda e, h=h, pprev=pprev: e.scalar_tensor_tensor(
                        out=Z.ap[:, h, :], in0=Z.ap[:, h, :], scalar=dec.ap[:, pprev, h:h + 1], in1=PSU[:, h * 256:(h + 1) * 256],
                        op0=ALU.mult, op1=ALU.add), reads=[dec], writes=[Z, g.bank[BU0 + h // 2]])
                for h in range(4):
                    kb.op("act", lambda e, h=h, par=par: e.activation(out=Sbf.ap[:, h, :], in_=Z.ap[:, h, :], func=AF.Copy,
                                                                      scale=dec.ap[:, par, h:h + 1]), reads=[Z, dec], writes=[Sbf])
                if d == 0:
                    kb.op("dve", lambda e, os_=os_, c0=c0: e.tensor_copy(out=os_.ap[:, :, c0:c0 + 128], in_=PSO.rearrange("p (c t) -> p c t", c=8)),
                          writes=[g.bank[BO0], g.bank[BO1], os_])
                else:
                    og_, of_ = ogB[slot], ofB[slot]
                    kb.op("dve", lambda e, of_=of_, c0=c0: e.tensor_tensor(out=O32.ap, in0=PSO.rearrange("p (c t) -> p c t", c=8),
                                                                          in1=of_.ap[:, :, c0:c0 + 128], op=ALU.add),
                          reads=[of_], writes=[g.bank[BO0], g.bank[BO1], O32])
                    kb.op("act", lambda e: e.activation(out=SQ.ap, in_=O32.ap, func=AF.Square), reads=[O32], writes=[SQ])
                    kb.group_begin("pe", reads=[SQ, g.onesb], writes=[g.bank[BL]])
                    for hc in range(8):
                        h, c = hc // 2, hc % 2
                        f = lambda e, hc=hc, h=h, c=c: e.matmul(PSL[:, h * 128:(h + 1) * 128], lhsT=g.onesb.ap, rhs=SQ.ap[:, hc, :],
                                                                start=(c == 0), stop=(c == 1))
                        if hc == 7:
                            kb.group_end("pe", f, reads=[SQ, g.onesb], writes=[g.bank[BL]])
                        else:
                            kb.raw("pe", f)
                    kb.op("act", lambda e: e.activation(out=SD.ap.rearrange("p h t -> p (h t)"), in_=PSL, func=AF.Sqrt, scale=1.0 / 256, bias=EPS),
                          writes=[g.bank[BL], SD])
                    kb.op("dve", lambda e: e.reciprocal(out=RS.ap, in_=SD.ap), reads=[SD], writes=[RS])
                    O32v = O32.ap.rearrange("p (h c) t -> p h c t", c=2)
                    T1v = T1.ap.rearrange("p (h c) t -> p h c t", c=2)
                    for c in range(2):
                        kb.op("dve", lambda e, c=c: e.tensor_scalar(out=RSg.ap[:, c, :, :], in0=RS.ap, scalar1=gain2.ap[:, c:c + 1], scalar2=None,
                                                                     op0=ALU.mult), reads=[RS, gain2], writes=[RSg])
                    for c in range(2):
                        kb.op("dve", lambda e, c=c: e.tensor_tensor(out=T1v[:, :, c, :], in0=O32v[:, :, c, :], in1=RSg.ap[:, c, :, :], op=ALU.mult),
                              reads=[O32, RSg], writes=[T1])
                    kb.op("dve", lambda e, og_=og_, os_=os_, c0=c0: e.tensor_tensor(out=os_.ap[:, :, c0:c0 + 128], in0=T1.ap,
                                                                                     in1=og_.ap[:, :, c0:c0 + 128], op=ALU.mult),
                          reads=[T1, og_], writes=[os_])
                if kpos == 3:
                    t0 = gi * 512
                    if d == 0:
                        kb.dma("sp", lambda e, os_=os_, t0=t0: e.dma_start(out=g.ofw.ap()[:, t0:t0 + 512].rearrange("(c p) t -> p c t", p=128), in_=os_.ap),
                               reads=[os_], writes=[g.v_ofw])
                    else:
                        kb.dma("sp", lambda e, os_=os_, t0=t0: e.dma_start(out=g.obT.ap()[:, t0:t0 + 512].rearrange("(c p) t -> p c t", p=128), in_=os_.ap),
                               reads=[os_], writes=[g.v_obT])

            return [stage0, stage1]
        run_pipeline(len(tiles), make, 2)
        kb.barrier()
    one_pass(0)
    one_pass(1)
    ar.reset(m0)


WNAMES = ("norm1", "w_in", "sink_a", "gla_w2", "gla_b", "gla_norm", "rpb_c", "w_br_a", "w_br_b", "w_br_c", "w_out", "norm2",
          "w_ffn_in", "w_ffn_out", "norm_f")
WSHAPES = {"norm1": [2, D], "w_in": [2, D, IN_COLS], "sink_a": [2, 8], "gla_w2": [2, 2, 16, 512], "gla_b": [2, 2, 512],
           "gla_norm": [2, 256], "rpb_c": [2, 8, 15, 31], "w_br_a": [2, 1024, D], "w_br_b": [2, 1024, D], "w_br_c": [2, 1024, D],
           "w_out": [2, D, D], "norm2": [2, D], "w_ffn_in": [2, D, 2 * DFF], "w_ffn_out": [2, DFF, D], "norm_f": [D]}
ARENA_BYTES = 206 * 1024


def build(SEG, NSEG, depth=2, debug=False, stop_after=None):
    nc = bass.Bass("TRN2", target_bir_lowering=False)
    T = SEG * NSEG
    g = G()
    g.nc, g.SEG, g.NSEG, g.T = nc, SEG, NSEG, T
    g.x_in = nc.dram_tensor("x", [T, D], F32, kind="ExternalInput")
    for nm in WNAMES:
        setattr(g, nm, nc.dram_tensor(nm, WSHAPES[nm], F32, kind="ExternalInput"))
    g.bandC, g.clsC, g.uniqC = plan_C(T, SEG)
    g.clsA, g.uniqA = plan_A(T, SEG)
    g.d_cst = nc.dram_tensor("cst", [128, 768], F32, kind="ExternalInput")
    g.d_biasA = nc.dram_tensor("biasA", [len(g.uniqA) * 8, 128, 384], F32, kind="ExternalInput")
    g.d_maskC = nc.dram_tensor("maskC", [len(g.uniqC), 128, 896], F32, kind="ExternalInput")
    g.d_cflag = nc.dram_tensor("cflag", [128, 1], F32, kind="ExternalInput")
    sk = "ExternalOutput" if debug else "Internal"
    g.xr = nc.dram_tensor("xr", [T, D], F32, kind=sk)
    g.pfm = nc.dram_tensor("pfm", [NFM, T], BF16, kind=sk)
    g.ptm = nc.dram_tensor("ptm", [T, NTM], BF16, kind=sk)
    g.oaT = nc.dram_tensor("oaT", [1024, T], BF16, kind=sk)
    g.obT = nc.dram_tensor("obT", [1024, T], BF16, kind=sk)
    g.ocT = nc.dram_tensor("ocT", [1024, T], BF16, kind=sk)
    g.ofw = nc.dram_tensor("ofw", [1024, T], F32, kind=sk)
    g.mT = nc.dram_tensor("mT", [D, T], BF16, kind=sk)
    g.actT = nc.dram_tensor("actT", [DFF, T], BF16, kind=sk)
    g.Gtab = nc.dram_tensor("Gtab", [8 * 15 * 64 * 128], F32)
    g.y = nc.dram_tensor("y", [T, D], F32, kind="ExternalOutput")
    es = ExitStack()
    with es:
        kb = KB(nc, es)
        for nm in ("pfm", "ptm", "oaT", "obT", "ocT", "ofw", "mT", "actT", "xr", "y", "G"):
            setattr(g, "v_" + nm, kb.vbuf("v_" + nm))
        g.ar = Arena(nc, kb, ARENA_BYTES)
        ar = g.ar
        PSt = nc.alloc_psum_tensor("PS", [128, 4096], F32)
        g.PS = PSt[:, :]
        g.bank = [kb.vbuf("bank%d" % i) for i in range(8)]
        g.cst = ar.alloc("cst", [6, 128], F32)
        g.ident = ar.alloc("ident", [128], BF16)
        g.onesb = ar.alloc("onesb", [128], BF16)
        g.cflag = ar.alloc("cflag", [1], F32)
        g.eps = ar.alloc("eps", [1], F32)
        kb.dma("sp", lambda e: e.dma_start(out=g.cst.ap, in_=g.d_cst.ap().rearrange("p (a b) -> p a b", a=6)), writes=[g.cst])
        kb.dma("sp", lambda e: e.dma_start(out=g.cflag.ap, in_=g.d_cflag.ap()), writes=[g.cflag])
        kb.op("dve", lambda e: e.tensor_copy(out=g.ident.ap, in_=g.cst.ap[:, 0, :]), reads=[g.cst], writes=[g.ident])
        kb.op("dve", lambda e: e.tensor_copy(out=g.onesb.ap, in_=g.cst.ap[:, 5, :]), reads=[g.cst], writes=[g.onesb])
        kb.op("dve", lambda e: e.memset(g.eps.ap, EPS), writes=[g.eps])
        ar.base = ar.mark()
        kb.barrier()

        def body():
            for l in range(depth):
                for seg in range(NSEG):
                    phase_G1(kb, g, l, seg)
                if stop_after == "G1":
                    return
                mixer_A(kb, g, l)
                if stop_after == "A":
                    return
                mixer_B(kb, g, l)
                if stop_after == "B":
                    return
                mixer_C(kb, g, l)
                if stop_after == "C":
                    return
                for seg in range(NSEG):
                    phase_G2(kb, g, l, seg)
                    phase_G3(kb, g, l, seg)
                    if stop_after == "G3":
                        continue
                    phase_G4(kb, g, l, seg)
                    phase_G5(kb, g, l, seg)
                if stop_after in ("G3", "L0"):
                    return
            phase_final(kb, g)
        body()
        kb.barrier()
        kb.emit()
        g.n_ins = kb.n_ins
    return nc, g


def core_inputs(typ, SEG, NSEG):
    T = SEG * NSEG
    bands, clsC, uniqC = plan_C(T, SEG)
    clsA, uniqA = plan_A(T, SEG)
    return {"cst": consts_np(), "biasA": bias_A(uniqA, typ), "maskC": masks_C(uniqC, typ),
            "cflag": np.full((128, 1), 1.0 if typ == "P" else 0.0, np.float32)}


_CACHE = {}


def kernel(**inputs):
    SEG, NSEG, NCORE = 2048, 4, 8
    T = SEG * NSEG
    xp = np.asarray(inputs["x_prompt"], np.float32)
    xs = np.asarray(inputs["x_sample"], np.float32)
    assert xp.shape == (1, T, D) and xs.shape[1:] == (SEG, D)
    nsamp = xs.shape[0]
    slots = {0: None}
    counts = [0] * NCORE
    for s in range(nsamp):
        c = 1 + (s % (NCORE - 1))
        counts[c] += 1
    assign, s = {}, 0
    for c in range(1, NCORE):
        assign[c] = list(range(s, s + counts[c]))
        s += counts[c]
        assert counts[c] <= NSEG
    if "nc" not in _CACHE:
        _CACHE["nc"] = build(SEG, NSEG)[0]
    nc = _CACHE["nc"]
    w = {nm: np.ascontiguousarray(np.asarray(inputs[nm], np.float32)) for nm in WNAMES}
    cP, cS = core_inputs("P", SEG, NSEG), core_inputs("S", SEG, NSEG)
    in_maps = []
    for c in range(NCORE):
        if c == 0:
            x = np.ascontiguousarray(xp[0])
            m = dict(cP)
        else:
            x = np.zeros((T, D), np.float32)
            for k, si in enumerate(assign[c]):
                x[k * SEG:(k + 1) * SEG] = xs[si]
            m = dict(cS)
        m["x"] = x
        m.update(w)
        in_maps.append(m)
    res = run_bass_kernel_spmd(nc, in_maps, core_ids=list(range(NCORE)))
    outs = [np.asarray(r["y"], np.float32) for r in res.results]
    y_prompt = outs[0].reshape(1, T, D)
    y_sample = np.zeros((nsamp, SEG, D), np.float32)
    for c in range(1, NCORE):
        for k, si in enumerate(assign[c]):
            y_sample[si] = outs[c][k * SEG:(k + 1) * SEG]
    return (y_prompt, y_sample)
```
